# Optimizing a Trainium2 kernel written in Bass

```python
import math
import jax, jax.numpy as jnp
from jax import lax
import numpy as np

D_MODEL = 2048
BATCH = 16
SEQ = 256
DEPTH = 4
DEC_BATCH = 2
DEC_SEQ = 1024
PAST_LEN = 256

GRID_W = 64
N_MIXERS = 4
Q_BLOCK = 128
ROPE_BASE = 10000.0
LN_EPS = 1e-5
RMS_EPS = 1e-6
NEG_INF = -1e30
DEEPNORM_ALPHA = (2 * DEPTH) ** 0.25
DEEPNORM_BETA = (8 * DEPTH) ** -0.25

MLA_HEADS = 16
MLA_Q_LORA = 512
MLA_KV_LORA = 512
MLA_NOPE = 128
MLA_ROPE = 64
MLA_V = 128
MLA_SCALE = (MLA_NOPE + MLA_ROPE) ** -0.5

GQA_HEADS = 32
GQA_KV_HEADS = 8
GQA_HEAD_DIM = 64
WINDOW = 128
BAND_BLOCK = 128
GQA_SCALE = GQA_HEAD_DIM ** -0.5

FNET_GROUPS = 4
FNET_GROUP_DIM = D_MODEL // FNET_GROUPS

CONV_WIDTH = 3

PEER_HEADS = 8
PEER_N_KEYS = 128
PEER_N_EXPERTS = PEER_N_KEYS * PEER_N_KEYS
PEER_QUERY_DIM = 256
PEER_HALF = PEER_QUERY_DIM // 2
PEER_TOPK = 16
TOKEN_BLOCK = 128

kernel_name = 'hybrid_diffusion_mla_swa_fnet_conv_peer_step'


def layer_norm(x, g, b):
    xf = x.astype(jnp.float32)
    mu = xf.mean(-1, keepdims=True)
    var = jnp.square(xf - mu).mean(-1, keepdims=True)
    return ((xf - mu) * lax.rsqrt(var + LN_EPS) * g.astype(jnp.float32) + b.astype(jnp.float32)).astype(x.dtype)


def rms_norm(x, g):
    xf = x.astype(jnp.float32)
    return (xf * lax.rsqrt(jnp.square(xf).mean(-1, keepdims=True) + RMS_EPS) * g.astype(jnp.float32)).astype(x.dtype)


def ada_modulation(cond, w, b):
    return jnp.split(jax.nn.silu(cond) @ w + b, 6, axis=-1)


def modulate(x, shift, scale):
    return x * (1 + scale[:, None, :]) + shift[:, None, :]


def post_norm(x, delta, gate, g, b):
    return layer_norm(DEEPNORM_ALPHA * x + gate[:, None, :] * delta, g, b)


def axial_rope(x):
    T, R = x.shape[1], x.shape[-1]
    rows = T // GRID_W
    row_id = jnp.repeat(jnp.arange(rows), GRID_W)
    col_id = jnp.tile(jnp.arange(GRID_W), rows)
    half = R // 2
    quarter = half // 2
    inv_freq = ROPE_BASE ** (-jnp.arange(quarter, dtype=jnp.float32) / quarter)

    def rot(xa, pos):
        ang = pos.astype(jnp.float32)[:, None] * inv_freq[None, :]
        cos = jnp.cos(ang)[None, :, None, :]
        sin = jnp.sin(ang)[None, :, None, :]
        x1, x2 = xa[..., :quarter], xa[..., quarter:]
        return jnp.concatenate([x1 * cos - x2 * sin, x1 * sin + x2 * cos], -1)

    xf = x.astype(jnp.float32)
    return jnp.concatenate([rot(xf[..., :half], row_id), rot(xf[..., half:], col_id)], -1).astype(x.dtype)


def block_attention(q, k, v, scale, sink=None):
    B, T, H, dq = q.shape
    Hk, dv = k.shape[2], v.shape[-1]
    G = H // Hk
    nb = T // Q_BLOCK
    qb = q.reshape(B, nb, Q_BLOCK, Hk, G, dq).transpose(1, 0, 2, 3, 4, 5)

    def one_block(qi):
        s = jnp.einsum('bqkgd,bskd->bkgqs', qi, k, preferred_element_type=jnp.float32) * scale
        if sink is not None:
            s_sink = jnp.broadcast_to(sink.reshape(Hk, G)[None, :, :, None, None].astype(jnp.float32), s.shape[:-1] + (1,))
            p = jax.nn.softmax(jnp.concatenate([s, s_sink], -1), axis=-1)[..., :-1]
        else:
            p = jax.nn.softmax(s, axis=-1)
        return jnp.einsum('bkgqs,bskd->bqkgd', p.astype(v.dtype), v)

    o = lax.map(one_block, qb)
    return o.transpose(1, 0, 2, 3, 4, 5).reshape(B, T, H, dv)


def banded_window_attention(q, k, v, k_ctx, v_ctx, sink, scale):
    B, T, H, d = q.shape
    Hk = k.shape[2]
    G = H // Hk
    bb = BAND_BLOCK
    nb = T // bb
    qb = q.reshape(B, nb, bb, Hk, G, d).transpose(1, 0, 2, 3, 4, 5)
    pad = ((0, 0), (bb, bb), (0, 0), (0, 0))
    kp = jnp.pad(k, pad).reshape(B, nb + 2, bb, Hk, d)
    vp = jnp.pad(v, pad).reshape(B, nb + 2, bb, Hk, d)
    kband = jnp.concatenate([kp[:, :-2], kp[:, 1:-1], kp[:, 2:]], axis=2).transpose(1, 0, 2, 3, 4)
    vband = jnp.concatenate([vp[:, :-2], vp[:, 1:-1], vp[:, 2:]], axis=2).transpose(1, 0, 2, 3, 4)
    blk = jnp.arange(nb)
    q_pos = blk[:, None] * bb + jnp.arange(bb)[None, :]
    k_pos = (blk[:, None] - 1) * bb + jnp.arange(3 * bb)[None, :]
    valid = ((jnp.abs(q_pos[:, :, None] - k_pos[:, None, :]) <= WINDOW)
             & (k_pos[:, None, :] >= 0) & (k_pos[:, None, :] < T))
    sink_l = sink.reshape(Hk, G)[None, :, :, None, None].astype(jnp.float32)
    n_band = 3 * bb

    def one_block(args):
        qi, kb, vb, ok = args
        s_band = jnp.einsum('bqkgd,bskd->bkgqs', qi, kb, preferred_element_type=jnp.float32) * scale
        s_band = jnp.where(ok[None, None, None], s_band, NEG_INF)
        s_ctx = jnp.einsum('bqkgd,bskd->bkgqs', qi, k_ctx, preferred_element_type=jnp.float32) * scale
        s_sink = jnp.broadcast_to(sink_l, s_band.shape[:-1] + (1,))
        p = jax.nn.softmax(jnp.concatenate([s_band, s_ctx, s_sink], -1), axis=-1).astype(v.dtype)
        return (jnp.einsum('bkgqs,bskd->bqkgd', p[..., :n_band], vb)
                + jnp.einsum('bkgqs,bskd->bqkgd', p[..., n_band:-1], v_ctx))

    o = lax.map(one_block, (qb, kband, vband, valid))
    return o.transpose(1, 0, 2, 3, 4, 5).reshape(B, T, H, d)


def mla_queries(u, w_dq, q_norm, w_uq):
    B, T, _ = u.shape
    q = (rms_norm(u @ w_dq, q_norm) @ w_uq).reshape(B, T, MLA_HEADS, MLA_NOPE + MLA_ROPE)
    return q[..., :MLA_NOPE], q[..., MLA_NOPE:]


def mla_compress(u, w_dkv, kv_norm):
    kv = u @ w_dkv
    return rms_norm(kv[..., :MLA_KV_LORA], kv_norm), kv[..., MLA_KV_LORA:]


def mla_attend(q_nope, q_rope, ckv, krope, w_uk, w_uv, w_o):
    B, S, _ = ckv.shape
    T = q_nope.shape[1]
    k_nope = (ckv @ w_uk).reshape(B, S, MLA_HEADS, MLA_NOPE)
    v = (ckv @ w_uv).reshape(B, S, MLA_HEADS, MLA_V)
    k = jnp.concatenate([k_nope, jnp.broadcast_to(krope[:, :, None, :], (B, S, MLA_HEADS, MLA_ROPE))], -1)
    q = jnp.concatenate([q_nope, q_rope], -1)
    o = block_attention(q, k, v, MLA_SCALE)
    return o.reshape(B, T, MLA_HEADS * MLA_V) @ w_o


def gqa_split(u, w_qkv):
    B, T, _ = u.shape
    qkv = u @ w_qkv
    nq = GQA_HEADS * GQA_HEAD_DIM
    nk = GQA_KV_HEADS * GQA_HEAD_DIM
    q = qkv[..., :nq].reshape(B, T, GQA_HEADS, GQA_HEAD_DIM)
    k = qkv[..., nq:nq + nk].reshape(B, T, GQA_KV_HEADS, GQA_HEAD_DIM)
    v = qkv[..., nq + nk:].reshape(B, T, GQA_KV_HEADS, GQA_HEAD_DIM)
    return q, k, v


def fourier_mix(u, w_out):
    B, T, D = u.shape
    ug = u.astype(jnp.float32).reshape(B, T, FNET_GROUPS, FNET_GROUP_DIM)
    f = jnp.fft.fftn(ug, axes=(1, 3), norm='ortho').real
    return f.reshape(B, T, D).astype(u.dtype) @ w_out


def short_conv_mix(u, w_in, conv_w, conv_b, w_out):
    D = u.shape[-1]
    b_gate, c_gate, h = jnp.split(u @ w_in, 3, axis=-1)
    z = c_gate * h
    conv = lax.conv_general_dilated(z, conv_w.reshape(CONV_WIDTH, 1, D).astype(z.dtype), window_strides=(1,),
                                    padding=((CONV_WIDTH // 2, CONV_WIDTH // 2),),
                                    dimension_numbers=('NWC', 'WIO', 'NWC'), feature_group_count=D)
    return (b_gate * (conv + conv_b)) @ w_out


def peer_ffn(u, w_q, sub_keys, exp_u, exp_v):
    B, T, D = u.shape
    N = B * T
    xt = u.reshape(N, D)
    q = (xt @ w_q).reshape(N, PEER_HEADS, 2, PEER_HALF)
    s = jnp.einsum('nhcd,hckd->nhck', q, sub_keys, preferred_element_type=jnp.float32)
    v1, i1 = lax.top_k(s[:, :, 0], PEER_TOPK)
    v2, i2 = lax.top_k(s[:, :, 1], PEER_TOPK)
    cand_s = (v1[..., :, None] + v2[..., None, :]).reshape(N, PEER_HEADS, PEER_TOPK * PEER_TOPK)
    cand_i = (i1[..., :, None] * PEER_N_KEYS + i2[..., None, :]).reshape(N, PEER_HEADS, PEER_TOPK * PEER_TOPK)
    top_s, pos = lax.top_k(cand_s, PEER_TOPK)
    idx = jnp.take_along_axis(cand_i, pos, axis=-1)
    gate = jax.nn.softmax(top_s, axis=-1)
    nb = N // TOKEN_BLOCK
    xb = xt.reshape(nb, TOKEN_BLOCK, D)
    ib = idx.reshape(nb, TOKEN_BLOCK, PEER_HEADS * PEER_TOPK)
    gb = gate.reshape(nb, TOKEN_BLOCK, PEER_HEADS * PEER_TOPK)

    def one_block(args):
        xi, ii, gi = args
        h = jnp.einsum('td,ted->te', xi, exp_u[ii], preferred_element_type=jnp.float32)
        a = (jax.nn.gelu(h, approximate=False) * gi).astype(xi.dtype)
        return jnp.einsum('te,ted->td', a, exp_v[ii])

    return lax.map(one_block, (xb, ib, gb)).reshape(B, T, D)


def setup_inputs(seed: int = 0) -> dict:
    key = jax.random.key(seed)
    ks = iter(jax.random.split(key, 40))
    f32 = jnp.float32
    D = D_MODEL

    def nrm(shape, scale):
        return jax.random.normal(next(ks), shape, f32) * scale

    inp = {}
    inp['x_prompt'] = nrm((BATCH, SEQ, D), 1.0)
    inp['x_sample'] = nrm((DEC_BATCH, DEC_SEQ, D), 1.0)
    inp['cache_l0_ckv'] = nrm((DEC_BATCH, PAST_LEN, MLA_KV_LORA), 1.0)
    inp['cache_l0_krope'] = nrm((DEC_BATCH, PAST_LEN, MLA_ROPE), 1.0)
    inp['cache_l1_k'] = nrm((DEC_BATCH, PAST_LEN, GQA_KV_HEADS, GQA_HEAD_DIM), 1.0)
    inp['cache_l1_v'] = nrm((DEC_BATCH, PAST_LEN, GQA_KV_HEADS, GQA_HEAD_DIM), 1.0)
    inp['c'] = nrm((DEC_BATCH, D), 1.0)
    inp['c_ctx'] = nrm((D,), 1.0)
    inp['ada_w'] = nrm((DEPTH, D, 6 * D), 0.5 * D ** -0.5)
    inp['ada_b'] = nrm((DEPTH, 6 * D), 0.01)
    inp['ln1_g'] = 1.0 + nrm((DEPTH, D), 0.01)
    inp['ln1_b'] = nrm((DEPTH, D), 0.01)
    inp['ln2_g'] = 1.0 + nrm((DEPTH, D), 0.01)
    inp['ln2_b'] = nrm((DEPTH, D), 0.01)
    inp['mla_w_dq'] = nrm((D, MLA_Q_LORA), D ** -0.5)
    inp['mla_q_norm'] = 1.0 + nrm((MLA_Q_LORA,), 0.01)
    inp['mla_w_uq'] = nrm((MLA_Q_LORA, MLA_HEADS * (MLA_NOPE + MLA_ROPE)), MLA_Q_LORA ** -0.5)
    inp['mla_w_dkv'] = nrm((D, MLA_KV_LORA + MLA_ROPE), D ** -0.5)
    inp['mla_kv_norm'] = 1.0 + nrm((MLA_KV_LORA,), 0.01)
    inp['mla_w_uk'] = nrm((MLA_KV_LORA, MLA_HEADS * MLA_NOPE), MLA_KV_LORA ** -0.5)
    inp['mla_w_uv'] = nrm((MLA_KV_LORA, MLA_HEADS * MLA_V), MLA_KV_LORA ** -0.5)
    inp['mla_w_o'] = nrm((MLA_HEADS * MLA_V, D), DEEPNORM_BETA * (MLA_HEADS * MLA_V) ** -0.5)
    inp['gqa_w_qkv'] = nrm((D, (GQA_HEADS + 2 * GQA_KV_HEADS) * GQA_HEAD_DIM), D ** -0.5)
    inp['gqa_sink'] = nrm((GQA_HEADS,), 0.5)
    inp['gqa_w_o'] = nrm((GQA_HEADS * GQA_HEAD_DIM, D), DEEPNORM_BETA * (GQA_HEADS * GQA_HEAD_DIM) ** -0.5)
    inp['fnet_w_out'] = nrm((D, D), DEEPNORM_BETA * D ** -0.5)
    inp['conv_w_in'] = nrm((D, 3 * D), D ** -0.5)
    inp['conv_w'] = nrm((CONV_WIDTH, D), CONV_WIDTH ** -0.5)
    inp['conv_b'] = nrm((D,), 0.01)
    inp['conv_w_out'] = nrm((D, D), DEEPNORM_BETA * D ** -0.5)
    inp['peer_w_q'] = nrm((DEPTH, D, PEER_HEADS * PEER_QUERY_DIM), D ** -0.5)
    inp['peer_sub_keys'] = nrm((DEPTH, PEER_HEADS, 2, PEER_N_KEYS, PEER_HALF), PEER_HALF ** -0.5)
    inp['peer_u'] = nrm((DEPTH, PEER_N_EXPERTS, D), D ** -0.5)
    inp['peer_v'] = nrm((DEPTH, PEER_N_EXPERTS, D), DEEPNORM_BETA)
    return inp


def reference(x_prompt, x_sample, cache_l0_ckv, cache_l0_krope, cache_l1_k, cache_l1_v, c, c_ctx,
              ada_w, ada_b, ln1_g, ln1_b, ln2_g, ln2_b,
              mla_w_dq, mla_q_norm, mla_w_uq, mla_w_dkv, mla_kv_norm, mla_w_uk, mla_w_uv, mla_w_o,
              gqa_w_qkv, gqa_sink, gqa_w_o,
              fnet_w_out,
              conv_w_in, conv_w, conv_b, conv_w_out,
              peer_w_q, peer_sub_keys, peer_u, peer_v):
    xp, xs = x_prompt, x_sample
    new_l0_ckv = new_l0_krope = new_l1_k = new_l1_v = None
    for i in range(DEPTH):
        m = i % N_MIXERS
        mp = ada_modulation(c_ctx[None, :], ada_w[i], ada_b[i])
        ms = ada_modulation(c, ada_w[i], ada_b[i])
        up = modulate(xp, mp[0], mp[1])
        us = modulate(xs, ms[0], ms[1])
        if m == 0:
            qn, qr = mla_queries(up, mla_w_dq, mla_q_norm, mla_w_uq)
            ckv_p, kr_p = mla_compress(up, mla_w_dkv, mla_kv_norm)
            op = mla_attend(qn, qr, ckv_p, kr_p, mla_w_uk, mla_w_uv, mla_w_o)
            new_l0_ckv, new_l0_krope = ckv_p, kr_p
            qn, qr = mla_queries(us, mla_w_dq, mla_q_norm, mla_w_uq)
            qr = axial_rope(qr)
            ckv_s, kr_s = mla_compress(us, mla_w_dkv, mla_kv_norm)
            kr_s = axial_rope(kr_s[:, :, None, :])[:, :, 0, :]
            os_ = mla_attend(qn, qr, jnp.concatenate([ckv_s, cache_l0_ckv], 1),
                             jnp.concatenate([kr_s, cache_l0_krope], 1), mla_w_uk, mla_w_uv, mla_w_o)
        elif m == 1:
            q, k, v = gqa_split(up, gqa_w_qkv)
            op = block_attention(q, k, v, GQA_SCALE, gqa_sink).reshape(xp.shape[0], xp.shape[1], -1) @ gqa_w_o
            new_l1_k, new_l1_v = k, v
            q, k, v = gqa_split(us, gqa_w_qkv)
            o = banded_window_attention(axial_rope(q), axial_rope(k), v, cache_l1_k, cache_l1_v, gqa_sink, GQA_SCALE)
            os_ = o.reshape(xs.shape[0], xs.shape[1], -1) @ gqa_w_o
        elif m == 2:
            op = fourier_mix(up, fnet_w_out)
            os_ = fourier_mix(us, fnet_w_out)
        else:
            op = short_conv_mix(up, conv_w_in, conv_w, conv_b, conv_w_out)
            os_ = short_conv_mix(us, conv_w_in, conv_w, conv_b, conv_w_out)
        xp = post_norm(xp, op, mp[2], ln1_g[i], ln1_b[i])
        xs = post_norm(xs, os_, ms[2], ln1_g[i], ln1_b[i])
        up = modulate(xp, mp[3], mp[4])
        us = modulate(xs, ms[3], ms[4])
        fp = peer_ffn(up, peer_w_q[i], peer_sub_keys[i], peer_u[i], peer_v[i])
        fs = peer_ffn(us, peer_w_q[i], peer_sub_keys[i], peer_u[i], peer_v[i])
        xp = post_norm(xp, fp, mp[5], ln2_g[i], ln2_b[i])
        xs = post_norm(xs, fs, ms[5], ln2_g[i], ln2_b[i])
    return (xp, xs, new_l0_ckv, new_l0_krope, new_l1_k, new_l1_v)
```

```python
import contextlib
import math
import numpy as np
import ml_dtypes
import concourse.bass as bass
import concourse.mybir as mybir
from concourse.bass_utils import run_bass_kernel_spmd

F32 = mybir.dt.float32
BF16 = mybir.dt.bfloat16
U32 = mybir.dt.uint32
I32 = mybir.dt.int32
AF = mybir.ActivationFunctionType
ALU = mybir.AluOpType
AX = mybir.AxisListType

D = 2048
DEPTH = 4
NP_TOK = 512
NS_TOK = 1024
NTOK = NP_TOK + NS_TOK
NT = NTOK // 128
ALPHA = (2 * DEPTH) ** 0.25
LN_EPS = 1e-5
RMS_EPS = 1e-6
NEG = -1e30


class Buf:
    def __init__(self, name, t=None):
        self.name = name
        self.t = t
        self.w = None
        self.r = {}
        self.alias = []

    def __getitem__(self, idx):
        return self.t[idx]


class Eng:
    def __init__(self, name, handle):
        self.name = name
        self.h = handle
        self.sems = []
        self.cur = 0
        self.count = 0
        self.known = {}
        self.prog = []
        self.dslots = []
        self.duse = []
        self.dnext = 0


class K:
    ROT = 20000

    def __init__(self, nc, stack):
        self.nc = nc
        self.stack = stack
        self.E = {
            "pe": Eng("pe", nc.tensor),
            "act": Eng("act", nc.scalar),
            "dve": Eng("dve", nc.vector),
            "pool": Eng("pool", nc.gpsimd),
            "sp": Eng("sp", nc.sync),
        }
        self.nsem = 0
        for e in self.E.values():
            e.sems.append(self.sem(e.name))
        for qn, n in (("sp", 24), ("pool", 24), ("act", 8)):
            e = self.E[qn]
            e.dslots = [self.sem(f"d{qn}{i}") for i in range(n)]
            e.duse = [0] * n
        self.nbuf = 0

    def sem(self, name):
        self.nsem += 1
        return self.stack.enter_context(self.nc.semaphore(f"s{self.nsem}_{name}"))

    def sb(self, name, shape, dt=F32):
        self.nbuf += 1
        t = self.stack.enter_context(self.nc.sbuf_tensor(f"{name}_{self.nbuf}", list(shape), dt))
        return Buf(name, t)

    def ps(self, name, shape, dt=F32):
        self.nbuf += 1
        t = self.stack.enter_context(self.nc.psum_tensor(f"{name}_{self.nbuf}", list(shape), dt))
        return Buf(name, t)

    def dram(self, name, shape, dt=F32, kind="Internal"):
        t = self.nc.dram_tensor(name, list(shape), dt, kind=kind).ap()
        return Buf(name, t)

    def _waits(self, e, reads, writes, skip_self=False):
        need = {}
        def add(ev):
            if ev is None:
                return
            sem, val = ev
            if skip_self and any(sem is s for s in e.sems):
                return
            k = id(sem)
            if e.known.get(k, 0) >= val:
                return
            if k not in need or need[k][1] < val:
                need[k] = (sem, val)
        for b in reads:
            add(b.w)
        for b in writes:
            add(b.w)
            for ev in b.r.values():
                add(ev)
        out = list(need.values())
        for sem, val in out:
            e.known[id(sem)] = val
        return out

    def _commit(self, ev, reads, writes):
        for b in reads:
            b.r[id(ev[0])] = ev
        for b in writes:
            b.w = ev
            b.r = {}

    @staticmethod
    def _expand(bufs):
        out = []
        for b in bufs:
            out.append(b)
            out.extend(b.alias)
        return out

    def op(self, en, fn, reads=(), writes=()):
        reads = self._expand(reads); writes = self._expand(writes)
        e = self.E[en]
        waits = self._waits(e, reads, writes, skip_self=(en == "pe"))
        if e.count >= self.ROT:
            e.sems.append(self.sem(e.name))
            e.cur += 1
            e.count = 0
        e.count += 1
        sem = e.sems[e.cur]
        ev = (sem, e.count)
        e.prog.append((waits, fn, sem, 1))
        if en == "pe":
            e.known[id(sem)] = e.count
        self._commit(ev, reads, writes)
        return ev

    def dma(self, qn, fn, reads=(), writes=()):
        reads = self._expand(reads); writes = self._expand(writes)
        e = self.E[qn]
        waits = self._waits(e, reads, writes)
        slot = e.dnext
        e.dnext = (e.dnext + 1) % len(e.dslots)
        sem = e.dslots[slot]
        prev = e.duse[slot] * 16
        if prev and e.known.get(id(sem), 0) < prev:
            waits.append((sem, prev))
            e.known[id(sem)] = prev
        e.duse[slot] += 1
        ev = (sem, e.duse[slot] * 16)
        e.prog.append((waits, fn, sem, 16))
        self._commit(ev, reads, writes)
        return ev

    def finish(self, final_bufs):
        e = self.E["sp"]
        fw = []
        seen = {}
        for b in final_bufs:
            for ev in ([b.w] if b.w else []) + list(b.r.values()):
                k = id(ev[0])
                if k not in seen or seen[k][1] < ev[1]:
                    seen[k] = ev
        fw = list(seen.values())
        with self.nc.Block() as block:
            def emit(en):
                eng = self.E[en]
                def body(h):
                    for waits, fn, sem, inc in eng.prog:
                        for s, v in waits:
                            h.wait_ge(s, v)
                        fn(h).then_inc(sem, inc)
                    if en == "sp":
                        for s, v in fw:
                            h.wait_ge(s, v)
                return body
            block.tensor(emit("pe"))
            block.scalar(emit("act"))
            block.vector(emit("dve"))
            block.gpsimd(emit("pool"))
            block.sync(emit("sp"))


SEQS = [(0, 2, "p"), (2, 2, "p"), (4, 8, "s")]


class P:
    pass


def build(nc, cfg=None):
    cfg = cfg or {}
    st = contextlib.ExitStack()
    k = K(nc, st)
    g = P()
    g.k = k
    g.cfg = cfg
    ein = lambda n, s: k.dram(n, s, F32, "ExternalInput")
    eout = lambda n, s: k.dram(n, s, F32, "ExternalOutput")
    g.xin = ein("xin", [NTOK, D])
    g.cond = ein("cond", [2, D])
    g.l0ckv = ein("l0ckv", [256, 512]); g.l0kr = ein("l0kr", [256, 64])
    g.l1k = ein("l1k", [256, 512]); g.l1v = ein("l1v", [256, 512])
    g.ada_w = ein("ada_w", [DEPTH, D, 6 * D]); g.ada_b = ein("ada_b", [DEPTH, 6 * D])
    g.ln_g = [ein("ln1_g", [DEPTH, D]), ein("ln2_g", [DEPTH, D])]
    g.ln_b = [ein("ln1_b", [DEPTH, D]), ein("ln2_b", [DEPTH, D])]
    g.mla_w_dq = ein("mla_w_dq", [D, 512]); g.mla_q_norm = ein("mla_q_norm", [1, 512])
    g.mla_w_uq = ein("mla_w_uq", [512, 3072]); g.mla_w_dkv = ein("mla_w_dkv", [D, 576])
    g.mla_kv_norm = ein("mla_kv_norm", [1, 512]); g.mla_w_uk = ein("mla_w_uk", [512, D])
    g.mla_w_uv = ein("mla_w_uv", [512, D]); g.mla_w_o = ein("mla_w_o", [D, D])
    g.gqa_w_qkv = ein("gqa_w_qkv", [D, 3072]); g.gqa_sink = ein("gqa_sink", [1, 32])
    g.gqa_w_o = ein("gqa_w_o", [D, D]); g.fnet_w_out = ein("fnet_w_out", [D, D])
    g.conv_w_in = ein("conv_w_in", [D, 3 * D]); g.conv_w = ein("conv_w", [3, D])
    g.conv_b = ein("conv_b", [1, D]); g.conv_w_out = ein("conv_w_out", [D, D])
    g.peer_w_q = ein("peer_w_q", [DEPTH, D, D]); g.peer_keys = ein("peer_sub_keys", [DEPTH * 16 * 128, 128])
    g.peer_u = ein("peer_u", [cfg.get("nexp", DEPTH * 16384), D]); g.peer_v = ein("peer_v", [cfg.get("nexp", DEPTH * 16384), D])
    g.c_ident = ein("c_ident", [128, 128])
    g.c_rcos = ein("c_rcos", [NTOK, 64]); g.c_rsin = ein("c_rsin", [NTOK, 64])
    g.c_cc = ein("c_cc", [512, 512]); g.c_sc = ein("c_sc", [512, 512])
    g.c_ct = {256: ein("c_ct256", [256, 256]), 1024: ein("c_ct1024", [1024, 1024])}
    g.c_st = {256: ein("c_st256", [256, 256]), 1024: ein("c_st1024", [1024, 1024])}
    g.c_mprev = ein("c_mprev", [128, 128]); g.c_mnext = ein("c_mnext", [128, 128])
    g.c_iota = ein("c_iota", [128, 256])
    g.y = eout("y", [NTOK, D])
    g.o_ckv = eout("o_ckv", [NP_TOK, 512]); g.o_kr = eout("o_kr", [NP_TOK, 64])
    g.o_k = eout("o_k", [NP_TOK, 512]); g.o_v = eout("o_v", [NP_TOK, 512])
    g.dbg = {}
    for name, shape in cfg.get("dbg", {}).items():
        g.dbg[name] = eout("dbg_" + name, shape)
    g.X = k.dram("X", [NTOK, D]); g.Y = k.dram("Y", [NTOK, D]); g.U2 = k.dram("U2", [NTOK, D])
    g.UT = k.dram("UT", [128, 16, NTOK], BF16)
    g.MOD = k.dram("MOD", [DEPTH * 2, 6 * D])
    g.SC = k.dram("SC", [NTOK, D])
    g.SC2 = k.dram("SC2", [NTOK, 3 * D])
    g.FT = k.dram("FT", [128, 16, NTOK + 256], BF16)
    g.FT2 = k.dram("FT2", [128, 16, NTOK + 256], BF16)
    g.Zp = k.dram("Zp", [NTOK + 4, D])
    g.VB = k.dram("VB", [NTOK + 256, 512], BF16)
    g.B = [k.sb(f"B{i}", [128, 4096]) for i in range(8)]
    g.T = [k.sb(f"T{i}", [128, 2048]) for i in range(7)]
    g.identf = k.sb("identf", [128, 128]); g.identb = k.sb("identb", [128, 128], BF16)
    g.csT = k.sb("csT", [128, 16, 2])
    g.modT = k.sb("modT", [128, 2, 4, 16])
    g.sm = [k.sb(f"sm{i}", [128, 256]) for i in range(8)]
    g.psA = k.ps("psA", [128, 1536])
    g.psb = [k.ps(f"ps{i}", [128, 512]) for i in range(5)]
    g.psA3 = [Buf(f"psA{j}", g.psA.t) for j in range(3)]
    for b_ in g.psA3:
        b_.alias = [g.psA]
    g.psA.alias = list(g.psA3)
    g.Gd = k.dram("Gd", [128, 128, NTOK], BF16)
    g.rr = {}
    g.pk_iall = k.sb("pk_iall", [128, 256], U32); g.pk_p8all = k.sb("pk_p8all", [128, 128], U32)
    g.pk_i1f = k.sb("pk_i1f", [128, 128]); g.pk_i2f = k.sb("pk_i2f", [128, 128])
    g.pk_idx = k.sb("pk_idx", [128, 8], U32); g.pk_gate = k.sb("pk_gate", [128, 128]); g.pk_h = k.sb("pk_h", [128, 256])
    g.pk_v = k.sb("pk_v", [128, 32]); g.pk_i = k.sb("pk_i", [128, 32], U32); g.pk_if = k.sb("pk_if", [128, 32])
    g.pk_w = k.sb("pk_w", [128, 128]); g.pk_ca = k.sb("pk_ca", [128, 256]); g.pk_cb = k.sb("pk_cb", [128, 256])
    g.pk_t8 = k.sb("pk_t8", [128, 48]); g.pk_p8 = k.sb("pk_p8", [128, 32], U32); g.pk_pf = k.sb("pk_pf", [128, 48])
    g.pk_oh = k.sb("pk_oh", [128, 256]); g.pk_sel = k.sb("pk_sel", [128, 48]); g.pk_iota = k.sb("pk_iota", [128, 256])
    prologue(g)
    first = cfg.get("first_layer", 0)
    last = cfg.get("last_layer", DEPTH - 1)
    for i in range(first, last + 1):
        layer(g, i)
    epilogue(g)
    finals = [g.y, g.o_ckv, g.o_kr, g.o_k, g.o_v] + list(g.dbg.values())
    k.finish(finals)
    st.close()
    return nc


def subs(g, parent, n):
    key = ("subs", id(parent), n)
    if key not in g.rr:
        lst = [Buf(f"{parent.name}_s{j}", parent.t) for j in range(n)]
        for b_ in lst:
            b_.alias = [parent]
        parent.alias = list(parent.alias) + lst
        g.rr[key] = lst
    return g.rr[key]


def rot(g, name, bufs):
    i = g.rr.get(name, 0)
    g.rr[name] = i + 1
    return bufs[i % len(bufs)]


def bf(ap):
    return ap.bitcast(BF16)


def ld(g, dst_buf, dst_ap, src_buf, src_ap, q="sp", **kw):
    g.k.dma(q, lambda h: h.dma_start(out=dst_ap, in_=src_ap, **kw), reads=[src_buf], writes=[dst_buf])


def st_(g, dst_buf, dst_ap, src_buf, src_ap, q="pool", **kw):
    g.k.dma(q, lambda h: h.dma_start(out=dst_ap, in_=src_ap, **kw), reads=[src_buf], writes=[dst_buf])


def prologue(g):
    k = g.k
    ld(g, g.identf, g.identf[:], g.c_ident, g.c_ident[:, :])
    k.op("dve", lambda h: h.tensor_copy(out=g.identb[:], in_=g.identf[:]), reads=[g.identf], writes=[g.identb])
    ld(g, g.pk_iota, g.pk_iota[:], g.c_iota, g.c_iota[:, :])
    g.pk_iotab = g.k.sb("pk_iotab", [128, 64])
    cp(g, "dve", g.pk_iotab, bf(g.pk_iotab[:])[:, 0:128], g.pk_iota, g.pk_iota[:, 0:128])
    g.pk_thr = g.pk_iota[:, 128:144]
    tsc(g, "dve", g.pk_iota, g.pk_thr, g.pk_iota, g.pk_iota[:, 0:16], 1.0, 16.0, ALU.add, ALU.mult)
    for r in range(2):
        ld(g, g.csT, g.csT[:, :, r], g.cond, g.cond[r, :].rearrange("(k p) -> p k", p=128),
           allow_slow_non_contiguous=True)
    k.op("act", lambda h: h.activation(out=g.csT[:], in_=g.csT[:], func=AF.Silu), reads=[g.csT], writes=[g.csT])


def epilogue(g):
    pass


def mm(g, pb, out, lb, lhsT, rb, rhs, start=True, stop=True):
    g.k.op("pe", lambda h: h.matmul(out, lhsT=lhsT, rhs=rhs, start=start, stop=stop), reads=[lb, rb], writes=[pb])


def tr(g, pb, out, sb, in_, fp32=True):
    ib = g.identf if fp32 else g.identb
    np_ = in_.shape[0]
    ident = ib[0:np_, 0:np_]
    g.k.op("pe", lambda h: h.transpose(out, in_, ident), reads=[sb, ib], writes=[pb])


def act(g, ob, out, ib, in_, func, bias=None, scale=None, accum=None, rd=(), wr=(), eng="act"):
    kw = {}
    if bias is not None:
        kw["bias"] = bias
    if scale is not None:
        kw["scale"] = scale
    if accum is not None:
        kw["accum_out"] = accum
    g.k.op("act", lambda h: h.activation(out=out, in_=in_, func=func, **kw), reads=[ib, *rd], writes=[ob, *wr])


def tsc(g, eng, ob, out, ib, in0, s1, s2=None, op0=ALU.mult, op1=None, rd=(), wr=(), accum=None):
    kw = {}
    if op1 is not None:
        kw["op1"] = op1
    if accum is not None:
        kw["accum_out"] = accum
    g.k.op(eng, lambda h: h.tensor_scalar(out=out, in0=in0, scalar1=s1, scalar2=s2, op0=op0, **kw),
           reads=[ib, *rd], writes=[ob, *wr])


def tt(g, eng, ob, out, ab, a, bb, b, op):
    g.k.op(eng, lambda h: h.tensor_tensor(out=out, in0=a, in1=b, op=op), reads=[ab, bb], writes=[ob])


def stt(g, ob, out, ab, a, scalar, bb, b, op0, op1, rd=(), accum=None, wr=()):
    kw = {}
    if accum is not None:
        kw["accum_out"] = accum
    g.k.op("dve", lambda h: h.scalar_tensor_tensor(out=out, in0=a, scalar=scalar, in1=b, op0=op0, op1=op1, **kw),
           reads=[ab, bb, *rd], writes=[ob, *wr])


def cp(g, eng, ob, out, ib, in_):
    if eng == "act":
        g.k.op("act", lambda h: h.copy(out=out, in_=in_), reads=[ib], writes=[ob])
    else:
        g.k.op(eng, lambda h: h.tensor_copy(out=out, in_=in_), reads=[ib], writes=[ob])


def rows(t, n=1):
    return slice(t * 128, (t + n) * 128)


def bcast_row(ap_row):
    return ap_row.partition_broadcast(128).rearrange("p o n -> p (o n)")


def ada(g, i):
    k = g.k
    for qt in range(4):
        c0 = qt * 3072
        banks = [(g.psb[s], g.psb[s][0:2, 0:512]) for s in range(5)] + [(g.psA, g.psA[0:2, 0:512])]
        for kc in range(16):
            wb = rot(g, "adaw", [g.B[0], g.B[1], g.B[2]])
            ld(g, wb, wb[:, 0:3072], g.ada_w, g.ada_w[i, kc * 128:(kc + 1) * 128, c0:c0 + 3072])
            for s in range(6):
                pb, out = banks[s]
                mm(g, pb, out, g.csT, g.csT[:, kc, :], wb, wb[:, s * 512:(s + 1) * 512], kc == 0, kc == 15)
        bt = g.B[3]
        for r in range(2):
            ld(g, bt, bt[r:r + 1, 0:3072], g.ada_b, g.ada_b[i:i + 1, c0:c0 + 3072])
        for s in range(6):
            pb, out = banks[s]
            tt(g, "dve", bt, bt[0:2, s * 512:(s + 1) * 512], pb, out, bt, bt[0:2, s * 512:(s + 1) * 512], ALU.add)
        st_(g, g.MOD, g.MOD[2 * i:2 * i + 2, c0:c0 + 3072], bt, bt[0:2, 0:3072])


def ada_piece(g, i, piece):
    c0 = piece * 1536
    banks = [(g.psA3[s], g.psA.t[0:2, s * 512:(s + 1) * 512]) for s in range(3)]
    for kc in range(16):
        wb = rot(g, "adaw2", [g.B[4], g.B[5], g.B[6]])
        ld(g, wb, wb[:, 0:1536], g.ada_w, g.ada_w[i, kc * 128:(kc + 1) * 128, c0:c0 + 1536])
        for s in range(3):
            pb, out = banks[s]
            mm(g, pb, out, g.csT, g.csT[:, kc, :], wb, wb[:, s * 512:(s + 1) * 512], kc == 0, kc == 15)
    bt = g.B[7]
    for r in range(2):
        ld(g, bt, bt[r:r + 1, 0:1536], g.ada_b, g.ada_b[i:i + 1, c0:c0 + 1536])
    for s in range(3):
        pb, out = banks[s]
        tt(g, "dve", bt, bt[0:2, s * 512:(s + 1) * 512], pb, out, bt, bt[0:2, s * 512:(s + 1) * 512], ALU.add)
    st_(g, g.MOD, g.MOD[2 * i:2 * i + 2, c0:c0 + 1536], bt, bt[0:2, 0:1536])


def load_modT(g, i, which):
    for r in range(2):
        for slot, j in ((0, 3 * which), (1, 3 * which + 1)):
            ld(g, g.modT, g.modT[:, r, slot, :],
               g.MOD, g.MOD[2 * i + r, j * D:(j + 1) * D].rearrange("(k p) -> p k", p=128),
               allow_slow_non_contiguous=True)
    tsc(g, "dve", g.modT, g.modT[:, :, 1, :], g.modT, g.modT[:, :, 1, :], 1.0, None, ALU.add)


def load_bc(g, i, which):
    gj = 3 * which + 2
    for r in range(2):
        ld(g, g.B[0], g.B[0][:, r * D:(r + 1) * D], g.MOD, bcast_row(g.MOD[2 * i + r:2 * i + r + 1, gj * D:(gj + 1) * D]))
    ld(g, g.B[1], g.B[1][:, 0:D], g.ln_g[which], bcast_row(g.ln_g[which][i:i + 1, :]))
    ld(g, g.B[1], g.B[1][:, D:2 * D], g.ln_b[which], bcast_row(g.ln_b[which][i:i + 1, :]))


def emit_ut(g, xb, t, r, dst=None, mod=True):
    dst = dst or g.UT
    stg = rot(g, "utst", [g.T[6], g.T[5]])
    sv = bf(stg[:]).rearrange("p (k n) -> p k n", k=32)
    for q4 in range(4):
        pb = rot(g, "trps", [g.psb[0], g.psb[1]])
        for j in range(4):
            kc = q4 * 4 + j
            tr(g, pb, pb[:, j * 128:(j + 1) * 128], xb, xb[:, kc * 128:(kc + 1) * 128])
        if mod:
            for j in range(4):
                kc = q4 * 4 + j
                act(g, stg, sv[:, kc, :], pb, pb[:, j * 128:(j + 1) * 128], AF.Identity,
                    bias=g.modT[:, r, 0, kc:kc + 1], scale=g.modT[:, r, 1, kc:kc + 1], rd=[g.modT])
        else:
            cp(g, "act", stg, sv[:, q4 * 4:q4 * 4 + 4, :], pb, pb[:, 0:512].rearrange("p (c n) -> p c n", c=4))
    st_(g, dst, dst[:, 0:16, rows(t)], stg, sv[:, 0:16, :], q="act")


def load_w(g, W_buf, W_ap, nK, c0, w):
    wb = rot(g, "wbf", [g.B[4], g.B[5]])
    wv = bf(wb[:])[:, 0:nK * w].rearrange("p (k n) -> p k n", k=nK)
    kstep = 4
    for k0 in range(0, nK, kstep):
        kn = min(kstep, nK - k0)
        ld(g, wb, wv[:, k0:k0 + kn, :], W_buf, W_ap[k0 * 128:(k0 + kn) * 128, c0:c0 + w].rearrange("(k p) n -> p k n", p=128), q="pool")
    return wb, wv


def ut_lhs(g, src, nK=16, bufs=None, kc0=0):
    state = {}
    bufs = bufs or [g.B[6], g.B[7]]

    def fn(t):
        grp = t // 4
        if state.get("grp") != grp:
            ub = rot(g, "ug", bufs)
            uv = bf(ub[:])[:, 0:nK * 512].rearrange("p (k n) -> p k n", k=nK)
            ld(g, ub, uv, src, src[:, kc0:kc0 + nK, grp * 512:(grp + 1) * 512])
            state.update(grp=grp, ub=ub, uv=uv)
        uv = state["uv"]
        j = t % 4
        return state["ub"], (lambda kc: uv[:, kc, j * 128:(j + 1) * 128])
    return fn


def linear_tm(g, lhs_fn, nK, W_buf, W_ap, slices, tiles, consume):
    for si, (c0, w) in enumerate(slices):
        wb, wv = load_w(g, W_buf, W_ap, nK, c0, w)
        for t in tiles:
            lb, lfn = lhs_fn(t)
            pb = rot(g, "linps", [g.psb[2], g.psb[3], g.psb[4]])
            for kc in range(nK):
                mm(g, pb, pb[:, 0:w], lb, lfn(kc), wb, wv[:, kc, :], kc == 0, kc == nK - 1)
            consume(t, si, c0, w, pb)


def pn_consume(g):
    def consume(t, si, c0, w, pb):
        r = 0 if t < 4 else 1
        for h0 in range(0, w, 256):
            hw = min(256, w - h0)
            xs = rot(g, "pnx", [g.sm[0], g.sm[1], g.sm[2]])
            ys = rot(g, "pny", [g.sm[3], g.sm[4], g.sm[5]])
            ld(g, xs, xs[:, 0:hw], g.X, g.X[rows(t), c0 + h0:c0 + h0 + hw])
            tt(g, "dve", ys, ys[:, 0:hw], pb, pb[:, h0:h0 + hw], g.B[0], g.B[0][:, r * D + c0 + h0:r * D + c0 + h0 + hw], ALU.mult)
            stt(g, ys, ys[:, 0:hw], xs, xs[:, 0:hw], ALPHA, ys, ys[:, 0:hw], ALU.mult, ALU.add)
            st_(g, g.Y, g.Y[rows(t), c0 + h0:c0 + h0 + hw], ys, ys[:, 0:hw])
    return consume


def lnt(g, i, which):
    last = (i == DEPTH - 1 and which == 1) or (g.cfg.get("last_layer", DEPTH - 1) == i and which == 1)
    for t in range(NT):
        r = 0 if t < 4 else 1
        yt = rot(g, "lny", [g.T[0], g.T[1]])
        ld(g, yt, yt[:], g.Y, g.Y[rows(t), :])
        s6 = rot(g, "lns", [g.sm[6], g.sm[7]])
        for c in range(4):
            g.k.op("dve", lambda h, c=c, s6=s6, yt=yt: h.bn_stats(out=s6[:, c * 6:(c + 1) * 6], in_=yt[:, c * 512:(c + 1) * 512]),
                   reads=[yt], writes=[s6])
        mv = s6[:, 32:34]
        g.k.op("dve", lambda h, s6=s6, mv=mv: h.bn_aggr(out=mv, in_=s6[:, 0:24]), reads=[s6], writes=[s6])
        rs = s6[:, 40:41]
        tsc(g, "dve", s6, rs, s6, s6[:, 33:34], LN_EPS, None, ALU.add)
        act(g, s6, rs, s6, rs, AF.Sqrt)
        g.k.op("dve", lambda h, rs=rs: h.reciprocal(out=rs, in_=rs), reads=[s6], writes=[s6])
        tsc(g, "dve", yt, yt[:], yt, yt[:], s6[:, 32:33], rs, ALU.subtract, ALU.mult, rd=[s6])
        tt(g, "pool", yt, yt[:], yt, yt[:], g.B[1], g.B[1][:, 0:D], ALU.mult)
        tt(g, "pool", yt, yt[:], yt, yt[:], g.B[1], g.B[1][:, D:2 * D], ALU.add)
        if last:
            st_(g, g.y, g.y[rows(t), :], yt, yt[:])
            continue
        st_(g, g.X, g.X[rows(t), :], yt, yt[:])
        emit_ut(g, yt, t, r)


def layer(g, i):
    cfg = g.cfg
    if i == cfg.get("first_layer", 0):
        ada(g, i)
        load_modT(g, i, 0)
        for t in range(NT):
            r = 0 if t < 4 else 1
            xt = rot(g, "lny", [g.T[0], g.T[1]])
            ld(g, xt, xt[:], g.xin, g.xin[rows(t), :])
            st_(g, g.X, g.X[rows(t), :], xt, xt[:])
            emit_ut(g, xt, t, r)
    g.pre_pn = lambda: load_bc(g, i, 0)
    mixer = cfg.get("mixer", {}).get(i, i % 4)
    if mixer == 0:
        mixer_mla(g)
    elif mixer == 1:
        mixer_gqa(g)
    elif mixer == 2:
        mixer_fnet(g)
    elif mixer == 3:
        mixer_conv(g)
    else:
        g.pre_pn()
        linear_tm(g, ut_lhs(g, g.UT), 16, g.conv_w_out, g.conv_w_out[:, :], [(c, 512) for c in range(0, D, 512)],
                  range(NT), pn_consume(g))
    load_modT(g, i, 1)
    lnt(g, i, 0)
    if cfg.get("stop_after_mixer"):
        return
    if i + 1 < DEPTH and g.cfg.get("no_ada_overlap"):
        ada(g, i + 1)
    peer(g, i)
    load_bc(g, i, 1)
    if i + 1 < DEPTH:
        load_modT(g, i + 1, 0)
    lnt(g, i, 1)


def load_rope_tabs(g):
    tab = g.T[3]
    tv = tab[:, 0:NT * 128].rearrange("p (t c) -> p t c", t=NT)
    ld(g, tab, tv[:, :, 0:64], g.c_rcos, g.c_rcos[:, :].rearrange("(t p) c -> p t c", p=128))
    ld(g, tab, tv[:, :, 64:128], g.c_rsin, g.c_rsin[:, :].rearrange("(t p) c -> p t c", p=128))
    return tab, tv


def rope(g, xb, x3, H, t, tmp_a, tmp_b):
    tab = g.T[3]
    tv = tab[:, 0:NT * 128].rearrange("p (t c) -> p t c", t=NT)
    a3 = tmp_a[:, 0:H * 64].rearrange("p (h c) -> p h c", h=H)
    b3 = tmp_b[:, 0:H * 64].rearrange("p (h c) -> p h c", h=H)
    cosb = tv[:, t, 0:64].unsqueeze(1).to_broadcast([128, H, 64])
    tt(g, "dve", tmp_a, a3, xb, x3, tab, cosb, ALU.mult)
    for qa, qb in ((0, 1), (1, 0), (2, 3), (3, 2)):
        sinb = tv[:, t, 64 + qa * 16:64 + (qa + 1) * 16].unsqueeze(1).to_broadcast([128, H, 16])
        tt(g, "pool" if H > 4 else "dve", tmp_b, b3[:, :, qa * 16:(qa + 1) * 16], xb, x3[:, :, qb * 16:(qb + 1) * 16], tab, sinb, ALU.mult)
    tt(g, "dve", xb, x3, tmp_a, a3, tmp_b, b3, ALU.add)


def tm2ft(g, xb, xap, w, dstb, dst_ap, fp32=True):
    pb = rot(g, "trps", [g.psb[0], g.psb[1]])
    if fp32:
        tr(g, pb, pb[0:w, 0:128], xb, xap, fp32=True)
        cp(g, "act", dstb, dst_ap, pb, pb[0:w, 0:128])
    else:
        pv = bf(pb[:])
        tr(g, pb, pv[0:w, 0:128], xb, xap, fp32=False)
        cp(g, "act", dstb, dst_ap, pb, pv[0:w, 0:128])


def rms_consume(g, norm_ap, dst, ch0, hook=None):
    def consume(t, si, c0, w, pb):
        st6 = rot(g, "lns", [g.sm[6], g.sm[7]])
        junk = rot(g, "rmsj", [g.T[4], g.T[5]])
        act(g, junk, junk[:, 0:512], pb, pb[:, 0:512], AF.Square, accum=st6[:, 0:1], wr=[st6])
        tsc(g, "dve", st6, st6[:, 1:2], st6, st6[:, 0:1], 1.0 / 512.0, RMS_EPS, ALU.mult, ALU.add)
        act(g, st6, st6[:, 1:2], st6, st6[:, 1:2], AF.Sqrt)
        g.k.op("dve", lambda h: h.reciprocal(out=st6[:, 2:3], in_=st6[:, 1:2]), reads=[st6], writes=[st6])
        stt(g, junk, junk[:, 512:1024], pb, pb[:, 0:512], st6[:, 2:3], g.T[2], norm_ap, ALU.mult, ALU.mult, rd=[st6])
        if hook:
            hook(t, junk, junk[:, 512:1024])
        stg = rot(g, "rmst", [g.sm[3], g.sm[4]])
        sv = bf(stg[:]).rearrange("p (k n) -> p k n", k=4)
        pt = rot(g, "trps", [g.psb[0], g.psb[1]])
        for j in range(4):
            tr(g, pt, pt[:, j * 128:(j + 1) * 128], junk, junk[:, 512 + j * 128:512 + (j + 1) * 128])
        cp(g, "act", stg, sv, pt, pt[:, 0:512].rearrange("p (k n) -> p k n", k=4))
        st_(g, dst, dst[:, ch0:ch0 + 4, rows(t)], stg, sv, q="act")
    return consume


def attention(g, S, qk_parts, v_fn, nblk, out_fn, scale, bias_fn=None, sink_ap=None, ranges=None, vblocks=None):
    k = g.k
    if S <= 512:
        ps = rot(g, "attps", g.psA3)
        pbase = g.psA3.index(ps) * 512
    else:
        ps = g.psA
        pbase = 0
    pst = g.psA.t
    ranges = ranges or [(0, S)]
    vblocks = list(vblocks) if vblocks is not None else list(range(nblk))
    col = 0
    pieces = []
    for (k0, kw) in ranges:
        while kw > 0:
            room = 512 - (col % 512)
            w_ = min(kw, room)
            pieces.append((col, k0, w_))
            col += w_; k0 += w_; kw -= w_
    assert col == S and len(vblocks) == nblk
    for (c0, k0, cw) in pieces:
        for pi, (lb, lhsT, rb, rfn) in enumerate(qk_parts):
            mm(g, ps, pst[:, pbase + c0:pbase + c0 + cw], lb, lhsT, rb, rfn(k0, cw), pi == 0, pi == len(qk_parts) - 1)
    st6 = rot(g, "lns", [g.sm[6], g.sm[7]])
    aset = g.rr.get("attset", 0) % 2
    g.rr["attset"] = g.rr.get("attset", 0) + 1
    pf = [g.T[4], g.B[4]][aset]
    if bias_fn is not None:
        src_b, src = pf, pf[:, 0:S]
        bias_fn(ps, pst[:, pbase:pbase + S], pf, pf[:, 0:S])
    else:
        src_b, src = ps, pst[:, pbase:pbase + S]
    k.op("dve", lambda h: h.reduce_max(out=st6[:, 0:1], in_=src, axis=AX.X), reads=[src_b], writes=[st6])
    if sink_ap is not None:
        tsc(g, "dve", st6, st6[:, 0:1], st6, st6[:, 0:1], scale, sink_ap[1], ALU.mult, ALU.max, rd=[sink_ap[0]])
        tsc(g, "dve", st6, st6[:, 1:2], st6, st6[:, 0:1], -1.0, None, ALU.mult)
    else:
        tsc(g, "dve", st6, st6[:, 1:2], st6, st6[:, 0:1], -scale, None, ALU.mult)
    act(g, pf, pf[:, 0:S], src_b, src, AF.Exp, bias=st6[:, 1:2], scale=scale, accum=st6[:, 2:3], rd=[st6], wr=[st6])
    if sink_ap is not None:
        act(g, st6, st6[:, 3:4], st6, st6[:, 1:2], AF.Exp, bias=sink_ap[1], scale=1.0, rd=[sink_ap[0]])
        tt(g, "dve", st6, st6[:, 2:3], st6, st6[:, 2:3], st6, st6[:, 3:4], ALU.add)
    k.op("dve", lambda h: h.reciprocal(out=st6[:, 4:5], in_=st6[:, 2:3]), reads=[st6], writes=[st6])
    pn = [g.T[5], g.B[5]][aset]
    pnv = bf(pn[:])
    tsc(g, "dve", pn, pnv[:, 0:S], pf, pf[:, 0:S], st6[:, 4:5], None, ALU.mult, rd=[st6])
    pT = [g.T[6], g.B[6]][aset]
    pTv = bf(pT[:])[:, 0:nblk * 128].rearrange("p (b n) -> p b n", b=nblk)
    for b0 in range(0, nblk, 8):
        bn = min(8, nblk - b0)
        pb = rot(g, "trps", [g.psb[0], g.psb[1]])
        pv = bf(pb[:])
        for j in range(bn):
            tr(g, pb, pv[:, j * 128:(j + 1) * 128], pn, pnv[:, (b0 + j) * 128:(b0 + j + 1) * 128], fp32=False)
        cp(g, "act", pT, pTv[:, b0:b0 + bn, :], pb, pv[:, 0:bn * 128].rearrange("p (b n) -> p b n", b=bn))
    po = rot(g, "linps", [g.psb[2], g.psb[3], g.psb[4]])
    dv = None
    for bi, blk in enumerate(vblocks):
        vb, vap = v_fn(blk)
        dv = vap.shape[-1]
        mm(g, po, po[0:dv, 0:128], vb, vap, pT, pTv[:, bi, :], bi == 0, bi == nblk - 1)
    out_fn(po, po[0:dv, 0:128])


def load_w_to(g, W_buf, W_ap, nK, w, dstb):
    wv = bf(dstb[:])[:, 0:nK * w].rearrange("p (k n) -> p k n", k=nK)
    for k0 in range(nK):
        ld(g, dstb, wv[:, k0, :], W_buf, W_ap[k0 * 128:(k0 + 1) * 128, :], q="pool")
    return wv


def mixer_mla(g):
    k = g.k
    MLA_SCALE = 192 ** -0.5
    FT2 = g.FT2
    nrm = g.T[2]
    ld(g, nrm, nrm[:, 0:512], g.mla_q_norm, bcast_row(g.mla_q_norm[0:1, :]))
    ld(g, nrm, nrm[:, 512:1024], g.mla_kv_norm, bcast_row(g.mla_kv_norm[0:1, :]))
    load_rope_tabs(g)
    linear_tm(g, ut_lhs(g, g.UT), 16, g.mla_w_dq, g.mla_w_dq[:, :], [(0, 512)], range(NT),
              rms_consume(g, nrm[:, 0:512], FT2, 0))
    def ckv_hook(t, b, ap):
        if t < 4:
            st_(g, g.o_ckv, g.o_ckv[rows(t), :], b, ap)
    rc = rms_consume(g, nrm[:, 512:1024], FT2, 4, hook=ckv_hook)

    def kr_to_ft(t_col, kb, kap):
        stg = rot(g, "rmst", [g.sm[3], g.sm[4]])
        sv = bf(stg[:])
        tm2ft(g, kb, kap, 64, stg, sv[0:64, 0:128])
        st_(g, FT2, FT2[0:64, 8, t_col * 128:(t_col + 1) * 128], stg, sv[0:64, 0:128], q="act")

    def kv_consume(t, si, c0, w, pb):
        if si == 0:
            return rc(t, si, c0, w, pb)
        kb = rot(g, "pnx", [g.sm[0], g.sm[1], g.sm[2]])
        cp(g, "act", kb, kb[:, 0:64], pb, pb[:, 0:64])
        if t < 4:
            st_(g, g.o_kr, g.o_kr[rows(t), :], kb, kb[:, 0:64])
        rope(g, kb, kb[:, 0:64].rearrange("p (h c) -> p h c", h=1), 1, t, g.sm[5], g.pk_ca)
        kr_to_ft(t, kb, kb[:, 0:64])
    linear_tm(g, ut_lhs(g, g.UT), 16, g.mla_w_dkv, g.mla_w_dkv[:, :], [(0, 512), (512, 64)], range(NT), kv_consume)
    for cb in range(2):
        ct = rot(g, "rmsj", [g.T[4], g.T[5]])
        ld(g, ct, ct[:, 0:512], g.l0ckv, g.l0ckv[rows(cb), :])
        stg = rot(g, "rmst", [g.sm[3], g.sm[4]])
        sv = bf(stg[:]).rearrange("p (k n) -> p k n", k=4)
        pt = rot(g, "trps", [g.psb[0], g.psb[1]])
        for j in range(4):
            tr(g, pt, pt[:, j * 128:(j + 1) * 128], ct, ct[:, j * 128:(j + 1) * 128])
        cp(g, "act", stg, sv, pt, pt[:, 0:512].rearrange("p (k n) -> p k n", k=4))
        st_(g, FT2, FT2[:, 4:8, NTOK + cb * 128:NTOK + (cb + 1) * 128], stg, sv)
        kb = rot(g, "pnx", [g.sm[0], g.sm[1], g.sm[2]])
        ld(g, kb, kb[:, 0:64], g.l0kr, g.l0kr[rows(cb), :])
        kr_to_ft(NT + cb, kb, kb[:, 0:64])
    def q_consume(t, si, c0, w, pb):
        qb = rot(g, "mlaq", [g.T[0], g.T[1]])
        cp(g, "act", qb, qb[:, 0:384], pb, pb[:, 0:384])
        x3 = qb[:, 0:384].rearrange("p (h c) -> p h c", h=2)[:, :, 128:192]
        rope(g, qb, x3, 2, t, g.sm[5], g.pk_ca)
        st_(g, g.SC2, g.SC2[rows(t), c0:c0 + 384], qb, qb[:, 0:384])
    linear_tm(g, ut_lhs(g, FT2, 4), 4, g.mla_w_uq, g.mla_w_uq[:, :], [(c, 384) for c in range(0, 3072, 384)], range(NT), q_consume)
    wuk = load_w_to(g, g.mla_w_uk, g.mla_w_uk[:, :], 4, D, g.B[2])
    wuv = load_w_to(g, g.mla_w_uv, g.mla_w_uv[:, :], 4, D, g.B[3])
    for (t0, nt, kind) in SEQS:
        S = nt * 128 + (256 if kind == "s" else 0)
        nblk = S // 128
        cb_ = g.B[0]
        ckvT = bf(cb_[:])[:, 0:4 * S].rearrange("p (k n) -> p k n", k=4)
        ld(g, cb_, ckvT[:, :, 0:nt * 128], FT2, FT2[:, 4:8, t0 * 128:(t0 + nt) * 128])
        kb_ = g.B[1]
        krT = bf(kb_[:])[0:64, 0:S]
        ld(g, kb_, krT[:, 0:nt * 128], FT2, FT2[0:64, 8, t0 * 128:(t0 + nt) * 128])
        if kind == "s":
            ld(g, cb_, ckvT[:, :, nt * 128:S], FT2, FT2[:, 4:8, NTOK:NTOK + 256])
            ld(g, kb_, krT[:, nt * 128:S], FT2, FT2[0:64, 8, NTOK:NTOK + 256])
        for h in range(16):
            kTb = rot(g, "mlakT", [g.T[0], g.B[7]])
            kTh = bf(kTb[:])[:, 0:S]
            for c0 in range(0, S, 512):
                cw = min(512, S - c0)
                pb = rot(g, "linps", [g.psb[2], g.psb[3], g.psb[4]])
                for kc in range(4):
                    mm(g, pb, pb[:, 0:cw], g.B[2], wuk[:, kc, h * 128:(h + 1) * 128], cb_, ckvT[:, kc, c0:c0 + cw], kc == 0, kc == 3)
                cp(g, "act", kTb, kTh[:, c0:c0 + cw], pb, pb[:, 0:cw])
            vb_ = g.T[1]
            vh = bf(vb_[:])[:, 0:nblk * 128].rearrange("p (b n) -> p b n", b=nblk)
            for b0 in range(0, nblk, 4):
                bn = min(4, nblk - b0)
                pb = rot(g, "linps", [g.psb[2], g.psb[3], g.psb[4]])
                for j in range(bn):
                    for kc in range(4):
                        mm(g, pb, pb[:, j * 128:(j + 1) * 128], cb_, ckvT[:, kc, (b0 + j) * 128:(b0 + j + 1) * 128],
                           g.B[3], wuv[:, kc, h * 128:(h + 1) * 128], kc == 0, kc == 3)
                cp(g, "act", vb_, vh[:, b0:b0 + bn, :], pb, pb[:, 0:bn * 128].rearrange("p (b n) -> p b n", b=bn))
            for tq in range(nt):
                t = t0 + tq
                qs = rot(g, "pnx", [g.sm[0], g.sm[1], g.sm[2]])
                ld(g, qs, qs[:, 0:192], g.SC2, g.SC2[rows(t), h * 192:(h + 1) * 192])
                qT = rot(g, "mqT", [g.pk_cb, g.pk_oh])
                qTv = bf(qT[:])
                tm2ft(g, qs, qs[:, 0:128], 128, qT, qTv[:, 0:128])
                tm2ft(g, qs, qs[:, 128:192], 64, qT, qTv[0:64, 128:256])

                def out_fn(po, oap, h=h, t=t):
                    ob = rot(g, "rmst", [g.sm[3], g.sm[4]])
                    ov = bf(ob[:])[:, 0:128]
                    cp(g, "act", ob, ov, po, oap)
                    st_(g, g.FT, g.FT[:, h, rows(t)], ob, ov, q="act")
                attention(g, S,
                          [(qT, qTv[:, 0:128], kTb, lambda c0, cw, kTh=kTh: kTh[:, c0:c0 + cw]),
                           (qT, qTv[0:64, 128:256], kb_, lambda c0, cw, krT=krT: krT[:, c0:c0 + cw])],
                          lambda blk, vb_=vb_, vh=vh: (vb_, vh[:, blk, :]), nblk, out_fn, MLA_SCALE)
    g.pre_pn()
    linear_tm(g, ut_lhs(g, g.FT), 16, g.mla_w_o, g.mla_w_o[:, :], [(c, 512) for c in range(0, D, 512)],
              range(NT), pn_consume(g))


def mixer_gqa(g):
    k = g.k
    GQA_SCALE = 64 ** -0.5
    FT2 = g.FT2
    VB = g.VB
    load_rope_tabs(g)
    sinkb = g.pk_w
    ld(g, sinkb, sinkb[:, 0:32], g.gqa_sink, bcast_row(g.gqa_sink[0:1, :]))

    def k_to_ft(col_tile, kb, kap512):
        stg = rot(g, "rmst", [g.sm[3], g.sm[4]])
        sv = bf(stg[:])[0:64, 0:512].rearrange("p (h n) -> p h n", h=4)
        for h4 in range(2):
            pt = rot(g, "trps", [g.psb[0], g.psb[1]])
            for j in range(4):
                tr(g, pt, pt[0:64, j * 128:(j + 1) * 128], kb, kap512[:, (h4 * 4 + j) * 64:(h4 * 4 + j + 1) * 64])
            stg = rot(g, "rmst", [g.sm[3], g.sm[4]])
            sv = bf(stg[:])[0:64, 0:512].rearrange("p (h n) -> p h n", h=4)
            cp(g, "act", stg, sv, pt, pt[0:64, 0:512].rearrange("p (h n) -> p h n", h=4))
            st_(g, FT2, FT2[0:64, h4 * 4:h4 * 4 + 4, col_tile * 128:(col_tile + 1) * 128], stg, sv, q="act")

    def v_to_vb(row_tile, vb_, vap512):
        stg = rot(g, "rmst", [g.sm[3], g.sm[4]])
        sv = bf(stg[:])[:, 0:512]
        cp(g, "dve", stg, sv, vb_, vap512)
        st_(g, VB, VB[row_tile * 128:(row_tile + 1) * 128, :], stg, sv)

    def consume(t, si, c0, w, pb):
        qb = rot(g, "mlaq", [g.T[0], g.T[1]])
        cp(g, "act", qb, qb[:, 0:512], pb, pb[:, 0:512])
        if si == 4 and t < 4:
            st_(g, g.o_k, g.o_k[rows(t), :], qb, qb[:, 0:512])
        if si == 5:
            if t < 4:
                st_(g, g.o_v, g.o_v[rows(t), :], qb, qb[:, 0:512])
            v_to_vb(t, qb, qb[:, 0:512])
            return
        rope(g, qb, qb[:, 0:512].rearrange("p (h c) -> p h c", h=8), 8, t, g.T[4], g.T[5])
        if si < 4:
            st_(g, g.SC2, g.SC2[rows(t), c0:c0 + 512], qb, qb[:, 0:512])
        else:
            k_to_ft(t, qb, qb[:, 0:512])
    linear_tm(g, ut_lhs(g, g.UT), 16, g.gqa_w_qkv, g.gqa_w_qkv[:, :], [(c, 512) for c in range(0, 3072, 512)], range(NT), consume)
    for cb in range(2):
        kb = rot(g, "mlaq", [g.T[0], g.T[1]])
        ld(g, kb, kb[:, 0:512], g.l1k, g.l1k[rows(cb), :])
        k_to_ft(NT + cb, kb, kb[:, 0:512])
        vb_ = rot(g, "mlaq", [g.T[0], g.T[1]])
        ld(g, vb_, vb_[:, 0:512], g.l1v, g.l1v[rows(cb), :])
        v_to_vb(NT + cb, vb_, vb_[:, 0:512])
    mk = g.B[0]
    mkv = mk[:, 0:3 * 640].rearrange("p (m n) -> p m n", m=3)
    k.op("pool", lambda h: h.memset(mk[:, 0:3 * 640], 0.0), reads=[], writes=[mk])
    ld(g, mk, mkv[:, 0, 128:256], g.c_mnext, g.c_mnext[:, :])
    ld(g, mk, mkv[:, 1, 0:128], g.c_mprev, g.c_mprev[:, :])
    ld(g, mk, mkv[:, 1, 256:384], g.c_mnext, g.c_mnext[:, :])
    ld(g, mk, mkv[:, 2, 0:128], g.c_mprev, g.c_mprev[:, :])
    for (t0, nt, kind) in SEQS:
        Sall = nt * 128 + (256 if kind == "s" else 0)
        nball = Sall // 128
        for kvh in range(8):
            kTb = rot(g, "gqakT", [g.T[0], g.T[2]])
            kT = bf(kTb[:])[0:64, 0:Sall]
            ld(g, kTb, kT[:, 0:nt * 128], FT2, FT2[0:64, kvh, t0 * 128:(t0 + nt) * 128])
            vb_ = rot(g, "gqavv", [g.T[1], g.B[7]])
            vv = bf(vb_[:])[:, 0:nball * 64].rearrange("p (b n) -> p b n", b=nball)
            ld(g, vb_, vv[:, 0:nt, :], VB, VB[t0 * 128:(t0 + nt) * 128, kvh * 64:(kvh + 1) * 64].rearrange("(b p) n -> p b n", p=128))
            if kind == "s":
                ld(g, kTb, kT[:, nt * 128:Sall], FT2, FT2[0:64, kvh, NTOK:NTOK + 256])
                ld(g, vb_, vv[:, nt:nball, :], VB, VB[NTOK:NTOK + 256, kvh * 64:(kvh + 1) * 64].rearrange("(b p) n -> p b n", p=128))
            for hi in range(4):
                h = kvh * 4 + hi
                for tq in range(nt):
                    t = t0 + tq
                    qs = rot(g, "pnx", [g.sm[0], g.sm[1], g.sm[2]])
                    ld(g, qs, qs[:, 0:64], g.SC2, g.SC2[rows(t), h * 64:(h + 1) * 64])
                    qT = rot(g, "mqT", [g.pk_cb, g.pk_oh])
                    qTv = bf(qT[:])
                    tm2ft(g, qs, qs[:, 0:64], 64, qT, qTv[0:64, 0:128])
                    if kind == "p":
                        ranges = [(0, Sall)]; vblocks = list(range(nball)); bias_fn = None
                    else:
                        lo, hi_ = max(0, tq - 1), min(nt - 1, tq + 1)
                        ranges = [(lo * 128, (hi_ - lo + 1) * 128), (nt * 128, 256)]
                        vblocks = list(range(lo, hi_ + 1)) + [nt, nt + 1]
                        mi = 0 if tq == 0 else (2 if tq == nt - 1 else 1)
                        def bias_fn(psb_, psap, sbb, sbap, mi=mi):
                            n = sbap.shape[-1]
                            tt(g, "dve", sbb, sbap, psb_, psap, mk, mkv[:, mi, 0:n], ALU.add)
                    Sq = sum(w_ for _, w_ in ranges)

                    def out_fn(po, oap, h=h, t=t):
                        ob = rot(g, "rmst", [g.sm[3], g.sm[4]])
                        ov = bf(ob[:])[0:64, 0:128]
                        cp(g, "act", ob, ov, po, oap)
                        st_(g, g.FT, g.FT[(h % 2) * 64:(h % 2 + 1) * 64, h // 2, rows(t)], ob, ov, q="act")
                    attention(g, Sq, [(qT, qTv[0:64, 0:128], kTb, lambda k0, kw, kT=kT: kT[:, k0:k0 + kw])],
                              lambda blk, vb_=vb_, vv=vv: (vb_, vv[:, blk, :]), len(vblocks), out_fn, GQA_SCALE,
                              bias_fn=bias_fn, sink_ap=(sinkb, sinkb[:, h:h + 1]), ranges=ranges, vblocks=vblocks)
    g.pre_pn()
    linear_tm(g, ut_lhs(g, g.FT), 16, g.gqa_w_o, g.gqa_w_o[:, :], [(c, 512) for c in range(0, D, 512)],
              range(NT), pn_consume(g))


def store_consume(g, dst, col0=0, dt_bf=False):
    def consume(t, si, c0, w, pb):
        for h0 in range(0, w, 256):
            hw = min(256, w - h0)
            ys = rot(g, "pny", [g.sm[3], g.sm[4], g.sm[5]])
            if dt_bf:
                yv = bf(ys[:])[:, 0:hw]
            else:
                yv = ys[:, 0:hw]
            cp(g, "act", ys, yv, pb, pb[:, h0:h0 + hw])
            st_(g, dst, dst[rows(t), col0 + c0 + h0:col0 + c0 + h0 + hw], ys, yv, q="act")
    return consume


def mixer_fnet(g):
    k = g.k
    PQ = g.SC2.t.bitcast(BF16)
    PQb = g.SC2
    for gi in range(4):
        for wi, W in enumerate((g.c_cc, g.c_sc)):
            def consume(t, si, c0, w, pb, gi=gi, wi=wi):
                for h0 in (0, 256):
                    ys = rot(g, "pny", [g.sm[3], g.sm[4], g.sm[5]])
                    yv = bf(ys[:])[:, 0:256]
                    cp(g, "act", ys, yv, pb, pb[:, h0:h0 + 256])
                    c = wi * D + gi * 512 + h0
                    st_(g, PQb, PQ[rows(t), c:c + 256], ys, yv, q="act")
            linear_tm(g, ut_lhs(g, g.UT, 4, kc0=4 * gi), 4, W, W[:, :], [(0, 512)], range(NT), consume)
    for (t0, nt, kind) in SEQS:
        T = nt * 128
        mats = []
        for mi, M in enumerate((g.c_ct[T], g.c_st[T])):
            mb = g.B[4 + mi]
            mv = bf(mb[:])[:, 0:nt * T].rearrange("p (k n) -> p k n", k=nt)
            for k0 in range(0, nt, 2):
                stg = rot(g, "wst", [g.T[4], g.T[3]])
                sv = stg[:, 0:2 * T].rearrange("p (k n) -> p k n", k=2)
                ld(g, stg, sv, M, M[k0 * 128:(k0 + 2) * 128, :].rearrange("(k p) n -> p k n", p=128))
                cp(g, "pool", mb, mv[:, k0:k0 + 2, :], stg, sv)
            mats.append((mb, mv))
        pq = []
        for wi in range(2):
            lst = []
            for k0 in range(0, nt, 4):
                kn = min(4, nt - k0)
                pb_ = g.B[wi * 2 + k0 // 4]
                pv = bf(pb_[:])[:, 0:kn * D].rearrange("p (k n) -> p k n", k=kn)
                ld(g, pb_, pv, PQb, PQ[(t0 + k0) * 128:(t0 + k0 + kn) * 128, wi * D:(wi + 1) * D].rearrange("(k p) n -> p k n", p=128))
                lst.append((pb_, pv))
            pq.append(lst)
        for tq in range(nt):
            ft = rot(g, "lny", [g.T[0], g.T[1]])
            for sl in range(4):
                pb = rot(g, "linps", [g.psb[2], g.psb[3], g.psb[4]])
                n = 0
                for wi in range(2):
                    mb, mv = mats[wi]
                    for tk in range(nt):
                        xb_, xv = pq[wi][tk // 4]
                        mm(g, pb, pb[:, 0:512], mb, mv[:, tk, tq * 128:(tq + 1) * 128], xb_, xv[:, tk % 4, sl * 512:(sl + 1) * 512],
                           n == 0, n == 2 * nt - 1)
                        n += 1
                cp(g, "act", ft, ft[:, sl * 512:(sl + 1) * 512], pb, pb[:, 0:512])
            emit_ut(g, ft, t0 + tq, 0, dst=g.FT, mod=False)
    g.pre_pn()
    linear_tm(g, ut_lhs(g, g.FT), 16, g.fnet_w_out, g.fnet_w_out[:, :], [(c, 512) for c in range(0, D, 512)],
              range(NT), pn_consume(g))


def mixer_conv(g):
    k = g.k
    BCH = g.SC2
    linear_tm(g, ut_lhs(g, g.UT), 16, g.conv_w_in, g.conv_w_in[:, :], [(c, 512) for c in range(0, 3 * D, 512)],
              range(NT), store_consume(g, BCH))
    Z = g.Zp
    zt = g.T[2]
    k.op("pool", lambda h: h.memset(zt[:], 0.0), reads=[], writes=[zt])
    for si, (t0, nt, kind) in enumerate(SEQS):
        for rr_ in (t0 * 128 + si, (t0 + nt) * 128 + si + 1):
            st_(g, Z, Z[rr_:rr_ + 1, :], zt, zt[0:1, :])
    for j in range(3):
        bb = g.B[2 + j // 2]
        ld(g, bb, bb[:, (j % 2) * D:(j % 2 + 1) * D], g.conv_w, bcast_row(g.conv_w[j:j + 1, :]))
    ld(g, g.B[3], g.B[3][:, D:2 * D], g.conv_b, bcast_row(g.conv_b[0:1, :]))
    for si, (t0, nt, kind) in enumerate(SEQS):
        for t in range(t0, t0 + nt):
            ct = rot(g, "cvc", [g.T[2], g.T[3]])
            ht = rot(g, "lny", [g.T[0], g.T[1]])
            ld(g, ct, ct[:], BCH, BCH[rows(t), D:2 * D])
            ld(g, ht, ht[:], BCH, BCH[rows(t), 2 * D:3 * D])
            tt(g, "dve", ct, ct[:], ct, ct[:], ht, ht[:], ALU.mult)
            st_(g, Z, Z[t * 128 + si + 1:t * 128 + si + 129, :], ct, ct[:])
    for si, (t0, nt, kind) in enumerate(SEQS):
        for t in range(t0, t0 + nt):
            acc = rot(g, "lny", [g.T[0], g.T[1]])
            base = t * 128 + si + 1
            for j in range(3):
                zt_ = rot(g, "cvc", [g.T[2], g.T[3]])
                ld(g, zt_, zt_[:], Z, Z[base + j - 1:base + j - 1 + 128, :])
                wbc = g.B[2 + j // 2][:, (j % 2) * D:(j % 2 + 1) * D]
                if j == 0:
                    tt(g, "dve", acc, acc[:], zt_, zt_[:], g.B[2], wbc, ALU.mult)
                else:
                    tt(g, "pool", zt_, zt_[:], zt_, zt_[:], g.B[2 + j // 2], wbc, ALU.mult)
                    tt(g, "dve", acc, acc[:], acc, acc[:], zt_, zt_[:], ALU.add)
            tt(g, "dve", acc, acc[:], acc, acc[:], g.B[3], g.B[3][:, D:2 * D], ALU.add)
            bt = rot(g, "cvc", [g.T[2], g.T[3]])
            ld(g, bt, bt[:], BCH, BCH[rows(t), 0:D])
            tt(g, "dve", acc, acc[:], acc, acc[:], bt, bt[:], ALU.mult)
            emit_ut(g, acc, t, 0, dst=g.FT, mod=False)
    g.pre_pn()
    linear_tm(g, ut_lhs(g, g.FT), 16, g.conv_w_out, g.conv_w_out[:, :], [(c, 512) for c in range(0, D, 512)],
              range(NT), pn_consume(g))


def peer(g, i):
    k = g.k
    nexp = 16384
    keyT = g.T[6]
    kT = keyT[:].rearrange("p (c n) -> p c n", c=16)
    for hc4 in range(4):
        raw = rot(g, "lny", [g.T[0], g.T[1]])
        rv = raw[:, 0:512].rearrange("p (c n) -> p c n", c=4)
        ld(g, raw, rv, g.peer_keys,
           g.peer_keys[(i * 16 + hc4 * 4) * 128:(i * 16 + hc4 * 4 + 4) * 128, :].rearrange("(c p) n -> p c n", p=128))
        pb = rot(g, "trps", [g.psb[0], g.psb[1]])
        for j in range(4):
            tr(g, pb, pb[:, j * 128:(j + 1) * 128], raw, rv[:, j, :])
        cp(g, "act", keyT, kT[:, hc4 * 4:hc4 * 4 + 4, :], pb, pb[:, 0:512].rearrange("p (c n) -> p c n", c=4))
    Wq = g.peer_w_q
    for sl in range(4):
        wb, wv = load_w(g, Wq, Wq[i, :, :], 16, sl * 512, 512)
        for grp in range(3):
            ub = rot(g, "ug", [g.B[6], g.B[7]])
            uv = bf(ub[:])[:, 0:16 * 512].rearrange("p (k n) -> p k n", k=16)
            ld(g, ub, uv, g.UT, g.UT[:, :, grp * 512:(grp + 1) * 512])
            q4 = g.T[3]
            q4v = q4[:].rearrange("p (c n) -> p c n", c=4)
            for j in range(4):
                pb = rot(g, "linps", [g.psb[2], g.psb[3], g.psb[4]])
                for kc in range(16):
                    mm(g, pb, pb[:, 0:512], wb, wv[:, kc, j * 128:(j + 1) * 128], ub, uv[:, kc, :], kc == 0, kc == 15)
                cp(g, "act", q4, q4v[:, j, :], pb, pb[:, 0:512])
            for tt_ in range(4):
                t = grp * 4 + tt_
                pb = rot(g, "trps", [g.psb[0], g.psb[1]])
                for j in range(4):
                    mm(g, pb, pb[:, j * 128:(j + 1) * 128], q4, q4v[:, j, tt_ * 128:(tt_ + 1) * 128],
                       keyT, kT[:, sl * 4 + j, :])
                ss = rot(g, "pss", [g.sm[0], g.sm[1]])
                sb2 = rot(g, "pss2", [g.sm[2], g.sm[3]])
                cp(g, "dve", ss, ss[:, 0:256], pb, pb[:, 0:256])
                cp(g, "dve", sb2, sb2[:, 0:256], pb, pb[:, 256:512])
                st_(g, g.SC, g.SC[rows(t), sl * 512:sl * 512 + 256], ss, ss[:, 0:256])
                st_(g, g.SC, g.SC[rows(t), sl * 512 + 256:sl * 512 + 512], sb2, sb2[:, 0:256])
    IDX = g.pk_idx; GATE = g.pk_gate; HB = g.pk_h
    V = g.pk_v; Vv = V[:, 0:32].rearrange("p (c n) -> p c n", c=2)
    I = g.pk_i; Iv = I[:, 0:32].rearrange("p (c n) -> p c n", c=2)
    IF = g.pk_if; IFv = IF[:, 0:32].rearrange("p (c n) -> p c n", c=2)
    W = g.pk_w; CA = g.pk_ca; CB = g.pk_cb; T8 = g.pk_t8; P8 = g.pk_p8; PF = g.pk_pf; OH = g.pk_oh; SEL = g.pk_sel
    dve = lambda fn, rd, wr: k.op("dve", fn, reads=rd, writes=wr)
    iota16 = g.pk_iota[:, 0:16]
    for t in range(NT):
        r = 0 if t < 4 else 1
        S = g.T[5]
        ld(g, S, S[:], g.SC, g.SC[rows(t), :])
        Vall = g.pk_ca; V4 = Vall[:, 0:256].rearrange("p (h c k) -> p h c k", h=8, c=2)
        Iall = g.pk_iall; I4 = Iall[:, 0:256].rearrange("p (h c k) -> p h c k", h=8, c=2)
        IFall = g.pk_cb; IF4 = IFall[:, 0:256].rearrange("p (h c k) -> p h c k", h=8, c=2)
        CANDb = g.B[2]
        CAND = CANDb[:, 0:2048].rearrange("p (h n) -> p h n", h=8)
        CAND2 = CANDb[:, 2048:4096].rearrange("p (h n) -> p h n", h=8)
        T8a = g.pk_oh; T8v = T8a[:, 0:128].rearrange("p (h k) -> p h k", h=8)
        PFv = T8a[:, 128:256]
        P8a = g.pk_p8all; P8v = P8a[:, 0:128].rearrange("p (h k) -> p h k", h=8)
        AB = g.pk_h
        for h in range(8):
            for c in range(2):
                s = S[:, (2 * h + c) * 128:(2 * h + c + 1) * 128]
                dve(lambda e, s=s, h=h, c=c: e.max(out=V4[:, h, c, 0:8], in_=s), [S], [Vall])
                dve(lambda e, s=s, h=h, c=c: e.max_index(out=I4[:, h, c, 0:8], in_max=V4[:, h, c, 0:8], in_values=s), [S, Vall], [Iall])
                dve(lambda e, s=s, h=h, c=c: e.match_replace(out=W[:, 0:128], in_to_replace=V4[:, h, c, 0:8], in_values=s, imm_value=NEG), [S, Vall], [W])
                dve(lambda e, h=h, c=c: e.max(out=V4[:, h, c, 8:16], in_=W[:, 0:128]), [W], [Vall])
                dve(lambda e, h=h, c=c: e.max_index(out=I4[:, h, c, 8:16], in_max=V4[:, h, c, 8:16], in_values=W[:, 0:128]), [W, Vall], [Iall])
        cp(g, "dve", IFall, IFall[:, 0:256], Iall, Iall[:, 0:256])
        tt(g, "dve", CANDb, CANDb[:, 0:2048].rearrange("p (h a b) -> p h a b", h=8, a=16),
           Vall, V4[:, :, 0, :].unsqueeze(3).to_broadcast([128, 8, 16, 16]),
           Vall, V4[:, :, 1, :].unsqueeze(2).to_broadcast([128, 8, 16, 16]), ALU.add)
        for h in range(8):
            dve(lambda e, h=h: e.max(out=T8v[:, h, 0:8], in_=CAND[:, h, :]), [CANDb], [T8a])
            dve(lambda e, h=h: e.max_index(out=P8v[:, h, 0:8], in_max=T8v[:, h, 0:8], in_values=CAND[:, h, :]), [CANDb, T8a], [P8a])
            dve(lambda e, h=h: e.match_replace(out=CAND2[:, h, :], in_to_replace=T8v[:, h, 0:8], in_values=CAND[:, h, :], imm_value=NEG), [CANDb, T8a], [CANDb])
            dve(lambda e, h=h: e.max(out=T8v[:, h, 8:16], in_=CAND2[:, h, :]), [CANDb], [T8a])
            dve(lambda e, h=h: e.max_index(out=P8v[:, h, 8:16], in_max=T8v[:, h, 8:16], in_values=CAND2[:, h, :]), [CANDb, T8a], [P8a])
        cp(g, "dve", T8a, PFv, P8a, P8a[:, 0:128])
        GEb = g.B[3]
        GE = GEb[:, 0:2048].rearrange("p (j m) -> p j m", j=128)
        tt(g, "dve", GEb, GE, T8a, PFv.unsqueeze(2).to_broadcast([128, 128, 16]),
           g.pk_iota, g.pk_thr[:, 0:16].unsqueeze(1).to_broadcast([128, 128, 16]), ALU.is_ge)
        dve(lambda e: e.tensor_reduce(out=AB[:, 0:128], in_=GE, axis=AX.X, op=ALU.add), [GEb], [AB])
        stt(g, AB, AB[:, 128:256], AB, AB[:, 0:128], -16.0, T8a, PFv, ALU.mult, ALU.add)
        for side, dstb in ((0, g.pk_i1f), (1, g.pk_i2f)):
            tt(g, "dve", GEb, GE, AB, AB[:, side * 128:(side + 1) * 128].unsqueeze(2).to_broadcast([128, 128, 16]),
               g.pk_iota, iota16.unsqueeze(1).to_broadcast([128, 128, 16]), ALU.is_equal)
            GE4 = GEb[:, 0:2048].rearrange("p (h k m) -> p h k m", h=8, k=16)
            tt(g, "dve", GEb, GE4, GEb, GE4, IFall, IF4[:, :, side, :].unsqueeze(2).to_broadcast([128, 8, 16, 16]), ALU.mult)
            dve(lambda e, dstb=dstb: e.tensor_reduce(out=dstb[:, :], in_=GE, axis=AX.X, op=ALU.add), [GEb], [dstb])
        tt(g, "dve", GATE, GATE[:, 0:128].rearrange("p (h k) -> p h k", h=8), T8a, T8v,
           T8a, T8v[:, :, 0:1].to_broadcast([128, 8, 16]), ALU.subtract)
        act(g, GATE, GATE[:, 0:128], GATE, GATE[:, 0:128], AF.Exp)
        dve(lambda e: e.tensor_reduce(out=SEL[:, 0:8], in_=GATE[:, 0:128].rearrange("p (h k) -> p h k", h=8), axis=AX.X, op=ALU.add), [GATE], [SEL])
        dve(lambda e: e.reciprocal(out=SEL[:, 8:16], in_=SEL[:, 0:8]), [SEL], [SEL])
        tt(g, "dve", GATE, GATE[:, 0:128].rearrange("p (h k) -> p h k", h=8), GATE, GATE[:, 0:128].rearrange("p (h k) -> p h k", h=8),
           SEL, SEL[:, 8:16].unsqueeze(2).to_broadcast([128, 8, 16]), ALU.mult)
        trb = g.T[4]
        trv = trb[:, 0:384].rearrange("p (a n) -> p a n", a=3)
        pt = rot(g, "trps", [g.psb[0], g.psb[1]])
        tr(g, pt, pt[:, 0:128], g.pk_i1f, g.pk_i1f[:, :])
        tr(g, pt, pt[:, 128:256], g.pk_i2f, g.pk_i2f[:, :])
        tr(g, pt, pt[:, 256:384], GATE, GATE[:, :])
        trv = bf(trb[:])[:, 0:384].rearrange("p (a n) -> p a n", a=3)
        cp(g, "act", trb, trv, pt, pt[:, 0:384].rearrange("p (a n) -> p a n", a=3))
        stA = g.B[0]; stB = g.B[1]
        sA = bf(stA[:]).rearrange("p (c n) -> p c n", c=64)
        sB = bf(stB[:]).rearrange("p (c n) -> p c n", c=64)
        iota128 = bf(g.pk_iotab[:])[:, 0:128]
        NB = 32
        for nb0 in range(0, 128, NB):
            o1 = rot(g, "oh1", [g.T[0], g.T[1]])
            o2 = rot(g, "oh2", [g.T[2], g.T[3]])
            o1v = bf(o1[:]).rearrange("p (n c) -> p n c", n=NB)
            o2v = bf(o2[:]).rearrange("p (n c) -> p n c", n=NB)
            iob = iota128.unsqueeze(1).to_broadcast([128, NB, 128])
            tt(g, "dve", o1, o1v, g.pk_iotab, iob, trb, trv[:, 0, nb0:nb0 + NB].unsqueeze(2).to_broadcast([128, NB, 128]), ALU.is_equal)
            tt(g, "dve", o1, o1v, o1, o1v, trb, trv[:, 2, nb0:nb0 + NB].unsqueeze(2).to_broadcast([128, NB, 128]), ALU.mult)
            tt(g, "dve", o2, o2v, g.pk_iotab, iob, trb, trv[:, 1, nb0:nb0 + NB].unsqueeze(2).to_broadcast([128, NB, 128]), ALU.is_equal)
            for n0 in range(nb0, nb0 + NB, 4):
                pg = rot(g, "linps", [g.psb[2], g.psb[3], g.psb[4]])
                for j in range(4):
                    n = n0 + j
                    mm(g, pg, pg[:, j * 128:(j + 1) * 128], o1, o1v[:, n - nb0, :], o2, o2v[:, n - nb0, :])
                pin = pg[:, 0:512].rearrange("p (n c) -> p n c", n=4)
                cp(g, "act", stA, sA.rearrange("p c n -> p n c")[:, n0:n0 + 4, :], pg, pin[:, :, 0:64])
                cp(g, "act", stB, sB.rearrange("p c n -> p n c")[:, n0:n0 + 4, :], pg, pin[:, :, 64:128])
        for q8 in range(4):
            st_(g, g.Gd, g.Gd[q8 * 16:(q8 + 1) * 16, :, rows(t)].rearrange("c p n -> p c n"), stA, sA[:, q8 * 16:(q8 + 1) * 16, :], q="sp")
            st_(g, g.Gd, g.Gd[64 + q8 * 16:64 + (q8 + 1) * 16, :, rows(t)].rearrange("c p n -> p c n"), stB, sB[:, q8 * 16:(q8 + 1) * 16, :], q="sp")
        if i + 1 < DEPTH and t < 8 and not g.cfg.get("no_ada_overlap"):
            ada_piece(g, i + 1, t)
    Ut = g.peer_u[:, :].rearrange("(l p c) d -> l c p d", l=g.cfg.get("nlay", DEPTH), c=128)
    Vt = g.peer_v[:, :].rearrange("(l p c) d -> l c p d", l=g.cfg.get("nlay", DEPTH), c=128)
    NG = 4
    for (tp0, ntp) in ((0, 4), (4, 8)):
        NTp = ntp * 128
        ngrp = ntp // 4
        ubs = [g.B[6], g.B[7]][:ngrp]
        uvs = []
        for gi, ub in enumerate(ubs):
            uv = bf(ub[:])[:, 0:16 * 512].rearrange("p (k n) -> p k n", k=16)
            ld(g, ub, uv, g.UT, g.UT[:, :, (tp0 + gi * 4) * 128:(tp0 + gi * 4 + 4) * 128])
            uvs.append(uv)
        accs = [(g.B[tt_ // 2], g.B[tt_ // 2][:, (tt_ % 2) * D:(tt_ % 2 + 1) * D]) for tt_ in range(ntp)]
        Vsets = []
        for pb_list in ([g.B[4], g.B[4]], [g.T[4], g.T[5]]):
            vl = []
            for cl in range(NG):
                pbuf = pb_list[cl // 2]
                if pbuf is g.B[4]:
                    sb_ = subs(g, pbuf, NG)[cl]
                    vl.append((sb_, bf(pbuf[:]).rearrange("p (c n) -> p c n", c=NG)[:, cl, :]))
                else:
                    sb_ = subs(g, pbuf, 2)[cl % 2]
                    vl.append((sb_, bf(pbuf[:]).rearrange("p (c n) -> p c n", c=2)[:, cl % 2, :]))
            Vsets.append(vl)
        ATb = g.T[6]
        ATv = bf(ATb[:]).rearrange("p (c n) -> p c n", c=NG)
        ATs = subs(g, ATb, NG)
        GX = g.B[5]
        GXv = bf(GX[:]).rearrange("p (s n) -> p s n", s=8)
        GXs = subs(g, GX, 8)
        u16l = []
        utl = []
        for cl in range(NG):
            ub_ = [g.T[0], g.T[1]][cl // 2]
            u16l.append((subs(g, ub_, 2)[cl % 2], bf(ub_[:]).rearrange("p (s n) -> p s n", s=2)[:, cl % 2, :]))
            tb_ = [g.T[2], g.T[3]][cl // 2]
            utl.append((subs(g, tb_, 2)[cl % 2], bf(tb_[:]).rearrange("p (s k n) -> p s k n", s=2, k=16)[:, cl % 2]))
        for cg in range(128 // NG):
            Vset = Vsets[cg % 2]
            for cl in range(NG):
                c = cg * NG + cl
                ld(g, u16l[cl][0], u16l[cl][1], g.peer_u, Ut[i, c], q="pool")
                ld(g, GXs[cl], GXv[:, cl, 0:NTp], g.Gd, g.Gd[c, :, tp0 * 128:tp0 * 128 + NTp])
            for cl in range(NG):
                c = cg * NG + cl
                ld(g, Vset[cl][0], Vset[cl][1], g.peer_v, Vt[i, c], q="pool")
            for cl in range(NG):
                ub_, uap = u16l[cl]
                tb_, tap = utl[cl]
                for half in range(2):
                    pt = rot(g, "dtr", [g.psb[0], g.psA3[2]])
                    pv = bf(pt[:])[:, 0:1024] if pt is g.psb[0] else bf(pt[:])[:, 2048:3072]
                    for j in range(8):
                        kc = half * 8 + j
                        tr(g, pt, pv[:, j * 128:(j + 1) * 128], ub_, uap[:, kc * 128:(kc + 1) * 128], fp32=False)
                    cp(g, "act", tb_, tap[:, half * 8:half * 8 + 8, :], pt, pv.rearrange("p (k n) -> p k n", k=8))
            for cl in range(NG):
                tb_, tap = utl[cl]
                for gi in range(ngrp):
                    ph = rot(g, "dph", [g.psA3[0], g.psA3[1]])
                    phv = ph[:, 0:512] if ph is g.psA3[0] else ph[:, 512:1024]
                    for kc in range(16):
                        mm(g, ph, phv, tb_, tap[:, kc, :], ubs[gi], uvs[gi][:, kc, :], kc == 0, kc == 15)
                    gsel = 4 + (g.rr.get("dge", 0) % 4); g.rr["dge"] = g.rr.get("dge", 0) + 1
                    ge = GXv[:, gsel, 0:512]
                    act(g, GXs[gsel], ge, ph, phv, AF.Gelu)
                    tt(g, "dve", ATs[cl], ATv[:, cl, gi * 512:(gi + 1) * 512], GXs[gsel], ge, GXs[cl], GXv[:, cl, gi * 512:(gi + 1) * 512], ALU.mult)
            for tt_ in range(ntp):
                ab, aap = accs[tt_]
                for sl in range(4):
                    po = g.psb[1 + sl]
                    for cl in range(NG):
                        mm(g, po, po[:, 0:512], ATs[cl], ATv[:, cl, tt_ * 128:(tt_ + 1) * 128], Vset[cl][0], Vset[cl][1][:, sl * 512:(sl + 1) * 512],
                           cl == 0, cl == NG - 1)
                    if cg == 0:
                        cp(g, "dve", ab, aap[:, sl * 512:(sl + 1) * 512], po, po[:, 0:512])
                    else:
                        tt(g, "dve", ab, aap[:, sl * 512:(sl + 1) * 512], po, po[:, 0:512], ab, aap[:, sl * 512:(sl + 1) * 512], ALU.add)
        r = 0 if tp0 < 4 else 1
        gt = g.T[0]
        ld(g, gt, gt[:], g.MOD, bcast_row(g.MOD[2 * i + r:2 * i + r + 1, 5 * D:6 * D]))
        for tt_ in range(ntp):
            t = tp0 + tt_
            ab, aap = accs[tt_]
            xt = g.T[1]
            ld(g, xt, xt[:], g.X, g.X[rows(t), :])
            tt(g, "dve", ab, aap, ab, aap, gt, gt[:], ALU.mult)
            stt(g, ab, aap, xt, xt[:], ALPHA, ab, aap, ALU.mult, ALU.add)
            st_(g, g.Y, g.Y[rows(t), :], ab, aap, q="sp")


def host_consts():
    c = {}
    c["c_ident"] = np.eye(128, dtype=np.float32)
    cos = np.ones((NTOK, 64), np.float32)
    sin = np.zeros((NTOK, 64), np.float32)
    pos = np.arange(NS_TOK)
    row = (pos // 64).astype(np.float32)
    col = (pos % 64).astype(np.float32)
    inv = (10000.0 ** (-np.arange(16, dtype=np.float32) / 16)).astype(np.float32)
    ar = (row[:, None] * inv[None, :]).astype(np.float32)
    ac = (col[:, None] * inv[None, :]).astype(np.float32)
    cos[NP_TOK:, 0:16] = np.cos(ar); cos[NP_TOK:, 16:32] = np.cos(ar)
    cos[NP_TOK:, 32:48] = np.cos(ac); cos[NP_TOK:, 48:64] = np.cos(ac)
    sin[NP_TOK:, 0:16] = -np.sin(ar); sin[NP_TOK:, 16:32] = np.sin(ar)
    sin[NP_TOK:, 32:48] = -np.sin(ac); sin[NP_TOK:, 48:64] = np.sin(ac)
    c["c_rcos"] = cos
    c["c_rsin"] = sin
    j = np.arange(512, dtype=np.float64)
    a = 2 * np.pi * np.outer(j, j) / 512
    c["c_cc"] = np.cos(a).astype(np.float32)
    c["c_sc"] = np.sin(a).astype(np.float32)
    for T in (256, 1024):
        tt_ = np.arange(T, dtype=np.float64)
        a = 2 * np.pi * np.outer(tt_, tt_) / T
        nrm = 1.0 / math.sqrt(T * 512)
        c[f"c_ct{T}"] = (np.cos(a) * nrm).astype(np.float32)
        c[f"c_st{T}"] = (-np.sin(a) * nrm).astype(np.float32)
    ii = np.arange(128)
    c["c_mprev"] = np.where(ii[None, :] >= ii[:, None], 0.0, NEG).astype(np.float32)
    c["c_mnext"] = np.where(ii[None, :] <= ii[:, None], 0.0, NEG).astype(np.float32)
    c["c_iota"] = np.tile(np.arange(256, dtype=np.float32)[None, :], (128, 1))
    return c


def make_in_maps(inp):
    consts = host_consts()
    f = lambda a: np.ascontiguousarray(np.asarray(a, dtype=np.float32))
    shared = {
        "ada_w": f(inp["ada_w"]), "ada_b": f(inp["ada_b"]),
        "ln1_g": f(inp["ln1_g"]), "ln1_b": f(inp["ln1_b"]), "ln2_g": f(inp["ln2_g"]), "ln2_b": f(inp["ln2_b"]),
        "mla_w_dq": f(inp["mla_w_dq"]), "mla_q_norm": f(inp["mla_q_norm"]).reshape(1, 512),
        "mla_w_uq": f(inp["mla_w_uq"]), "mla_w_dkv": f(inp["mla_w_dkv"]),
        "mla_kv_norm": f(inp["mla_kv_norm"]).reshape(1, 512), "mla_w_uk": f(inp["mla_w_uk"]),
        "mla_w_uv": f(inp["mla_w_uv"]), "mla_w_o": f(inp["mla_w_o"]),
        "gqa_w_qkv": f(inp["gqa_w_qkv"]), "gqa_sink": f(inp["gqa_sink"]).reshape(1, 32),
        "gqa_w_o": f(inp["gqa_w_o"]), "fnet_w_out": f(inp["fnet_w_out"]),
        "conv_w_in": f(inp["conv_w_in"]), "conv_w": f(inp["conv_w"]), "conv_b": f(inp["conv_b"]).reshape(1, D),
        "conv_w_out": f(inp["conv_w_out"]), "peer_w_q": f(inp["peer_w_q"]),
        "peer_sub_keys": f(inp["peer_sub_keys"]).reshape(DEPTH * 16 * 128, 128),
        "peer_u": f(inp["peer_u"]).reshape(DEPTH * 16384, D), "peer_v": f(inp["peer_v"]).reshape(DEPTH * 16384, D),
    }
    shared.update(consts)
    xp = f(inp["x_prompt"]); xs = f(inp["x_sample"])
    maps = []
    for c in range(8):
        b = c // 4
        m = dict(shared)
        m["xin"] = np.concatenate([xp[2 * c].reshape(256, D), xp[2 * c + 1].reshape(256, D), xs[b]], axis=0)
        m["cond"] = np.stack([f(inp["c_ctx"]), f(inp["c"])[b]], axis=0)
        m["l0ckv"] = f(inp["cache_l0_ckv"])[b]; m["l0kr"] = f(inp["cache_l0_krope"])[b]
        m["l1k"] = f(inp["cache_l1_k"])[b].reshape(256, 512); m["l1v"] = f(inp["cache_l1_v"])[b].reshape(256, 512)
        maps.append(m)
    return maps


def kernel(**inp):
    nc = bass.Bass("TRN2", target_bir_lowering=False)
    build(nc)
    maps = make_in_maps(inp)
    res = run_bass_kernel_spmd(nc, maps, core_ids=list(range(8))).results
    yp = np.zeros((16, 256, D), np.float32); ys = np.zeros((2, 1024, D), np.float32)
    ckv = np.zeros((16, 256, 512), np.float32); kr = np.zeros((16, 256, 64), np.float32)
    kk = np.zeros((16, 256, 8, 64), np.float32); vv = np.zeros((16, 256, 8, 64), np.float32)
    for c in range(8):
        r = res[c]
        b, qd = c // 4, c % 4
        yp[2 * c] = r["y"][0:256]; yp[2 * c + 1] = r["y"][256:512]
        ys[b, qd * 256:(qd + 1) * 256] = r["y"][512 + qd * 256:512 + (qd + 1) * 256]
        ckv[2 * c] = r["o_ckv"][0:256]; ckv[2 * c + 1] = r["o_ckv"][256:512]
        kr[2 * c] = r["o_kr"][0:256]; kr[2 * c + 1] = r["o_kr"][256:512]
        kk[2 * c] = r["o_k"][0:256].reshape(256, 8, 64); kk[2 * c + 1] = r["o_k"][256:512].reshape(256, 8, 64)
        vv[2 * c] = r["o_v"][0:256].reshape(256, 8, 64); vv[2 * c + 1] = r["o_v"][256:512].reshape(256, 8, 64)
    return (yp, ys, ckv, kr, kk, vv)
```

```python
import contextlib
import math
import numpy as np
import ml_dtypes
import concourse.bass as bass
import concourse.mybir as mybir
from concourse.bass_utils import run_bass_kernel_spmd

F32 = mybir.dt.float32
BF16 = mybir.dt.bfloat16
U32 = mybir.dt.uint32
I32 = mybir.dt.int32
AF = mybir.ActivationFunctionType
ALU = mybir.AluOpType
AX = mybir.AxisListType

D = 2048
DEPTH = 4
NP_TOK = 512
NS_TOK = 1024
NTOK = NP_TOK + NS_TOK
NT = NTOK // 128
ALPHA = (2 * DEPTH) ** 0.25
LN_EPS = 1e-5
RMS_EPS = 1e-6
NEG = -1e30


class Buf:
    def __init__(self, name, t=None):
        self.name = name
        self.t = t
        self.w = None
        self.r = {}
        self.alias = []

    def __getitem__(self, idx):
        return self.t[idx]


class Eng:
    def __init__(self, name, handle):
        self.name = name
        self.h = handle
        self.sems = []
        self.cur = 0
        self.count = 0
        self.known = {}
        self.prog = []
        self.dslots = []
        self.duse = []
        self.dnext = 0


class K:
    ROT = 20000

    def __init__(self, nc, stack):
        self.nc = nc
        self.stack = stack
        self.E = {
            "pe": Eng("pe", nc.tensor),
            "act": Eng("act", nc.scalar),
            "dve": Eng("dve", nc.vector),
            "pool": Eng("pool", nc.gpsimd),
            "sp": Eng("sp", nc.sync),
        }
        self.nsem = 0
        for e in self.E.values():
            e.sems.append(self.sem(e.name))
        for qn, n in (("sp", 24), ("pool", 24), ("act", 8)):
            e = self.E[qn]
            e.dslots = [self.sem(f"d{qn}{i}") for i in range(n)]
            e.duse = [0] * n
        self.nbuf = 0

    def sem(self, name):
        self.nsem += 1
        return self.stack.enter_context(self.nc.semaphore(f"s{self.nsem}_{name}"))

    def sb(self, name, shape, dt=F32):
        self.nbuf += 1
        t = self.stack.enter_context(self.nc.sbuf_tensor(f"{name}_{self.nbuf}", list(shape), dt))
        return Buf(name, t)

    def ps(self, name, shape, dt=F32):
        self.nbuf += 1
        t = self.stack.enter_context(self.nc.psum_tensor(f"{name}_{self.nbuf}", list(shape), dt))
        return Buf(name, t)

    def dram(self, name, shape, dt=F32, kind="Internal"):
        t = self.nc.dram_tensor(name, list(shape), dt, kind=kind).ap()
        return Buf(name, t)

    def _waits(self, e, reads, writes, skip_self=False):
        need = {}
        def add(ev):
            if ev is None:
                return
            sem, val = ev
            if skip_self and any(sem is s for s in e.sems):
                return
            k = id(sem)
            if e.known.get(k, 0) >= val:
                return
            if k not in need or need[k][1] < val:
                need[k] = (sem, val)
        for b in reads:
            add(b.w)
        for b in writes:
            add(b.w)
            for ev in b.r.values():
                add(ev)
        out = list(need.values())
        for sem, val in out:
            e.known[id(sem)] = val
        return out

    def _commit(self, ev, reads, writes):
        for b in reads:
            b.r[id(ev[0])] = ev
        for b in writes:
            b.w = ev
            b.r = {}

    @staticmethod
    def _expand(bufs):
        out = []
        for b in bufs:
            out.append(b)
            out.extend(b.alias)
        return out

    def op(self, en, fn, reads=(), writes=()):
        reads = self._expand(reads); writes = self._expand(writes)
        e = self.E[en]
        waits = self._waits(e, reads, writes, skip_self=(en == "pe"))
        if e.count >= self.ROT:
            e.sems.append(self.sem(e.name))
            e.cur += 1
            e.count = 0
        e.count += 1
        sem = e.sems[e.cur]
        ev = (sem, e.count)
        e.prog.append((waits, fn, sem, 1))
        if en == "pe":
            e.known[id(sem)] = e.count
        self._commit(ev, reads, writes)
        return ev

    def dma(self, qn, fn, reads=(), writes=()):
        reads = self._expand(reads); writes = self._expand(writes)
        e = self.E[qn]
        waits = self._waits(e, reads, writes)
        slot = e.dnext
        e.dnext = (e.dnext + 1) % len(e.dslots)
        sem = e.dslots[slot]
        prev = e.duse[slot] * 16
        if prev and e.known.get(id(sem), 0) < prev:
            waits.append((sem, prev))
            e.known[id(sem)] = prev
        e.duse[slot] += 1
        ev = (sem, e.duse[slot] * 16)
        e.prog.append((waits, fn, sem, 16))
        self._commit(ev, reads, writes)
        return ev

    def finish(self, final_bufs):
        e = self.E["sp"]
        fw = []
        seen = {}
        for b in final_bufs:
            for ev in ([b.w] if b.w else []) + list(b.r.values()):
                k = id(ev[0])
                if k not in seen or seen[k][1] < ev[1]:
                    seen[k] = ev
        fw = list(seen.values())
        with self.nc.Block() as block:
            def emit(en):
                eng = self.E[en]
                def body(h):
                    for waits, fn, sem, inc in eng.prog:
                        for s, v in waits:
                            h.wait_ge(s, v)
                        fn(h).then_inc(sem, inc)
                    if en == "sp":
                        for s, v in fw:
                            h.wait_ge(s, v)
                return body
            block.tensor(emit("pe"))
            block.scalar(emit("act"))
            block.vector(emit("dve"))
            block.gpsimd(emit("pool"))
            block.sync(emit("sp"))


SEQS = [(0, 2, "p"), (2, 2, "p"), (4, 8, "s")]


class P:
    pass


def build(nc, cfg=None):
    cfg = cfg or {}
    st = contextlib.ExitStack()
    k = K(nc, st)
    g = P()
    g.k = k
    g.cfg = cfg
    ein = lambda n, s: k.dram(n, s, F32, "ExternalInput")
    eout = lambda n, s: k.dram(n, s, F32, "ExternalOutput")
    g.xin = ein("xin", [NTOK, D])
    g.cond = ein("cond", [2, D])
    g.l0ckv = ein("l0ckv", [256, 512]); g.l0kr = ein("l0kr", [256, 64])
    g.l1k = ein("l1k", [256, 512]); g.l1v = ein("l1v", [256, 512])
    g.ada_w = ein("ada_w", [DEPTH, D, 6 * D]); g.ada_b = ein("ada_b", [DEPTH, 6 * D])
    g.ln_g = [ein("ln1_g", [DEPTH, D]), ein("ln2_g", [DEPTH, D])]
    g.ln_b = [ein("ln1_b", [DEPTH, D]), ein("ln2_b", [DEPTH, D])]
    g.mla_w_dq = ein("mla_w_dq", [D, 512]); g.mla_q_norm = ein("mla_q_norm", [1, 512])
    g.mla_w_uq = ein("mla_w_uq", [512, 3072]); g.mla_w_dkv = ein("mla_w_dkv", [D, 576])
    g.mla_kv_norm = ein("mla_kv_norm", [1, 512]); g.mla_w_uk = ein("mla_w_uk", [512, D])
    g.mla_w_uv = ein("mla_w_uv", [512, D]); g.mla_w_o = ein("mla_w_o", [D, D])
    g.gqa_w_qkv = ein("gqa_w_qkv", [D, 3072]); g.gqa_sink = ein("gqa_sink", [1, 32])
    g.gqa_w_o = ein("gqa_w_o", [D, D]); g.fnet_w_out = ein("fnet_w_out", [D, D])
    g.conv_w_in = ein("conv_w_in", [D, 3 * D]); g.conv_w = ein("conv_w", [3, D])
    g.conv_b = ein("conv_b", [1, D]); g.conv_w_out = ein("conv_w_out", [D, D])
    g.peer_w_q = ein("peer_w_q", [DEPTH, D, D]); g.peer_keys = ein("peer_sub_keys", [DEPTH * 16 * 128, 128])
    g.peer_u = ein("peer_u", [cfg.get("nexp", DEPTH * 16384), D]); g.peer_v = ein("peer_v", [cfg.get("nexp", DEPTH * 16384), D])
    g.c_ident = ein("c_ident", [128, 128])
    g.c_rcos = ein("c_rcos", [NTOK, 64]); g.c_rsin = ein("c_rsin", [NTOK, 64])
    g.c_cc = ein("c_cc", [512, 512]); g.c_sc = ein("c_sc", [512, 512])
    g.c_ct = {256: ein("c_ct256", [256, 256]), 1024: ein("c_ct1024", [1024, 1024])}
    g.c_st = {256: ein("c_st256", [256, 256]), 1024: ein("c_st1024", [1024, 1024])}
    g.c_mprev = ein("c_mprev", [128, 128]); g.c_mnext = ein("c_mnext", [128, 128])
    g.c_iota = ein("c_iota", [128, 256])
    g.y = eout("y", [NTOK, D])
    g.o_ckv = eout("o_ckv", [NP_TOK, 512]); g.o_kr = eout("o_kr", [NP_TOK, 64])
    g.o_k = eout("o_k", [NP_TOK, 512]); g.o_v = eout("o_v", [NP_TOK, 512])
    g.dbg = {}
    for name, shape in cfg.get("dbg", {}).items():
        g.dbg[name] = eout("dbg_" + name, shape)
    g.X = k.dram("X", [NTOK, D]); g.Y = k.dram("Y", [NTOK, D]); g.U2 = k.dram("U2", [NTOK, D])
    g.UT = k.dram("UT", [128, 16, NTOK], BF16)
    g.MOD = k.dram("MOD", [DEPTH * 2, 6 * D])
    g.SC = k.dram("SC", [NTOK, D])
    g.SC2 = k.dram("SC2", [NTOK, 3 * D])
    g.FT = k.dram("FT", [128, 16, NTOK + 256], BF16)
    g.FT2 = k.dram("FT2", [128, 16, NTOK + 256], BF16)
    g.Zp = k.dram("Zp", [NTOK + 4, D])
    g.VB = k.dram("VB", [NTOK + 256, 512], BF16)
    g.B = [k.sb(f"B{i}", [128, 4096]) for i in range(8)]
    g.T = [k.sb(f"T{i}", [128, 2048]) for i in range(7)]
    g.identf = k.sb("identf", [128, 128]); g.identb = k.sb("identb", [128, 128], BF16)
    g.csT = k.sb("csT", [128, 16, 2])
    g.modT = k.sb("modT", [128, 2, 4, 16])
    g.sm = [k.sb(f"sm{i}", [128, 256]) for i in range(8)]
    g.psA = k.ps("psA", [128, 1536])
    g.psb = [k.ps(f"ps{i}", [128, 512]) for i in range(5)]
    g.psA3 = [Buf(f"psA{j}", g.psA.t) for j in range(3)]
    for b_ in g.psA3:
        b_.alias = [g.psA]
    g.psA.alias = list(g.psA3)
    g.Gd = k.dram("Gd", [128, 128, NTOK], BF16)
    g.rr = {}
    g.pk_iall = k.sb("pk_iall", [128, 256], U32); g.pk_p8all = k.sb("pk_p8all", [128, 128], U32)
    g.pk_i1f = k.sb("pk_i1f", [128, 128]); g.pk_i2f = k.sb("pk_i2f", [128, 128])
    g.pk_idx = k.sb("pk_idx", [128, 8], U32); g.pk_gate = k.sb("pk_gate", [128, 128]); g.pk_h = k.sb("pk_h", [128, 256])
    g.pk_v = k.sb("pk_v", [128, 32]); g.pk_i = k.sb("pk_i", [128, 32], U32); g.pk_if = k.sb("pk_if", [128, 32])
    g.pk_w = k.sb("pk_w", [128, 128]); g.pk_ca = k.sb("pk_ca", [128, 256]); g.pk_cb = k.sb("pk_cb", [128, 256])
    g.pk_t8 = k.sb("pk_t8", [128, 48]); g.pk_p8 = k.sb("pk_p8", [128, 32], U32); g.pk_pf = k.sb("pk_pf", [128, 48])
    g.pk_oh = k.sb("pk_oh", [128, 256]); g.pk_sel = k.sb("pk_sel", [128, 48]); g.pk_iota = k.sb("pk_iota", [128, 256])
    prologue(g)
    first = cfg.get("first_layer", 0)
    last = cfg.get("last_layer", DEPTH - 1)
    for i in range(first, last + 1):
        layer(g, i)
    epilogue(g)
    finals = [g.y, g.o_ckv, g.o_kr, g.o_k, g.o_v] + list(g.dbg.values())
    k.finish(finals)
    st.close()
    return nc


def subs(g, parent, n):
    key = ("subs", id(parent), n)
    if key not in g.rr:
        lst = [Buf(f"{parent.name}_s{j}", parent.t) for j in range(n)]
        for b_ in lst:
            b_.alias = [parent]
        parent.alias = list(parent.alias) + lst
        g.rr[key] = lst
    return g.rr[key]


def rot(g, name, bufs):
    i = g.rr.get(name, 0)
    g.rr[name] = i + 1
    return bufs[i % len(bufs)]


def bf(ap):
    return ap.bitcast(BF16)


def ld(g, dst_buf, dst_ap, src_buf, src_ap, q="sp", **kw):
    g.k.dma(q, lambda h: h.dma_start(out=dst_ap, in_=src_ap, **kw), reads=[src_buf], writes=[dst_buf])


def st_(g, dst_buf, dst_ap, src_buf, src_ap, q="pool", **kw):
    g.k.dma(q, lambda h: h.dma_start(out=dst_ap, in_=src_ap, **kw), reads=[src_buf], writes=[dst_buf])


def prologue(g):
    k = g.k
    ld(g, g.identf, g.identf[:], g.c_ident, g.c_ident[:, :])
    k.op("dve", lambda h: h.tensor_copy(out=g.identb[:], in_=g.identf[:]), reads=[g.identf], writes=[g.identb])
    ld(g, g.pk_iota, g.pk_iota[:], g.c_iota, g.c_iota[:, :])
    g.pk_iotab = g.k.sb("pk_iotab", [128, 64])
    cp(g, "dve", g.pk_iotab, bf(g.pk_iotab[:])[:, 0:128], g.pk_iota, g.pk_iota[:, 0:128])
    g.pk_thr = g.pk_iota[:, 128:144]
    tsc(g, "dve", g.pk_iota, g.pk_thr, g.pk_iota, g.pk_iota[:, 0:16], 1.0, 16.0, ALU.add, ALU.mult)
    for r in range(2):
        ld(g, g.csT, g.csT[:, :, r], g.cond, g.cond[r, :].rearrange("(k p) -> p k", p=128),
           allow_slow_non_contiguous=True)
    k.op("act", lambda h: h.activation(out=g.csT[:], in_=g.csT[:], func=AF.Silu), reads=[g.csT], writes=[g.csT])


def epilogue(g):
    pass


def mm(g, pb, out, lb, lhsT, rb, rhs, start=True, stop=True):
    g.k.op("pe", lambda h: h.matmul(out, lhsT=lhsT, rhs=rhs, start=start, stop=stop), reads=[lb, rb], writes=[pb])


def tr(g, pb, out, sb, in_, fp32=True):
    ib = g.identf if fp32 else g.identb
    np_ = in_.shape[0]
    ident = ib[0:np_, 0:np_]
    g.k.op("pe", lambda h: h.transpose(out, in_, ident), reads=[sb, ib], writes=[pb])


def act(g, ob, out, ib, in_, func, bias=None, scale=None, accum=None, rd=(), wr=(), eng="act"):
    kw = {}
    if bias is not None:
        kw["bias"] = bias
    if scale is not None:
        kw["scale"] = scale
    if accum is not None:
        kw["accum_out"] = accum
    g.k.op("act", lambda h: h.activation(out=out, in_=in_, func=func, **kw), reads=[ib, *rd], writes=[ob, *wr])


def tsc(g, eng, ob, out, ib, in0, s1, s2=None, op0=ALU.mult, op1=None, rd=(), wr=(), accum=None):
    kw = {}
    if op1 is not None:
        kw["op1"] = op1
    if accum is not None:
        kw["accum_out"] = accum
    g.k.op(eng, lambda h: h.tensor_scalar(out=out, in0=in0, scalar1=s1, scalar2=s2, op0=op0, **kw),
           reads=[ib, *rd], writes=[ob, *wr])


def tt(g, eng, ob, out, ab, a, bb, b, op):
    g.k.op(eng, lambda h: h.tensor_tensor(out=out, in0=a, in1=b, op=op), reads=[ab, bb], writes=[ob])


def stt(g, ob, out, ab, a, scalar, bb, b, op0, op1, rd=(), accum=None, wr=()):
    kw = {}
    if accum is not None:
        kw["accum_out"] = accum
    g.k.op("dve", lambda h: h.scalar_tensor_tensor(out=out, in0=a, scalar=scalar, in1=b, op0=op0, op1=op1, **kw),
           reads=[ab, bb, *rd], writes=[ob, *wr])


def cp(g, eng, ob, out, ib, in_):
    if eng == "act":
        g.k.op("act", lambda h: h.copy(out=out, in_=in_), reads=[ib], writes=[ob])
    else:
        g.k.op(eng, lambda h: h.tensor_copy(out=out, in_=in_), reads=[ib], writes=[ob])


def rows(t, n=1):
    return slice(t * 128, (t + n) * 128)


def bcast_row(ap_row):
    return ap_row.partition_broadcast(128).rearrange("p o n -> p (o n)")


def ada(g, i):
    k = g.k
    for qt in range(4):
        c0 = qt * 3072
        banks = [(g.psb[s], g.psb[s][0:2, 0:512]) for s in range(5)] + [(g.psA, g.psA[0:2, 0:512])]
        for kc in range(16):
            wb = rot(g, "adaw", [g.B[0], g.B[1], g.B[2]])
            ld(g, wb, wb[:, 0:3072], g.ada_w, g.ada_w[i, kc * 128:(kc + 1) * 128, c0:c0 + 3072])
            for s in range(6):
                pb, out = banks[s]
                mm(g, pb, out, g.csT, g.csT[:, kc, :], wb, wb[:, s * 512:(s + 1) * 512], kc == 0, kc == 15)
        bt = g.B[3]
        for r in range(2):
            ld(g, bt, bt[r:r + 1, 0:3072], g.ada_b, g.ada_b[i:i + 1, c0:c0 + 3072])
        for s in range(6):
            pb, out = banks[s]
            tt(g, "dve", bt, bt[0:2, s * 512:(s + 1) * 512], pb, out, bt, bt[0:2, s * 512:(s + 1) * 512], ALU.add)
        st_(g, g.MOD, g.MOD[2 * i:2 * i + 2, c0:c0 + 3072], bt, bt[0:2, 0:3072])


def load_modT(g, i, which):
    for r in range(2):
        for slot, j in ((0, 3 * which), (1, 3 * which + 1)):
            ld(g, g.modT, g.modT[:, r, slot, :],
               g.MOD, g.MOD[2 * i + r, j * D:(j + 1) * D].rearrange("(k p) -> p k", p=128),
               allow_slow_non_contiguous=True)
    tsc(g, "dve", g.modT, g.modT[:, :, 1, :], g.modT, g.modT[:, :, 1, :], 1.0, None, ALU.add)


def load_bc(g, i, which):
    gj = 3 * which + 2
    for r in range(2):
        ld(g, g.B[0], g.B[0][:, r * D:(r + 1) * D], g.MOD, bcast_row(g.MOD[2 * i + r:2 * i + r + 1, gj * D:(gj + 1) * D]))
    ld(g, g.B[1], g.B[1][:, 0:D], g.ln_g[which], bcast_row(g.ln_g[which][i:i + 1, :]))
    ld(g, g.B[1], g.B[1][:, D:2 * D], g.ln_b[which], bcast_row(g.ln_b[which][i:i + 1, :]))


def emit_ut(g, xb, t, r, dst=None, mod=True):
    dst = dst or g.UT
    stg = rot(g, "utst", [g.T[6], g.T[5]])
    sv = bf(stg[:]).rearrange("p (k n) -> p k n", k=32)
    for q4 in range(4):
        pb = rot(g, "trps", [g.psb[0], g.psb[1]])
        for j in range(4):
            kc = q4 * 4 + j
            tr(g, pb, pb[:, j * 128:(j + 1) * 128], xb, xb[:, kc * 128:(kc + 1) * 128])
        if mod:
            for j in range(4):
                kc = q4 * 4 + j
                act(g, stg, sv[:, kc, :], pb, pb[:, j * 128:(j + 1) * 128], AF.Identity,
                    bias=g.modT[:, r, 0, kc:kc + 1], scale=g.modT[:, r, 1, kc:kc + 1], rd=[g.modT])
        else:
            cp(g, "act", stg, sv[:, q4 * 4:q4 * 4 + 4, :], pb, pb[:, 0:512].rearrange("p (c n) -> p c n", c=4))
    st_(g, dst, dst[:, 0:16, rows(t)], stg, sv[:, 0:16, :], q="act")


def load_w(g, W_buf, W_ap, nK, c0, w):
    wb = rot(g, "wbf", [g.B[4], g.B[5]])
    wv = bf(wb[:])[:, 0:nK * w].rearrange("p (k n) -> p k n", k=nK)
    kstep = 4
    for k0 in range(0, nK, kstep):
        kn = min(kstep, nK - k0)
        ld(g, wb, wv[:, k0:k0 + kn, :], W_buf, W_ap[k0 * 128:(k0 + kn) * 128, c0:c0 + w].rearrange("(k p) n -> p k n", p=128), q="pool")
    return wb, wv


def ut_lhs(g, src, nK=16, bufs=None, kc0=0):
    state = {}
    bufs = bufs or [g.B[6], g.B[7]]

    def fn(t):
        grp = t // 4
        if state.get("grp") != grp:
            ub = rot(g, "ug", bufs)
            uv = bf(ub[:])[:, 0:nK * 512].rearrange("p (k n) -> p k n", k=nK)
            ld(g, ub, uv, src, src[:, kc0:kc0 + nK, grp * 512:(grp + 1) * 512])
            state.update(grp=grp, ub=ub, uv=uv)
        uv = state["uv"]
        j = t % 4
        return state["ub"], (lambda kc: uv[:, kc, j * 128:(j + 1) * 128])
    return fn


def linear_tm(g, lhs_fn, nK, W_buf, W_ap, slices, tiles, consume):
    for si, (c0, w) in enumerate(slices):
        wb, wv = load_w(g, W_buf, W_ap, nK, c0, w)
        for t in tiles:
            lb, lfn = lhs_fn(t)
            pb = rot(g, "linps", [g.psb[2], g.psb[3], g.psb[4]])
            for kc in range(nK):
                mm(g, pb, pb[:, 0:w], lb, lfn(kc), wb, wv[:, kc, :], kc == 0, kc == nK - 1)
            consume(t, si, c0, w, pb)


def pn_consume(g):
    def consume(t, si, c0, w, pb):
        r = 0 if t < 4 else 1
        for h0 in range(0, w, 256):
            hw = min(256, w - h0)
            xs = rot(g, "pnx", [g.sm[0], g.sm[1], g.sm[2]])
            ys = rot(g, "pny", [g.sm[3], g.sm[4], g.sm[5]])
            ld(g, xs, xs[:, 0:hw], g.X, g.X[rows(t), c0 + h0:c0 + h0 + hw])
            tt(g, "dve", ys, ys[:, 0:hw], pb, pb[:, h0:h0 + hw], g.B[0], g.B[0][:, r * D + c0 + h0:r * D + c0 + h0 + hw], ALU.mult)
            stt(g, ys, ys[:, 0:hw], xs, xs[:, 0:hw], ALPHA, ys, ys[:, 0:hw], ALU.mult, ALU.add)
            st_(g, g.Y, g.Y[rows(t), c0 + h0:c0 + h0 + hw], ys, ys[:, 0:hw])
    return consume


def lnt(g, i, which):
    last = (i == DEPTH - 1 and which == 1) or (g.cfg.get("last_layer", DEPTH - 1) == i and which == 1)
    for t in range(NT):
        r = 0 if t < 4 else 1
        yt = rot(g, "lny", [g.T[0], g.T[1]])
        ld(g, yt, yt[:], g.Y, g.Y[rows(t), :])
        s6 = rot(g, "lns", [g.sm[6], g.sm[7]])
        for c in range(4):
            g.k.op("dve", lambda h, c=c, s6=s6, yt=yt: h.bn_stats(out=s6[:, c * 6:(c + 1) * 6], in_=yt[:, c * 512:(c + 1) * 512]),
                   reads=[yt], writes=[s6])
        mv = s6[:, 32:34]
        g.k.op("dve", lambda h, s6=s6, mv=mv: h.bn_aggr(out=mv, in_=s6[:, 0:24]), reads=[s6], writes=[s6])
        rs = s6[:, 40:41]
        tsc(g, "dve", s6, rs, s6, s6[:, 33:34], LN_EPS, None, ALU.add)
        act(g, s6, rs, s6, rs, AF.Sqrt)
        g.k.op("dve", lambda h, rs=rs: h.reciprocal(out=rs, in_=rs), reads=[s6], writes=[s6])
        tsc(g, "dve", yt, yt[:], yt, yt[:], s6[:, 32:33], rs, ALU.subtract, ALU.mult, rd=[s6])
        tt(g, "pool", yt, yt[:], yt, yt[:], g.B[1], g.B[1][:, 0:D], ALU.mult)
        tt(g, "pool", yt, yt[:], yt, yt[:], g.B[1], g.B[1][:, D:2 * D], ALU.add)
        if last:
            st_(g, g.y, g.y[rows(t), :], yt, yt[:])
            continue
        st_(g, g.X, g.X[rows(t), :], yt, yt[:])
        emit_ut(g, yt, t, r)


def layer(g, i):
    cfg = g.cfg
    if i == cfg.get("first_layer", 0):
        ada(g, i)
        load_modT(g, i, 0)
        for t in range(NT):
            r = 0 if t < 4 else 1
            xt = rot(g, "lny", [g.T[0], g.T[1]])
            ld(g, xt, xt[:], g.xin, g.xin[rows(t), :])
            st_(g, g.X, g.X[rows(t), :], xt, xt[:])
            emit_ut(g, xt, t, r)
    g.pre_pn = lambda: load_bc(g, i, 0)
    mixer = cfg.get("mixer", {}).get(i, i % 4)
    if mixer == 0:
        mixer_mla(g)
    elif mixer == 1:
        mixer_gqa(g)
    elif mixer == 2:
        mixer_fnet(g)
    elif mixer == 3:
        mixer_conv(g)
    else:
        g.pre_pn()
        linear_tm(g, ut_lhs(g, g.UT), 16, g.conv_w_out, g.conv_w_out[:, :], [(c, 512) for c in range(0, D, 512)],
                  range(NT), pn_consume(g))
    load_modT(g, i, 1)
    lnt(g, i, 0)
    if cfg.get("stop_after_mixer"):
        return
    if i + 1 < DEPTH:
        ada(g, i + 1)
    peer(g, i)
    load_bc(g, i, 1)
    if i + 1 < DEPTH:
        load_modT(g, i + 1, 0)
    lnt(g, i, 1)


def load_rope_tabs(g):
    tab = g.T[3]
    tv = tab[:, 0:NT * 128].rearrange("p (t c) -> p t c", t=NT)
    ld(g, tab, tv[:, :, 0:64], g.c_rcos, g.c_rcos[:, :].rearrange("(t p) c -> p t c", p=128))
    ld(g, tab, tv[:, :, 64:128], g.c_rsin, g.c_rsin[:, :].rearrange("(t p) c -> p t c", p=128))
    return tab, tv


def rope(g, xb, x3, H, t, tmp_a, tmp_b):
    tab = g.T[3]
    tv = tab[:, 0:NT * 128].rearrange("p (t c) -> p t c", t=NT)
    a3 = tmp_a[:, 0:H * 64].rearrange("p (h c) -> p h c", h=H)
    b3 = tmp_b[:, 0:H * 64].rearrange("p (h c) -> p h c", h=H)
    cosb = tv[:, t, 0:64].unsqueeze(1).to_broadcast([128, H, 64])
    tt(g, "dve", tmp_a, a3, xb, x3, tab, cosb, ALU.mult)
    for qa, qb in ((0, 1), (1, 0), (2, 3), (3, 2)):
        sinb = tv[:, t, 64 + qa * 16:64 + (qa + 1) * 16].unsqueeze(1).to_broadcast([128, H, 16])
        tt(g, "pool" if H > 4 else "dve", tmp_b, b3[:, :, qa * 16:(qa + 1) * 16], xb, x3[:, :, qb * 16:(qb + 1) * 16], tab, sinb, ALU.mult)
    tt(g, "dve", xb, x3, tmp_a, a3, tmp_b, b3, ALU.add)


def tm2ft(g, xb, xap, w, dstb, dst_ap, fp32=True):
    pb = rot(g, "trps", [g.psb[0], g.psb[1]])
    if fp32:
        tr(g, pb, pb[0:w, 0:128], xb, xap, fp32=True)
        cp(g, "act", dstb, dst_ap, pb, pb[0:w, 0:128])
    else:
        pv = bf(pb[:])
        tr(g, pb, pv[0:w, 0:128], xb, xap, fp32=False)
        cp(g, "act", dstb, dst_ap, pb, pv[0:w, 0:128])


def rms_consume(g, norm_ap, dst, ch0, hook=None):
    def consume(t, si, c0, w, pb):
        st6 = rot(g, "lns", [g.sm[6], g.sm[7]])
        junk = rot(g, "rmsj", [g.T[4], g.T[5]])
        act(g, junk, junk[:, 0:512], pb, pb[:, 0:512], AF.Square, accum=st6[:, 0:1], wr=[st6])
        tsc(g, "dve", st6, st6[:, 1:2], st6, st6[:, 0:1], 1.0 / 512.0, RMS_EPS, ALU.mult, ALU.add)
        act(g, st6, st6[:, 1:2], st6, st6[:, 1:2], AF.Sqrt)
        g.k.op("dve", lambda h: h.reciprocal(out=st6[:, 2:3], in_=st6[:, 1:2]), reads=[st6], writes=[st6])
        stt(g, junk, junk[:, 512:1024], pb, pb[:, 0:512], st6[:, 2:3], g.T[2], norm_ap, ALU.mult, ALU.mult, rd=[st6])
        if hook:
            hook(t, junk, junk[:, 512:1024])
        stg = rot(g, "rmst", [g.sm[3], g.sm[4]])
        sv = bf(stg[:]).rearrange("p (k n) -> p k n", k=4)
        pt = rot(g, "trps", [g.psb[0], g.psb[1]])
        for j in range(4):
            tr(g, pt, pt[:, j * 128:(j + 1) * 128], junk, junk[:, 512 + j * 128:512 + (j + 1) * 128])
        cp(g, "act", stg, sv, pt, pt[:, 0:512].rearrange("p (k n) -> p k n", k=4))
        st_(g, dst, dst[:, ch0:ch0 + 4, rows(t)], stg, sv, q="act")
    return consume


def attention(g, S, qk_parts, v_fn, nblk, out_fn, scale, bias_fn=None, sink_ap=None, ranges=None, vblocks=None):
    k = g.k
    if S <= 512:
        ps = rot(g, "attps", g.psA3)
        pbase = g.psA3.index(ps) * 512
    else:
        ps = g.psA
        pbase = 0
    pst = g.psA.t
    ranges = ranges or [(0, S)]
    vblocks = list(vblocks) if vblocks is not None else list(range(nblk))
    col = 0
    pieces = []
    for (k0, kw) in ranges:
        while kw > 0:
            room = 512 - (col % 512)
            w_ = min(kw, room)
            pieces.append((col, k0, w_))
            col += w_; k0 += w_; kw -= w_
    assert col == S and len(vblocks) == nblk
    for (c0, k0, cw) in pieces:
        for pi, (lb, lhsT, rb, rfn) in enumerate(qk_parts):
            mm(g, ps, pst[:, pbase + c0:pbase + c0 + cw], lb, lhsT, rb, rfn(k0, cw), pi == 0, pi == len(qk_parts) - 1)
    st6 = rot(g, "lns", [g.sm[6], g.sm[7]])
    aset = g.rr.get("attset", 0) % 2
    g.rr["attset"] = g.rr.get("attset", 0) + 1
    pf = [g.T[4], g.B[4]][aset]
    if bias_fn is not None:
        src_b, src = pf, pf[:, 0:S]
        bias_fn(ps, pst[:, pbase:pbase + S], pf, pf[:, 0:S])
    else:
        src_b, src = ps, pst[:, pbase:pbase + S]
    k.op("dve", lambda h: h.reduce_max(out=st6[:, 0:1], in_=src, axis=AX.X), reads=[src_b], writes=[st6])
    if sink_ap is not None:
        tsc(g, "dve", st6, st6[:, 0:1], st6, st6[:, 0:1], scale, sink_ap[1], ALU.mult, ALU.max, rd=[sink_ap[0]])
        tsc(g, "dve", st6, st6[:, 1:2], st6, st6[:, 0:1], -1.0, None, ALU.mult)
    else:
        tsc(g, "dve", st6, st6[:, 1:2], st6, st6[:, 0:1], -scale, None, ALU.mult)
    act(g, pf, pf[:, 0:S], src_b, src, AF.Exp, bias=st6[:, 1:2], scale=scale, accum=st6[:, 2:3], rd=[st6], wr=[st6])
    if sink_ap is not None:
        act(g, st6, st6[:, 3:4], st6, st6[:, 1:2], AF.Exp, bias=sink_ap[1], scale=1.0, rd=[sink_ap[0]])
        tt(g, "dve", st6, st6[:, 2:3], st6, st6[:, 2:3], st6, st6[:, 3:4], ALU.add)
    k.op("dve", lambda h: h.reciprocal(out=st6[:, 4:5], in_=st6[:, 2:3]), reads=[st6], writes=[st6])
    pn = [g.T[5], g.B[5]][aset]
    pnv = bf(pn[:])
    tsc(g, "dve", pn, pnv[:, 0:S], pf, pf[:, 0:S], st6[:, 4:5], None, ALU.mult, rd=[st6])
    pT = [g.T[6], g.B[6]][aset]
    pTv = bf(pT[:])[:, 0:nblk * 128].rearrange("p (b n) -> p b n", b=nblk)
    for b0 in range(0, nblk, 8):
        bn = min(8, nblk - b0)
        pb = rot(g, "trps", [g.psb[0], g.psb[1]])
        pv = bf(pb[:])
        for j in range(bn):
            tr(g, pb, pv[:, j * 128:(j + 1) * 128], pn, pnv[:, (b0 + j) * 128:(b0 + j + 1) * 128], fp32=False)
        cp(g, "act", pT, pTv[:, b0:b0 + bn, :], pb, pv[:, 0:bn * 128].rearrange("p (b n) -> p b n", b=bn))
    po = rot(g, "linps", [g.psb[2], g.psb[3], g.psb[4]])
    dv = None
    for bi, blk in enumerate(vblocks):
        vb, vap = v_fn(blk)
        dv = vap.shape[-1]
        mm(g, po, po[0:dv, 0:128], vb, vap, pT, pTv[:, bi, :], bi == 0, bi == nblk - 1)
    out_fn(po, po[0:dv, 0:128])


def load_w_to(g, W_buf, W_ap, nK, w, dstb):
    wv = bf(dstb[:])[:, 0:nK * w].rearrange("p (k n) -> p k n", k=nK)
    for k0 in range(nK):
        ld(g, dstb, wv[:, k0, :], W_buf, W_ap[k0 * 128:(k0 + 1) * 128, :], q="pool")
    return wv


def mixer_mla(g):
    k = g.k
    MLA_SCALE = 192 ** -0.5
    FT2 = g.FT2
    nrm = g.T[2]
    ld(g, nrm, nrm[:, 0:512], g.mla_q_norm, bcast_row(g.mla_q_norm[0:1, :]))
    ld(g, nrm, nrm[:, 512:1024], g.mla_kv_norm, bcast_row(g.mla_kv_norm[0:1, :]))
    load_rope_tabs(g)
    linear_tm(g, ut_lhs(g, g.UT), 16, g.mla_w_dq, g.mla_w_dq[:, :], [(0, 512)], range(NT),
              rms_consume(g, nrm[:, 0:512], FT2, 0))
    def ckv_hook(t, b, ap):
        if t < 4:
            st_(g, g.o_ckv, g.o_ckv[rows(t), :], b, ap)
    rc = rms_consume(g, nrm[:, 512:1024], FT2, 4, hook=ckv_hook)

    def kr_to_ft(t_col, kb, kap):
        stg = rot(g, "rmst", [g.sm[3], g.sm[4]])
        sv = bf(stg[:])
        tm2ft(g, kb, kap, 64, stg, sv[0:64, 0:128])
        st_(g, FT2, FT2[0:64, 8, t_col * 128:(t_col + 1) * 128], stg, sv[0:64, 0:128], q="act")

    def kv_consume(t, si, c0, w, pb):
        if si == 0:
            return rc(t, si, c0, w, pb)
        kb = rot(g, "pnx", [g.sm[0], g.sm[1], g.sm[2]])
        cp(g, "act", kb, kb[:, 0:64], pb, pb[:, 0:64])
        if t < 4:
            st_(g, g.o_kr, g.o_kr[rows(t), :], kb, kb[:, 0:64])
        rope(g, kb, kb[:, 0:64].rearrange("p (h c) -> p h c", h=1), 1, t, g.sm[5], g.pk_ca)
        kr_to_ft(t, kb, kb[:, 0:64])
    linear_tm(g, ut_lhs(g, g.UT), 16, g.mla_w_dkv, g.mla_w_dkv[:, :], [(0, 512), (512, 64)], range(NT), kv_consume)
    for cb in range(2):
        ct = rot(g, "rmsj", [g.T[4], g.T[5]])
        ld(g, ct, ct[:, 0:512], g.l0ckv, g.l0ckv[rows(cb), :])
        stg = rot(g, "rmst", [g.sm[3], g.sm[4]])
        sv = bf(stg[:]).rearrange("p (k n) -> p k n", k=4)
        pt = rot(g, "trps", [g.psb[0], g.psb[1]])
        for j in range(4):
            tr(g, pt, pt[:, j * 128:(j + 1) * 128], ct, ct[:, j * 128:(j + 1) * 128])
        cp(g, "act", stg, sv, pt, pt[:, 0:512].rearrange("p (k n) -> p k n", k=4))
        st_(g, FT2, FT2[:, 4:8, NTOK + cb * 128:NTOK + (cb + 1) * 128], stg, sv)
        kb = rot(g, "pnx", [g.sm[0], g.sm[1], g.sm[2]])
        ld(g, kb, kb[:, 0:64], g.l0kr, g.l0kr[rows(cb), :])
        kr_to_ft(NT + cb, kb, kb[:, 0:64])
    def q_consume(t, si, c0, w, pb):
        qb = rot(g, "mlaq", [g.T[0], g.T[1]])
        cp(g, "act", qb, qb[:, 0:384], pb, pb[:, 0:384])
        x3 = qb[:, 0:384].rearrange("p (h c) -> p h c", h=2)[:, :, 128:192]
        rope(g, qb, x3, 2, t, g.sm[5], g.pk_ca)
        st_(g, g.SC2, g.SC2[rows(t), c0:c0 + 384], qb, qb[:, 0:384])
    linear_tm(g, ut_lhs(g, FT2, 4), 4, g.mla_w_uq, g.mla_w_uq[:, :], [(c, 384) for c in range(0, 3072, 384)], range(NT), q_consume)
    wuk = load_w_to(g, g.mla_w_uk, g.mla_w_uk[:, :], 4, D, g.B[2])
    wuv = load_w_to(g, g.mla_w_uv, g.mla_w_uv[:, :], 4, D, g.B[3])
    for (t0, nt, kind) in SEQS:
        S = nt * 128 + (256 if kind == "s" else 0)
        nblk = S // 128
        cb_ = g.B[0]
        ckvT = bf(cb_[:])[:, 0:4 * S].rearrange("p (k n) -> p k n", k=4)
        ld(g, cb_, ckvT[:, :, 0:nt * 128], FT2, FT2[:, 4:8, t0 * 128:(t0 + nt) * 128])
        kb_ = g.B[1]
        krT = bf(kb_[:])[0:64, 0:S]
        ld(g, kb_, krT[:, 0:nt * 128], FT2, FT2[0:64, 8, t0 * 128:(t0 + nt) * 128])
        if kind == "s":
            ld(g, cb_, ckvT[:, :, nt * 128:S], FT2, FT2[:, 4:8, NTOK:NTOK + 256])
            ld(g, kb_, krT[:, nt * 128:S], FT2, FT2[0:64, 8, NTOK:NTOK + 256])
        for h in range(16):
            kTb = rot(g, "mlakT", [g.T[0], g.B[7]])
            kTh = bf(kTb[:])[:, 0:S]
            for c0 in range(0, S, 512):
                cw = min(512, S - c0)
                pb = rot(g, "linps", [g.psb[2], g.psb[3], g.psb[4]])
                for kc in range(4):
                    mm(g, pb, pb[:, 0:cw], g.B[2], wuk[:, kc, h * 128:(h + 1) * 128], cb_, ckvT[:, kc, c0:c0 + cw], kc == 0, kc == 3)
                cp(g, "act", kTb, kTh[:, c0:c0 + cw], pb, pb[:, 0:cw])
            vb_ = g.T[1]
            vh = bf(vb_[:])[:, 0:nblk * 128].rearrange("p (b n) -> p b n", b=nblk)
            for b0 in range(0, nblk, 4):
                bn = min(4, nblk - b0)
                pb = rot(g, "linps", [g.psb[2], g.psb[3], g.psb[4]])
                for j in range(bn):
                    for kc in range(4):
                        mm(g, pb, pb[:, j * 128:(j + 1) * 128], cb_, ckvT[:, kc, (b0 + j) * 128:(b0 + j + 1) * 128],
                           g.B[3], wuv[:, kc, h * 128:(h + 1) * 128], kc == 0, kc == 3)
                cp(g, "act", vb_, vh[:, b0:b0 + bn, :], pb, pb[:, 0:bn * 128].rearrange("p (b n) -> p b n", b=bn))
            for tq in range(nt):
                t = t0 + tq
                qs = rot(g, "pnx", [g.sm[0], g.sm[1], g.sm[2]])
                ld(g, qs, qs[:, 0:192], g.SC2, g.SC2[rows(t), h * 192:(h + 1) * 192])
                qT = rot(g, "mqT", [g.pk_cb, g.pk_oh])
                qTv = bf(qT[:])
                tm2ft(g, qs, qs[:, 0:128], 128, qT, qTv[:, 0:128])
                tm2ft(g, qs, qs[:, 128:192], 64, qT, qTv[0:64, 128:256])

                def out_fn(po, oap, h=h, t=t):
                    ob = rot(g, "rmst", [g.sm[3], g.sm[4]])
                    ov = bf(ob[:])[:, 0:128]
                    cp(g, "act", ob, ov, po, oap)
                    st_(g, g.FT, g.FT[:, h, rows(t)], ob, ov, q="act")
                attention(g, S,
                          [(qT, qTv[:, 0:128], kTb, lambda c0, cw, kTh=kTh: kTh[:, c0:c0 + cw]),
                           (qT, qTv[0:64, 128:256], kb_, lambda c0, cw, krT=krT: krT[:, c0:c0 + cw])],
                          lambda blk, vb_=vb_, vh=vh: (vb_, vh[:, blk, :]), nblk, out_fn, MLA_SCALE)
    g.pre_pn()
    linear_tm(g, ut_lhs(g, g.FT), 16, g.mla_w_o, g.mla_w_o[:, :], [(c, 512) for c in range(0, D, 512)],
              range(NT), pn_consume(g))


def mixer_gqa(g):
    k = g.k
    GQA_SCALE = 64 ** -0.5
    FT2 = g.FT2
    VB = g.VB
    load_rope_tabs(g)
    sinkb = g.pk_w
    ld(g, sinkb, sinkb[:, 0:32], g.gqa_sink, bcast_row(g.gqa_sink[0:1, :]))

    def k_to_ft(col_tile, kb, kap512):
        stg = rot(g, "rmst", [g.sm[3], g.sm[4]])
        sv = bf(stg[:])[0:64, 0:512].rearrange("p (h n) -> p h n", h=4)
        for h4 in range(2):
            pt = rot(g, "trps", [g.psb[0], g.psb[1]])
            for j in range(4):
                tr(g, pt, pt[0:64, j * 128:(j + 1) * 128], kb, kap512[:, (h4 * 4 + j) * 64:(h4 * 4 + j + 1) * 64])
            stg = rot(g, "rmst", [g.sm[3], g.sm[4]])
            sv = bf(stg[:])[0:64, 0:512].rearrange("p (h n) -> p h n", h=4)
            cp(g, "act", stg, sv, pt, pt[0:64, 0:512].rearrange("p (h n) -> p h n", h=4))
            st_(g, FT2, FT2[0:64, h4 * 4:h4 * 4 + 4, col_tile * 128:(col_tile + 1) * 128], stg, sv, q="act")

    def v_to_vb(row_tile, vb_, vap512):
        stg = rot(g, "rmst", [g.sm[3], g.sm[4]])
        sv = bf(stg[:])[:, 0:512]
        cp(g, "dve", stg, sv, vb_, vap512)
        st_(g, VB, VB[row_tile * 128:(row_tile + 1) * 128, :], stg, sv)

    def consume(t, si, c0, w, pb):
        qb = rot(g, "mlaq", [g.T[0], g.T[1]])
        cp(g, "act", qb, qb[:, 0:512], pb, pb[:, 0:512])
        if si == 4 and t < 4:
            st_(g, g.o_k, g.o_k[rows(t), :], qb, qb[:, 0:512])
        if si == 5:
            if t < 4:
                st_(g, g.o_v, g.o_v[rows(t), :], qb, qb[:, 0:512])
            v_to_vb(t, qb, qb[:, 0:512])
            return
        rope(g, qb, qb[:, 0:512].rearrange("p (h c) -> p h c", h=8), 8, t, g.T[4], g.T[5])
        if si < 4:
            st_(g, g.SC2, g.SC2[rows(t), c0:c0 + 512], qb, qb[:, 0:512])
        else:
            k_to_ft(t, qb, qb[:, 0:512])
    linear_tm(g, ut_lhs(g, g.UT), 16, g.gqa_w_qkv, g.gqa_w_qkv[:, :], [(c, 512) for c in range(0, 3072, 512)], range(NT), consume)
    for cb in range(2):
        kb = rot(g, "mlaq", [g.T[0], g.T[1]])
        ld(g, kb, kb[:, 0:512], g.l1k, g.l1k[rows(cb), :])
        k_to_ft(NT + cb, kb, kb[:, 0:512])
        vb_ = rot(g, "mlaq", [g.T[0], g.T[1]])
        ld(g, vb_, vb_[:, 0:512], g.l1v, g.l1v[rows(cb), :])
        v_to_vb(NT + cb, vb_, vb_[:, 0:512])
    mk = g.B[0]
    mkv = mk[:, 0:3 * 640].rearrange("p (m n) -> p m n", m=3)
    k.op("pool", lambda h: h.memset(mk[:, 0:3 * 640], 0.0), reads=[], writes=[mk])
    ld(g, mk, mkv[:, 0, 128:256], g.c_mnext, g.c_mnext[:, :])
    ld(g, mk, mkv[:, 1, 0:128], g.c_mprev, g.c_mprev[:, :])
    ld(g, mk, mkv[:, 1, 256:384], g.c_mnext, g.c_mnext[:, :])
    ld(g, mk, mkv[:, 2, 0:128], g.c_mprev, g.c_mprev[:, :])
    for (t0, nt, kind) in SEQS:
        Sall = nt * 128 + (256 if kind == "s" else 0)
        nball = Sall // 128
        for kvh in range(8):
            kTb = rot(g, "gqakT", [g.T[0], g.T[2]])
            kT = bf(kTb[:])[0:64, 0:Sall]
            ld(g, kTb, kT[:, 0:nt * 128], FT2, FT2[0:64, kvh, t0 * 128:(t0 + nt) * 128])
            vb_ = rot(g, "gqavv", [g.T[1], g.B[7]])
            vv = bf(vb_[:])[:, 0:nball * 64].rearrange("p (b n) -> p b n", b=nball)
            ld(g, vb_, vv[:, 0:nt, :], VB, VB[t0 * 128:(t0 + nt) * 128, kvh * 64:(kvh + 1) * 64].rearrange("(b p) n -> p b n", p=128))
            if kind == "s":
                ld(g, kTb, kT[:, nt * 128:Sall], FT2, FT2[0:64, kvh, NTOK:NTOK + 256])
                ld(g, vb_, vv[:, nt:nball, :], VB, VB[NTOK:NTOK + 256, kvh * 64:(kvh + 1) * 64].rearrange("(b p) n -> p b n", p=128))
            for hi in range(4):
                h = kvh * 4 + hi
                for tq in range(nt):
                    t = t0 + tq
                    qs = rot(g, "pnx", [g.sm[0], g.sm[1], g.sm[2]])
                    ld(g, qs, qs[:, 0:64], g.SC2, g.SC2[rows(t), h * 64:(h + 1) * 64])
                    qT = rot(g, "mqT", [g.pk_cb, g.pk_oh])
                    qTv = bf(qT[:])
                    tm2ft(g, qs, qs[:, 0:64], 64, qT, qTv[0:64, 0:128])
                    if kind == "p":
                        ranges = [(0, Sall)]; vblocks = list(range(nball)); bias_fn = None
                    else:
                        lo, hi_ = max(0, tq - 1), min(nt - 1, tq + 1)
                        ranges = [(lo * 128, (hi_ - lo + 1) * 128), (nt * 128, 256)]
                        vblocks = list(range(lo, hi_ + 1)) + [nt, nt + 1]
                        mi = 0 if tq == 0 else (2 if tq == nt - 1 else 1)
                        def bias_fn(psb_, psap, sbb, sbap, mi=mi):
                            n = sbap.shape[-1]
                            tt(g, "dve", sbb, sbap, psb_, psap, mk, mkv[:, mi, 0:n], ALU.add)
                    Sq = sum(w_ for _, w_ in ranges)

                    def out_fn(po, oap, h=h, t=t):
                        ob = rot(g, "rmst", [g.sm[3], g.sm[4]])
                        ov = bf(ob[:])[0:64, 0:128]
                        cp(g, "act", ob, ov, po, oap)
                        st_(g, g.FT, g.FT[(h % 2) * 64:(h % 2 + 1) * 64, h // 2, rows(t)], ob, ov, q="act")
                    attention(g, Sq, [(qT, qTv[0:64, 0:128], kTb, lambda k0, kw, kT=kT: kT[:, k0:k0 + kw])],
                              lambda blk, vb_=vb_, vv=vv: (vb_, vv[:, blk, :]), len(vblocks), out_fn, GQA_SCALE,
                              bias_fn=bias_fn, sink_ap=(sinkb, sinkb[:, h:h + 1]), ranges=ranges, vblocks=vblocks)
    g.pre_pn()
    linear_tm(g, ut_lhs(g, g.FT), 16, g.gqa_w_o, g.gqa_w_o[:, :], [(c, 512) for c in range(0, D, 512)],
              range(NT), pn_consume(g))


def store_consume(g, dst, col0=0, dt_bf=False):
    def consume(t, si, c0, w, pb):
        for h0 in range(0, w, 256):
            hw = min(256, w - h0)
            ys = rot(g, "pny", [g.sm[3], g.sm[4], g.sm[5]])
            if dt_bf:
                yv = bf(ys[:])[:, 0:hw]
            else:
                yv = ys[:, 0:hw]
            cp(g, "act", ys, yv, pb, pb[:, h0:h0 + hw])
            st_(g, dst, dst[rows(t), col0 + c0 + h0:col0 + c0 + h0 + hw], ys, yv, q="act")
    return consume


def mixer_fnet(g):
    k = g.k
    PQ = g.SC2.t.bitcast(BF16)
    PQb = g.SC2
    for gi in range(4):
        for wi, W in enumerate((g.c_cc, g.c_sc)):
            def consume(t, si, c0, w, pb, gi=gi, wi=wi):
                for h0 in (0, 256):
                    ys = rot(g, "pny", [g.sm[3], g.sm[4], g.sm[5]])
                    yv = bf(ys[:])[:, 0:256]
                    cp(g, "act", ys, yv, pb, pb[:, h0:h0 + 256])
                    c = wi * D + gi * 512 + h0
                    st_(g, PQb, PQ[rows(t), c:c + 256], ys, yv, q="act")
            linear_tm(g, ut_lhs(g, g.UT, 4, kc0=4 * gi), 4, W, W[:, :], [(0, 512)], range(NT), consume)
    for (t0, nt, kind) in SEQS:
        T = nt * 128
        mats = []
        for mi, M in enumerate((g.c_ct[T], g.c_st[T])):
            mb = g.B[4 + mi]
            mv = bf(mb[:])[:, 0:nt * T].rearrange("p (k n) -> p k n", k=nt)
            for k0 in range(0, nt, 2):
                stg = rot(g, "wst", [g.T[4], g.T[3]])
                sv = stg[:, 0:2 * T].rearrange("p (k n) -> p k n", k=2)
                ld(g, stg, sv, M, M[k0 * 128:(k0 + 2) * 128, :].rearrange("(k p) n -> p k n", p=128))
                cp(g, "pool", mb, mv[:, k0:k0 + 2, :], stg, sv)
            mats.append((mb, mv))
        pq = []
        for wi in range(2):
            lst = []
            for k0 in range(0, nt, 4):
                kn = min(4, nt - k0)
                pb_ = g.B[wi * 2 + k0 // 4]
                pv = bf(pb_[:])[:, 0:kn * D].rearrange("p (k n) -> p k n", k=kn)
                ld(g, pb_, pv, PQb, PQ[(t0 + k0) * 128:(t0 + k0 + kn) * 128, wi * D:(wi + 1) * D].rearrange("(k p) n -> p k n", p=128))
                lst.append((pb_, pv))
            pq.append(lst)
        for tq in range(nt):
            ft = rot(g, "lny", [g.T[0], g.T[1]])
            for sl in range(4):
                pb = rot(g, "linps", [g.psb[2], g.psb[3], g.psb[4]])
                n = 0
                for wi in range(2):
                    mb, mv = mats[wi]
                    for tk in range(nt):
                        xb_, xv = pq[wi][tk // 4]
                        mm(g, pb, pb[:, 0:512], mb, mv[:, tk, tq * 128:(tq + 1) * 128], xb_, xv[:, tk % 4, sl * 512:(sl + 1) * 512],
                           n == 0, n == 2 * nt - 1)
                        n += 1
                cp(g, "act", ft, ft[:, sl * 512:(sl + 1) * 512], pb, pb[:, 0:512])
            emit_ut(g, ft, t0 + tq, 0, dst=g.FT, mod=False)
    g.pre_pn()
    linear_tm(g, ut_lhs(g, g.FT), 16, g.fnet_w_out, g.fnet_w_out[:, :], [(c, 512) for c in range(0, D, 512)],
              range(NT), pn_consume(g))


def mixer_conv(g):
    k = g.k
    BCH = g.SC2
    linear_tm(g, ut_lhs(g, g.UT), 16, g.conv_w_in, g.conv_w_in[:, :], [(c, 512) for c in range(0, 3 * D, 512)],
              range(NT), store_consume(g, BCH))
    Z = g.Zp
    zt = g.T[2]
    k.op("pool", lambda h: h.memset(zt[:], 0.0), reads=[], writes=[zt])
    for si, (t0, nt, kind) in enumerate(SEQS):
        for rr_ in (t0 * 128 + si, (t0 + nt) * 128 + si + 1):
            st_(g, Z, Z[rr_:rr_ + 1, :], zt, zt[0:1, :])
    for j in range(3):
        bb = g.B[2 + j // 2]
        ld(g, bb, bb[:, (j % 2) * D:(j % 2 + 1) * D], g.conv_w, bcast_row(g.conv_w[j:j + 1, :]))
    ld(g, g.B[3], g.B[3][:, D:2 * D], g.conv_b, bcast_row(g.conv_b[0:1, :]))
    for si, (t0, nt, kind) in enumerate(SEQS):
        for t in range(t0, t0 + nt):
            ct = rot(g, "cvc", [g.T[2], g.T[3]])
            ht = rot(g, "lny", [g.T[0], g.T[1]])
            ld(g, ct, ct[:], BCH, BCH[rows(t), D:2 * D])
            ld(g, ht, ht[:], BCH, BCH[rows(t), 2 * D:3 * D])
            tt(g, "dve", ct, ct[:], ct, ct[:], ht, ht[:], ALU.mult)
            st_(g, Z, Z[t * 128 + si + 1:t * 128 + si + 129, :], ct, ct[:])
    for si, (t0, nt, kind) in enumerate(SEQS):
        for t in range(t0, t0 + nt):
            acc = rot(g, "lny", [g.T[0], g.T[1]])
            base = t * 128 + si + 1
            for j in range(3):
                zt_ = rot(g, "cvc", [g.T[2], g.T[3]])
                ld(g, zt_, zt_[:], Z, Z[base + j - 1:base + j - 1 + 128, :])
                wbc = g.B[2 + j // 2][:, (j % 2) * D:(j % 2 + 1) * D]
                if j == 0:
                    tt(g, "dve", acc, acc[:], zt_, zt_[:], g.B[2], wbc, ALU.mult)
                else:
                    tt(g, "pool", zt_, zt_[:], zt_, zt_[:], g.B[2 + j // 2], wbc, ALU.mult)
                    tt(g, "dve", acc, acc[:], acc, acc[:], zt_, zt_[:], ALU.add)
            tt(g, "dve", acc, acc[:], acc, acc[:], g.B[3], g.B[3][:, D:2 * D], ALU.add)
            bt = rot(g, "cvc", [g.T[2], g.T[3]])
            ld(g, bt, bt[:], BCH, BCH[rows(t), 0:D])
            tt(g, "dve", acc, acc[:], acc, acc[:], bt, bt[:], ALU.mult)
            emit_ut(g, acc, t, 0, dst=g.FT, mod=False)
    g.pre_pn()
    linear_tm(g, ut_lhs(g, g.FT), 16, g.conv_w_out, g.conv_w_out[:, :], [(c, 512) for c in range(0, D, 512)],
              range(NT), pn_consume(g))


def peer(g, i):
    k = g.k
    nexp = 16384
    keyT = g.T[6]
    kT = keyT[:].rearrange("p (c n) -> p c n", c=16)
    for hc4 in range(4):
        raw = rot(g, "lny", [g.T[0], g.T[1]])
        rv = raw[:, 0:512].rearrange("p (c n) -> p c n", c=4)
        ld(g, raw, rv, g.peer_keys,
           g.peer_keys[(i * 16 + hc4 * 4) * 128:(i * 16 + hc4 * 4 + 4) * 128, :].rearrange("(c p) n -> p c n", p=128))
        pb = rot(g, "trps", [g.psb[0], g.psb[1]])
        for j in range(4):
            tr(g, pb, pb[:, j * 128:(j + 1) * 128], raw, rv[:, j, :])
        cp(g, "act", keyT, kT[:, hc4 * 4:hc4 * 4 + 4, :], pb, pb[:, 0:512].rearrange("p (c n) -> p c n", c=4))
    Wq = g.peer_w_q
    for sl in range(4):
        wb, wv = load_w(g, Wq, Wq[i, :, :], 16, sl * 512, 512)
        for grp in range(3):
            ub = rot(g, "ug", [g.B[6], g.B[7]])
            uv = bf(ub[:])[:, 0:16 * 512].rearrange("p (k n) -> p k n", k=16)
            ld(g, ub, uv, g.UT, g.UT[:, :, grp * 512:(grp + 1) * 512])
            q4 = g.T[3]
            q4v = q4[:].rearrange("p (c n) -> p c n", c=4)
            for j in range(4):
                pb = rot(g, "linps", [g.psb[2], g.psb[3], g.psb[4]])
                for kc in range(16):
                    mm(g, pb, pb[:, 0:512], wb, wv[:, kc, j * 128:(j + 1) * 128], ub, uv[:, kc, :], kc == 0, kc == 15)
                cp(g, "act", q4, q4v[:, j, :], pb, pb[:, 0:512])
            for tt_ in range(4):
                t = grp * 4 + tt_
                pb = rot(g, "trps", [g.psb[0], g.psb[1]])
                for j in range(4):
                    mm(g, pb, pb[:, j * 128:(j + 1) * 128], q4, q4v[:, j, tt_ * 128:(tt_ + 1) * 128],
                       keyT, kT[:, sl * 4 + j, :])
                ss = rot(g, "pss", [g.sm[0], g.sm[1]])
                sb2 = rot(g, "pss2", [g.sm[2], g.sm[3]])
                cp(g, "dve", ss, ss[:, 0:256], pb, pb[:, 0:256])
                cp(g, "dve", sb2, sb2[:, 0:256], pb, pb[:, 256:512])
                st_(g, g.SC, g.SC[rows(t), sl * 512:sl * 512 + 256], ss, ss[:, 0:256])
                st_(g, g.SC, g.SC[rows(t), sl * 512 + 256:sl * 512 + 512], sb2, sb2[:, 0:256])
    IDX = g.pk_idx; GATE = g.pk_gate; HB = g.pk_h
    V = g.pk_v; Vv = V[:, 0:32].rearrange("p (c n) -> p c n", c=2)
    I = g.pk_i; Iv = I[:, 0:32].rearrange("p (c n) -> p c n", c=2)
    IF = g.pk_if; IFv = IF[:, 0:32].rearrange("p (c n) -> p c n", c=2)
    W = g.pk_w; CA = g.pk_ca; CB = g.pk_cb; T8 = g.pk_t8; P8 = g.pk_p8; PF = g.pk_pf; OH = g.pk_oh; SEL = g.pk_sel
    dve = lambda fn, rd, wr: k.op("dve", fn, reads=rd, writes=wr)
    iota16 = g.pk_iota[:, 0:16]
    def topk_gen(t):
            r = 0 if t < 4 else 1
            S = g.T[5]
            ld(g, S, S[:], g.SC, g.SC[rows(t), :])
            Vall = g.pk_ca; V4 = Vall[:, 0:256].rearrange("p (h c k) -> p h c k", h=8, c=2)
            Iall = g.pk_iall; I4 = Iall[:, 0:256].rearrange("p (h c k) -> p h c k", h=8, c=2)
            IFall = g.pk_cb; IF4 = IFall[:, 0:256].rearrange("p (h c k) -> p h c k", h=8, c=2)
            CANDb = g.B[2]
            CAND = CANDb[:, 0:2048].rearrange("p (h n) -> p h n", h=8)
            CAND2 = CANDb[:, 2048:4096].rearrange("p (h n) -> p h n", h=8)
            T8a = g.pk_oh; T8v = T8a[:, 0:128].rearrange("p (h k) -> p h k", h=8)
            PFv = T8a[:, 128:256]
            P8a = g.pk_p8all; P8v = P8a[:, 0:128].rearrange("p (h k) -> p h k", h=8)
            AB = g.pk_h
            for h in range(8):
                for c in range(2):
                    s = S[:, (2 * h + c) * 128:(2 * h + c + 1) * 128]
                    dve(lambda e, s=s, h=h, c=c: e.max(out=V4[:, h, c, 0:8], in_=s), [S], [Vall])
                    dve(lambda e, s=s, h=h, c=c: e.max_index(out=I4[:, h, c, 0:8], in_max=V4[:, h, c, 0:8], in_values=s), [S, Vall], [Iall])
                    dve(lambda e, s=s, h=h, c=c: e.match_replace(out=W[:, 0:128], in_to_replace=V4[:, h, c, 0:8], in_values=s, imm_value=NEG), [S, Vall], [W])
                    dve(lambda e, h=h, c=c: e.max(out=V4[:, h, c, 8:16], in_=W[:, 0:128]), [W], [Vall])
                    dve(lambda e, h=h, c=c: e.max_index(out=I4[:, h, c, 8:16], in_max=V4[:, h, c, 8:16], in_values=W[:, 0:128]), [W, Vall], [Iall])
                    yield
            cp(g, "dve", IFall, IFall[:, 0:256], Iall, Iall[:, 0:256])
            tt(g, "dve", CANDb, CANDb[:, 0:2048].rearrange("p (h a b) -> p h a b", h=8, a=16),
               Vall, V4[:, :, 0, :].unsqueeze(3).to_broadcast([128, 8, 16, 16]),
               Vall, V4[:, :, 1, :].unsqueeze(2).to_broadcast([128, 8, 16, 16]), ALU.add)
            for h in range(8):
                dve(lambda e, h=h: e.max(out=T8v[:, h, 0:8], in_=CAND[:, h, :]), [CANDb], [T8a])
                dve(lambda e, h=h: e.max_index(out=P8v[:, h, 0:8], in_max=T8v[:, h, 0:8], in_values=CAND[:, h, :]), [CANDb, T8a], [P8a])
                dve(lambda e, h=h: e.match_replace(out=CAND2[:, h, :], in_to_replace=T8v[:, h, 0:8], in_values=CAND[:, h, :], imm_value=NEG), [CANDb, T8a], [CANDb])
                dve(lambda e, h=h: e.max(out=T8v[:, h, 8:16], in_=CAND2[:, h, :]), [CANDb], [T8a])
                dve(lambda e, h=h: e.max_index(out=P8v[:, h, 8:16], in_max=T8v[:, h, 8:16], in_values=CAND2[:, h, :]), [CANDb, T8a], [P8a])
                yield
            cp(g, "dve", T8a, PFv, P8a, P8a[:, 0:128])
            GEb = g.B[3]
            GE = GEb[:, 0:2048].rearrange("p (j m) -> p j m", j=128)
            tt(g, "dve", GEb, GE, T8a, PFv.unsqueeze(2).to_broadcast([128, 128, 16]),
               g.pk_iota, g.pk_thr[:, 0:16].unsqueeze(1).to_broadcast([128, 128, 16]), ALU.is_ge)
            dve(lambda e: e.tensor_reduce(out=AB[:, 0:128], in_=GE, axis=AX.X, op=ALU.add), [GEb], [AB])
            yield
            stt(g, AB, AB[:, 128:256], AB, AB[:, 0:128], -16.0, T8a, PFv, ALU.mult, ALU.add)
            for side, dstb in ((0, g.pk_i1f), (1, g.pk_i2f)):
                tt(g, "dve", GEb, GE, AB, AB[:, side * 128:(side + 1) * 128].unsqueeze(2).to_broadcast([128, 128, 16]),
                   g.pk_iota, iota16.unsqueeze(1).to_broadcast([128, 128, 16]), ALU.is_equal)
                GE4 = GEb[:, 0:2048].rearrange("p (h k m) -> p h k m", h=8, k=16)
                tt(g, "dve", GEb, GE4, GEb, GE4, IFall, IF4[:, :, side, :].unsqueeze(2).to_broadcast([128, 8, 16, 16]), ALU.mult)
                dve(lambda e, dstb=dstb: e.tensor_reduce(out=dstb[:, :], in_=GE, axis=AX.X, op=ALU.add), [GEb], [dstb])
            tt(g, "dve", GATE, GATE[:, 0:128].rearrange("p (h k) -> p h k", h=8), T8a, T8v,
               T8a, T8v[:, :, 0:1].to_broadcast([128, 8, 16]), ALU.subtract)
            act(g, GATE, GATE[:, 0:128], GATE, GATE[:, 0:128], AF.Exp)
            dve(lambda e: e.tensor_reduce(out=SEL[:, 0:8], in_=GATE[:, 0:128].rearrange("p (h k) -> p h k", h=8), axis=AX.X, op=ALU.add), [GATE], [SEL])
            dve(lambda e: e.reciprocal(out=SEL[:, 8:16], in_=SEL[:, 0:8]), [SEL], [SEL])
            tt(g, "dve", GATE, GATE[:, 0:128].rearrange("p (h k) -> p h k", h=8), GATE, GATE[:, 0:128].rearrange("p (h k) -> p h k", h=8),
               SEL, SEL[:, 8:16].unsqueeze(2).to_broadcast([128, 8, 16]), ALU.mult)
            yield
    def g_form(t, filler):
            r = 0 if t < 4 else 1
            trb = g.T[4]
            trv = trb[:, 0:384].rearrange("p (a n) -> p a n", a=3)
            pt = rot(g, "trps", [g.psb[0], g.psb[1]])
            tr(g, pt, pt[:, 0:128], g.pk_i1f, g.pk_i1f[:, :])
            tr(g, pt, pt[:, 128:256], g.pk_i2f, g.pk_i2f[:, :])
            tr(g, pt, pt[:, 256:384], GATE, GATE[:, :])
            trv = bf(trb[:])[:, 0:384].rearrange("p (a n) -> p a n", a=3)
            cp(g, "act", trb, trv, pt, pt[:, 0:384].rearrange("p (a n) -> p a n", a=3))
            stA = g.B[0]; stB = g.B[1]
            sA = bf(stA[:]).rearrange("p (c n) -> p c n", c=64)
            sB = bf(stB[:]).rearrange("p (c n) -> p c n", c=64)
            iota128 = bf(g.pk_iotab[:])[:, 0:128]
            NB = 32
            for nb0 in range(0, 128, NB):
                o1 = rot(g, "oh1", [g.T[0], g.T[1]])
                o2 = rot(g, "oh2", [g.T[2], g.T[3]])
                o1v = bf(o1[:]).rearrange("p (n c) -> p n c", n=NB)
                o2v = bf(o2[:]).rearrange("p (n c) -> p n c", n=NB)
                iob = iota128.unsqueeze(1).to_broadcast([128, NB, 128])
                tt(g, "dve", o1, o1v, g.pk_iotab, iob, trb, trv[:, 0, nb0:nb0 + NB].unsqueeze(2).to_broadcast([128, NB, 128]), ALU.is_equal)
                tt(g, "dve", o1, o1v, o1, o1v, trb, trv[:, 2, nb0:nb0 + NB].unsqueeze(2).to_broadcast([128, NB, 128]), ALU.mult)
                tt(g, "dve", o2, o2v, g.pk_iotab, iob, trb, trv[:, 1, nb0:nb0 + NB].unsqueeze(2).to_broadcast([128, NB, 128]), ALU.is_equal)
                for n0 in range(nb0, nb0 + NB, 4):
                    pg = rot(g, "linps", [g.psb[2], g.psb[3], g.psb[4]])
                    for j in range(4):
                        n = n0 + j
                        mm(g, pg, pg[:, j * 128:(j + 1) * 128], o1, o1v[:, n - nb0, :], o2, o2v[:, n - nb0, :])
                    pin = pg[:, 0:512].rearrange("p (n c) -> p n c", n=4)
                    cp(g, "act", stA, sA.rearrange("p c n -> p n c")[:, n0:n0 + 4, :], pg, pin[:, :, 0:64])
                    cp(g, "act", stB, sB.rearrange("p c n -> p n c")[:, n0:n0 + 4, :], pg, pin[:, :, 64:128])
                for _ in range(5):
                    next(filler, None)
            for q8 in range(4):
                st_(g, g.Gd, g.Gd[q8 * 16:(q8 + 1) * 16, :, rows(t)].rearrange("c p n -> p c n"), stA, sA[:, q8 * 16:(q8 + 1) * 16, :], q="sp")
                st_(g, g.Gd, g.Gd[64 + q8 * 16:64 + (q8 + 1) * 16, :, rows(t)].rearrange("c p n -> p c n"), stB, sB[:, q8 * 16:(q8 + 1) * 16, :], q="sp")
    gens = [topk_gen(t) for t in range(NT)]
    for _ in gens[0]:
        pass
    for t in range(NT):
        filler = gens[t + 1] if t + 1 < NT else iter(())
        g_form(t, filler)
        for _ in filler:
            pass
    Ut = g.peer_u[:, :].rearrange("(l p c) d -> l c p d", l=g.cfg.get("nlay", DEPTH), c=128)
    Vt = g.peer_v[:, :].rearrange("(l p c) d -> l c p d", l=g.cfg.get("nlay", DEPTH), c=128)
    NG = 4
    for (tp0, ntp) in ((0, 4), (4, 8)):
        NTp = ntp * 128
        ngrp = ntp // 4
        ubs = [g.B[6], g.B[7]][:ngrp]
        uvs = []
        for gi, ub in enumerate(ubs):
            uv = bf(ub[:])[:, 0:16 * 512].rearrange("p (k n) -> p k n", k=16)
            ld(g, ub, uv, g.UT, g.UT[:, :, (tp0 + gi * 4) * 128:(tp0 + gi * 4 + 4) * 128])
            uvs.append(uv)
        accs = [(g.B[tt_ // 2], g.B[tt_ // 2][:, (tt_ % 2) * D:(tt_ % 2 + 1) * D]) for tt_ in range(ntp)]
        Vsets = []
        for pb_list in ([g.B[4], g.B[4]], [g.T[4], g.T[5]]):
            vl = []
            for cl in range(NG):
                pbuf = pb_list[cl // 2]
                if pbuf is g.B[4]:
                    sb_ = subs(g, pbuf, NG)[cl]
                    vl.append((sb_, bf(pbuf[:]).rearrange("p (c n) -> p c n", c=NG)[:, cl, :]))
                else:
                    sb_ = subs(g, pbuf, 2)[cl % 2]
                    vl.append((sb_, bf(pbuf[:]).rearrange("p (c n) -> p c n", c=2)[:, cl % 2, :]))
            Vsets.append(vl)
        ATb = g.T[6]
        ATv = bf(ATb[:]).rearrange("p (c n) -> p c n", c=NG)
        ATs = subs(g, ATb, NG)
        GX = g.B[5]
        GXv = bf(GX[:]).rearrange("p (s n) -> p s n", s=8)
        GXs = subs(g, GX, 8)
        u16l = []
        utl = []
        for cl in range(NG):
            ub_ = [g.T[0], g.T[1]][cl // 2]
            u16l.append((subs(g, ub_, 2)[cl % 2], bf(ub_[:]).rearrange("p (s n) -> p s n", s=2)[:, cl % 2, :]))
            tb_ = [g.T[2], g.T[3]][cl // 2]
            utl.append((subs(g, tb_, 2)[cl % 2], bf(tb_[:]).rearrange("p (s k n) -> p s k n", s=2, k=16)[:, cl % 2]))
        for cg in range(128 // NG):
            Vset = Vsets[cg % 2]
            for cl in range(NG):
                c = cg * NG + cl
                ld(g, u16l[cl][0], u16l[cl][1], g.peer_u, Ut[i, c], q="pool")
                ld(g, GXs[cl], GXv[:, cl, 0:NTp], g.Gd, g.Gd[c, :, tp0 * 128:tp0 * 128 + NTp])
            for cl in range(NG):
                c = cg * NG + cl
                ld(g, Vset[cl][0], Vset[cl][1], g.peer_v, Vt[i, c], q="pool")
            for cl in range(NG):
                ub_, uap = u16l[cl]
                tb_, tap = utl[cl]
                for half in range(2):
                    pt = rot(g, "dtr", [g.psb[0], g.psA3[2]])
                    pv = bf(pt[:])[:, 0:1024] if pt is g.psb[0] else bf(pt[:])[:, 2048:3072]
                    for j in range(8):
                        kc = half * 8 + j
                        tr(g, pt, pv[:, j * 128:(j + 1) * 128], ub_, uap[:, kc * 128:(kc + 1) * 128], fp32=False)
                    cp(g, "act", tb_, tap[:, half * 8:half * 8 + 8, :], pt, pv.rearrange("p (k n) -> p k n", k=8))
            for cl in range(NG):
                tb_, tap = utl[cl]
                for gi in range(ngrp):
                    ph = rot(g, "dph", [g.psA3[0], g.psA3[1]])
                    phv = ph[:, 0:512] if ph is g.psA3[0] else ph[:, 512:1024]
                    for kc in range(16):
                        mm(g, ph, phv, tb_, tap[:, kc, :], ubs[gi], uvs[gi][:, kc, :], kc == 0, kc == 15)
                    gsel = 4 + (g.rr.get("dge", 0) % 4); g.rr["dge"] = g.rr.get("dge", 0) + 1
                    ge = GXv[:, gsel, 0:512]
                    act(g, GXs[gsel], ge, ph, phv, AF.Gelu)
                    tt(g, "dve", ATs[cl], ATv[:, cl, gi * 512:(gi + 1) * 512], GXs[gsel], ge, GXs[cl], GXv[:, cl, gi * 512:(gi + 1) * 512], ALU.mult)
            for tt_ in range(ntp):
                ab, aap = accs[tt_]
                for sl in range(4):
                    po = g.psb[1 + sl]
                    for cl in range(NG):
                        mm(g, po, po[:, 0:512], ATs[cl], ATv[:, cl, tt_ * 128:(tt_ + 1) * 128], Vset[cl][0], Vset[cl][1][:, sl * 512:(sl + 1) * 512],
                           cl == 0, cl == NG - 1)
                    if cg == 0:
                        cp(g, "dve", ab, aap[:, sl * 512:(sl + 1) * 512], po, po[:, 0:512])
                    else:
                        tt(g, "dve", ab, aap[:, sl * 512:(sl + 1) * 512], po, po[:, 0:512], ab, aap[:, sl * 512:(sl + 1) * 512], ALU.add)
        r = 0 if tp0 < 4 else 1
        gt = g.T[0]
        ld(g, gt, gt[:], g.MOD, bcast_row(g.MOD[2 * i + r:2 * i + r + 1, 5 * D:6 * D]))
        for tt_ in range(ntp):
            t = tp0 + tt_
            ab, aap = accs[tt_]
            xt = g.T[1]
            ld(g, xt, xt[:], g.X, g.X[rows(t), :])
            tt(g, "dve", ab, aap, ab, aap, gt, gt[:], ALU.mult)
            stt(g, ab, aap, xt, xt[:], ALPHA, ab, aap, ALU.mult, ALU.add)
            st_(g, g.Y, g.Y[rows(t), :], ab, aap, q="sp")


def host_consts():
    c = {}
    c["c_ident"] = np.eye(128, dtype=np.float32)
    cos = np.ones((NTOK, 64), np.float32)
    sin = np.zeros((NTOK, 64), np.float32)
    pos = np.arange(NS_TOK)
    row = (pos // 64).astype(np.float32)
    col = (pos % 64).astype(np.float32)
    inv = (10000.0 ** (-np.arange(16, dtype=np.float32) / 16)).astype(np.float32)
    ar = (row[:, None] * inv[None, :]).astype(np.float32)
    ac = (col[:, None] * inv[None, :]).astype(np.float32)
    cos[NP_TOK:, 0:16] = np.cos(ar); cos[NP_TOK:, 16:32] = np.cos(ar)
    cos[NP_TOK:, 32:48] = np.cos(ac); cos[NP_TOK:, 48:64] = np.cos(ac)
    sin[NP_TOK:, 0:16] = -np.sin(ar); sin[NP_TOK:, 16:32] = np.sin(ar)
    sin[NP_TOK:, 32:48] = -np.sin(ac); sin[NP_TOK:, 48:64] = np.sin(ac)
    c["c_rcos"] = cos
    c["c_rsin"] = sin
    j = np.arange(512, dtype=np.float64)
    a = 2 * np.pi * np.outer(j, j) / 512
    c["c_cc"] = np.cos(a).astype(np.float32)
    c["c_sc"] = np.sin(a).astype(np.float32)
    for T in (256, 1024):
        tt_ = np.arange(T, dtype=np.float64)
        a = 2 * np.pi * np.outer(tt_, tt_) / T
        nrm = 1.0 / math.sqrt(T * 512)
        c[f"c_ct{T}"] = (np.cos(a) * nrm).astype(np.float32)
        c[f"c_st{T}"] = (-np.sin(a) * nrm).astype(np.float32)
    ii = np.arange(128)
    c["c_mprev"] = np.where(ii[None, :] >= ii[:, None], 0.0, NEG).astype(np.float32)
    c["c_mnext"] = np.where(ii[None, :] <= ii[:, None], 0.0, NEG).astype(np.float32)
    c["c_iota"] = np.tile(np.arange(256, dtype=np.float32)[None, :], (128, 1))
    return c


def make_in_maps(inp):
    consts = host_consts()
    f = lambda a: np.ascontiguousarray(np.asarray(a, dtype=np.float32))
    shared = {
        "ada_w": f(inp["ada_w"]), "ada_b": f(inp["ada_b"]),
        "ln1_g": f(inp["ln1_g"]), "ln1_b": f(inp["ln1_b"]), "ln2_g": f(inp["ln2_g"]), "ln2_b": f(inp["ln2_b"]),
        "mla_w_dq": f(inp["mla_w_dq"]), "mla_q_norm": f(inp["mla_q_norm"]).reshape(1, 512),
        "mla_w_uq": f(inp["mla_w_uq"]), "mla_w_dkv": f(inp["mla_w_dkv"]),
        "mla_kv_norm": f(inp["mla_kv_norm"]).reshape(1, 512), "mla_w_uk": f(inp["mla_w_uk"]),
        "mla_w_uv": f(inp["mla_w_uv"]), "mla_w_o": f(inp["mla_w_o"]),
        "gqa_w_qkv": f(inp["gqa_w_qkv"]), "gqa_sink": f(inp["gqa_sink"]).reshape(1, 32),
        "gqa_w_o": f(inp["gqa_w_o"]), "fnet_w_out": f(inp["fnet_w_out"]),
        "conv_w_in": f(inp["conv_w_in"]), "conv_w": f(inp["conv_w"]), "conv_b": f(inp["conv_b"]).reshape(1, D),
        "conv_w_out": f(inp["conv_w_out"]), "peer_w_q": f(inp["peer_w_q"]),
        "peer_sub_keys": f(inp["peer_sub_keys"]).reshape(DEPTH * 16 * 128, 128),
        "peer_u": f(inp["peer_u"]).reshape(DEPTH * 16384, D), "peer_v": f(inp["peer_v"]).reshape(DEPTH * 16384, D),
    }
    shared.update(consts)
    xp = f(inp["x_prompt"]); xs = f(inp["x_sample"])
    maps = []
    for c in range(8):
        b = c // 4
        m = dict(shared)
        m["xin"] = np.concatenate([xp[2 * c].reshape(256, D), xp[2 * c + 1].reshape(256, D), xs[b]], axis=0)
        m["cond"] = np.stack([f(inp["c_ctx"]), f(inp["c"])[b]], axis=0)
        m["l0ckv"] = f(inp["cache_l0_ckv"])[b]; m["l0kr"] = f(inp["cache_l0_krope"])[b]
        m["l1k"] = f(inp["cache_l1_k"])[b].reshape(256, 512); m["l1v"] = f(inp["cache_l1_v"])[b].reshape(256, 512)
        maps.append(m)
    return maps


def kernel(**inp):
    nc = bass.Bass("TRN2", target_bir_lowering=False)
    build(nc)
    maps = make_in_maps(inp)
    res = run_bass_kernel_spmd(nc, maps, core_ids=list(range(8))).results
    yp = np.zeros((16, 256, D), np.float32); ys = np.zeros((2, 1024, D), np.float32)
    ckv = np.zeros((16, 256, 512), np.float32); kr = np.zeros((16, 256, 64), np.float32)
    kk = np.zeros((16, 256, 8, 64), np.float32); vv = np.zeros((16, 256, 8, 64), np.float32)
    for c in range(8):
        r = res[c]
        b, qd = c // 4, c % 4
        yp[2 * c] = r["y"][0:256]; yp[2 * c + 1] = r["y"][256:512]
        ys[b, qd * 256:(qd + 1) * 256] = r["y"][512 + qd * 256:512 + (qd + 1) * 256]
        ckv[2 * c] = r["o_ckv"][0:256]; ckv[2 * c + 1] = r["o_ckv"][256:512]
        kr[2 * c] = r["o_kr"][0:256]; kr[2 * c + 1] = r["o_kr"][256:512]
        kk[2 * c] = r["o_k"][0:256].reshape(256, 8, 64); kk[2 * c + 1] = r["o_k"][256:512].reshape(256, 8, 64)
        vv[2 * c] = r["o_v"][0:256].reshape(256, 8, 64); vv[2 * c + 1] = r["o_v"][256:512].reshape(256, 8, 64)
    return (yp, ys, ckv, kr, kk, vv)
```

```python
import contextlib
import math
import numpy as np
import ml_dtypes
import concourse.bass as bass
import concourse.mybir as mybir
from concourse.bass_utils import run_bass_kernel_spmd

F32 = mybir.dt.float32
BF16 = mybir.dt.bfloat16
U32 = mybir.dt.uint32
I32 = mybir.dt.int32
AF = mybir.ActivationFunctionType
ALU = mybir.AluOpType
AX = mybir.AxisListType

D = 2048
DEPTH = 4
NP_TOK = 512
NS_TOK = 1024
NTOK = NP_TOK + NS_TOK
NT = NTOK // 128
ALPHA = (2 * DEPTH) ** 0.25
LN_EPS = 1e-5
RMS_EPS = 1e-6
NEG = -1e30


class Buf:
    def __init__(self, name, t=None):
        self.name = name
        self.t = t
        self.w = None
        self.r = {}
        self.alias = []

    def __getitem__(self, idx):
        return self.t[idx]


class Eng:
    def __init__(self, name, handle):
        self.name = name
        self.h = handle
        self.sems = []
        self.cur = 0
        self.count = 0
        self.known = {}
        self.prog = []
        self.dslots = []
        self.duse = []
        self.dnext = 0


class K:
    ROT = 20000

    def __init__(self, nc, stack):
        self.nc = nc
        self.stack = stack
        self.E = {
            "pe": Eng("pe", nc.tensor),
            "act": Eng("act", nc.scalar),
            "dve": Eng("dve", nc.vector),
            "pool": Eng("pool", nc.gpsimd),
            "sp": Eng("sp", nc.sync),
        }
        self.nsem = 0
        for e in self.E.values():
            e.sems.append(self.sem(e.name))
        for qn, n in (("sp", 24), ("pool", 24), ("act", 8)):
            e = self.E[qn]
            e.dslots = [self.sem(f"d{qn}{i}") for i in range(n)]
            e.duse = [0] * n
        self.nbuf = 0

    def sem(self, name):
        self.nsem += 1
        return self.stack.enter_context(self.nc.semaphore(f"s{self.nsem}_{name}"))

    def sb(self, name, shape, dt=F32):
        self.nbuf += 1
        t = self.stack.enter_context(self.nc.sbuf_tensor(f"{name}_{self.nbuf}", list(shape), dt))
        return Buf(name, t)

    def ps(self, name, shape, dt=F32):
        self.nbuf += 1
        t = self.stack.enter_context(self.nc.psum_tensor(f"{name}_{self.nbuf}", list(shape), dt))
        return Buf(name, t)

    def dram(self, name, shape, dt=F32, kind="Internal"):
        t = self.nc.dram_tensor(name, list(shape), dt, kind=kind).ap()
        return Buf(name, t)

    def _waits(self, e, reads, writes, skip_self=False):
        need = {}
        def add(ev):
            if ev is None:
                return
            sem, val = ev
            if skip_self and any(sem is s for s in e.sems):
                return
            k = id(sem)
            if e.known.get(k, 0) >= val:
                return
            if k not in need or need[k][1] < val:
                need[k] = (sem, val)
        for b in reads:
            add(b.w)
        for b in writes:
            add(b.w)
            for ev in b.r.values():
                add(ev)
        out = list(need.values())
        for sem, val in out:
            e.known[id(sem)] = val
        return out

    def _commit(self, ev, reads, writes):
        for b in reads:
            b.r[id(ev[0])] = ev
        for b in writes:
            b.w = ev
            b.r = {}

    @staticmethod
    def _expand(bufs):
        out = []
        for b in bufs:
            out.append(b)
            out.extend(b.alias)
        return out

    def op(self, en, fn, reads=(), writes=()):
        reads = self._expand(reads); writes = self._expand(writes)
        e = self.E[en]
        waits = self._waits(e, reads, writes, skip_self=(en == "pe"))
        if e.count >= self.ROT:
            e.sems.append(self.sem(e.name))
            e.cur += 1
            e.count = 0
        e.count += 1
        sem = e.sems[e.cur]
        ev = (sem, e.count)
        e.prog.append((waits, fn, sem, 1))
        if en == "pe":
            e.known[id(sem)] = e.count
        self._commit(ev, reads, writes)
        return ev

    def dma(self, qn, fn, reads=(), writes=()):
        reads = self._expand(reads); writes = self._expand(writes)
        e = self.E[qn]
        waits = self._waits(e, reads, writes)
        slot = e.dnext
        e.dnext = (e.dnext + 1) % len(e.dslots)
        sem = e.dslots[slot]
        prev = e.duse[slot] * 16
        if prev and e.known.get(id(sem), 0) < prev:
            waits.append((sem, prev))
            e.known[id(sem)] = prev
        e.duse[slot] += 1
        ev = (sem, e.duse[slot] * 16)
        e.prog.append((waits, fn, sem, 16))
        self._commit(ev, reads, writes)
        return ev

    def finish(self, final_bufs):
        e = self.E["sp"]
        fw = []
        seen = {}
        for b in final_bufs:
            for ev in ([b.w] if b.w else []) + list(b.r.values()):
                k = id(ev[0])
                if k not in seen or seen[k][1] < ev[1]:
                    seen[k] = ev
        fw = list(seen.values())
        with self.nc.Block() as block:
            def emit(en):
                eng = self.E[en]
                def body(h):
                    for waits, fn, sem, inc in eng.prog:
                        for s, v in waits:
                            h.wait_ge(s, v)
                        fn(h).then_inc(sem, inc)
                    if en == "sp":
                        for s, v in fw:
                            h.wait_ge(s, v)
                return body
            block.tensor(emit("pe"))
            block.scalar(emit("act"))
            block.vector(emit("dve"))
            block.gpsimd(emit("pool"))
            block.sync(emit("sp"))


SEQS = [(0, 2, "p"), (2, 2, "p"), (4, 8, "s")]


class P:
    pass


def build(nc, cfg=None):
    cfg = cfg or {}
    st = contextlib.ExitStack()
    k = K(nc, st)
    g = P()
    g.k = k
    g.cfg = cfg
    ein = lambda n, s: k.dram(n, s, F32, "ExternalInput")
    eout = lambda n, s: k.dram(n, s, F32, "ExternalOutput")
    g.xin = ein("xin", [NTOK, D])
    g.cond = ein("cond", [2, D])
    g.l0ckv = ein("l0ckv", [256, 512]); g.l0kr = ein("l0kr", [256, 64])
    g.l1k = ein("l1k", [256, 512]); g.l1v = ein("l1v", [256, 512])
    g.ada_w = ein("ada_w", [DEPTH, D, 6 * D]); g.ada_b = ein("ada_b", [DEPTH, 6 * D])
    g.ln_g = [ein("ln1_g", [DEPTH, D]), ein("ln2_g", [DEPTH, D])]
    g.ln_b = [ein("ln1_b", [DEPTH, D]), ein("ln2_b", [DEPTH, D])]
    g.mla_w_dq = ein("mla_w_dq", [D, 512]); g.mla_q_norm = ein("mla_q_norm", [1, 512])
    g.mla_w_uq = ein("mla_w_uq", [512, 3072]); g.mla_w_dkv = ein("mla_w_dkv", [D, 576])
    g.mla_kv_norm = ein("mla_kv_norm", [1, 512]); g.mla_w_uk = ein("mla_w_uk", [512, D])
    g.mla_w_uv = ein("mla_w_uv", [512, D]); g.mla_w_o = ein("mla_w_o", [D, D])
    g.gqa_w_qkv = ein("gqa_w_qkv", [D, 3072]); g.gqa_sink = ein("gqa_sink", [1, 32])
    g.gqa_w_o = ein("gqa_w_o", [D, D]); g.fnet_w_out = ein("fnet_w_out", [D, D])
    g.conv_w_in = ein("conv_w_in", [D, 3 * D]); g.conv_w = ein("conv_w", [3, D])
    g.conv_b = ein("conv_b", [1, D]); g.conv_w_out = ein("conv_w_out", [D, D])
    g.peer_w_q = ein("peer_w_q", [DEPTH, D, D]); g.peer_keys = ein("peer_sub_keys", [DEPTH * 16 * 128, 128])
    g.peer_u = ein("peer_u", [cfg.get("nexp", DEPTH * 16384), D]); g.peer_v = ein("peer_v", [cfg.get("nexp", DEPTH * 16384), D])
    g.c_ident = ein("c_ident", [128, 128])
    g.c_rcos = ein("c_rcos", [NTOK, 64]); g.c_rsin = ein("c_rsin", [NTOK, 64])
    g.c_cc = ein("c_cc", [512, 512]); g.c_sc = ein("c_sc", [512, 512])
    g.c_ct = {256: ein("c_ct256", [256, 256]), 1024: ein("c_ct1024", [1024, 1024])}
    g.c_st = {256: ein("c_st256", [256, 256]), 1024: ein("c_st1024", [1024, 1024])}
    g.c_mprev = ein("c_mprev", [128, 128]); g.c_mnext = ein("c_mnext", [128, 128])
    g.c_iota = ein("c_iota", [128, 256])
    g.y = eout("y", [NTOK, D])
    g.o_ckv = eout("o_ckv", [NP_TOK, 512]); g.o_kr = eout("o_kr", [NP_TOK, 64])
    g.o_k = eout("o_k", [NP_TOK, 512]); g.o_v = eout("o_v", [NP_TOK, 512])
    g.dbg = {}
    for name, shape in cfg.get("dbg", {}).items():
        g.dbg[name] = eout("dbg_" + name, shape)
    g.X = k.dram("X", [NTOK, D]); g.Y = k.dram("Y", [NTOK, D]); g.U2 = k.dram("U2", [NTOK, D])
    g.UT = k.dram("UT", [128, 16, NTOK], BF16)
    g.MOD = k.dram("MOD", [DEPTH * 2, 6 * D])
    g.SC = k.dram("SC", [NTOK, D])
    g.SC2 = k.dram("SC2", [NTOK, 3 * D])
    g.FT = k.dram("FT", [128, 16, NTOK + 256], BF16)
    g.FT2 = k.dram("FT2", [128, 16, NTOK + 256], BF16)
    g.Zp = k.dram("Zp", [NTOK + 4, D])
    g.VB = k.dram("VB", [NTOK + 256, 512], BF16)
    g.B = [k.sb(f"B{i}", [128, 4096]) for i in range(8)]
    g.T = [k.sb(f"T{i}", [128, 2048]) for i in range(7)]
    g.identf = k.sb("identf", [128, 128]); g.identb = k.sb("identb", [128, 128], BF16)
    g.csT = k.sb("csT", [128, 16, 2])
    g.modT = k.sb("modT", [128, 2, 4, 16])
    g.sm = [k.sb(f"sm{i}", [128, 256]) for i in range(8)]
    g.psA = k.ps("psA", [128, 1536])
    g.psb = [k.ps(f"ps{i}", [128, 512]) for i in range(5)]
    g.psA3 = [Buf(f"psA{j}", g.psA.t) for j in range(3)]
    for b_ in g.psA3:
        b_.alias = [g.psA]
    g.psA.alias = list(g.psA3)
    g.Gd = k.dram("Gd", [128, 128, NTOK], BF16)
    g.rr = {}
    g.pk_iall = k.sb("pk_iall", [128, 256], U32); g.pk_p8all = k.sb("pk_p8all", [128, 128], U32)
    g.pk_i1f = k.sb("pk_i1f", [128, 128]); g.pk_i2f = k.sb("pk_i2f", [128, 128])
    g.pk_idx = k.sb("pk_idx", [128, 8], U32); g.pk_gate = k.sb("pk_gate", [128, 128]); g.pk_h = k.sb("pk_h", [128, 256])
    g.pk_v = k.sb("pk_v", [128, 32]); g.pk_i = k.sb("pk_i", [128, 32], U32); g.pk_if = k.sb("pk_if", [128, 32])
    g.pk_w = k.sb("pk_w", [128, 128]); g.pk_ca = k.sb("pk_ca", [128, 256]); g.pk_cb = k.sb("pk_cb", [128, 256])
    g.pk_t8 = k.sb("pk_t8", [128, 48]); g.pk_p8 = k.sb("pk_p8", [128, 32], U32); g.pk_pf = k.sb("pk_pf", [128, 48])
    g.pk_oh = k.sb("pk_oh", [128, 256]); g.pk_sel = k.sb("pk_sel", [128, 48]); g.pk_iota = k.sb("pk_iota", [128, 256])
    prologue(g)
    first = cfg.get("first_layer", 0)
    last = cfg.get("last_layer", DEPTH - 1)
    for i in range(first, last + 1):
        layer(g, i)
    epilogue(g)
    finals = [g.y, g.o_ckv, g.o_kr, g.o_k, g.o_v] + list(g.dbg.values())
    k.finish(finals)
    st.close()
    return nc


def subs(g, parent, n):
    key = ("subs", id(parent), n)
    if key not in g.rr:
        lst = [Buf(f"{parent.name}_s{j}", parent.t) for j in range(n)]
        for b_ in lst:
            b_.alias = [parent]
        parent.alias = list(parent.alias) + lst
        g.rr[key] = lst
    return g.rr[key]


def rot(g, name, bufs):
    i = g.rr.get(name, 0)
    g.rr[name] = i + 1
    return bufs[i % len(bufs)]


def bf(ap):
    return ap.bitcast(BF16)


def ld(g, dst_buf, dst_ap, src_buf, src_ap, q="sp", **kw):
    g.k.dma(q, lambda h: h.dma_start(out=dst_ap, in_=src_ap, **kw), reads=[src_buf], writes=[dst_buf])


def st_(g, dst_buf, dst_ap, src_buf, src_ap, q="pool", **kw):
    g.k.dma(q, lambda h: h.dma_start(out=dst_ap, in_=src_ap, **kw), reads=[src_buf], writes=[dst_buf])


def prologue(g):
    k = g.k
    ld(g, g.identf, g.identf[:], g.c_ident, g.c_ident[:, :])
    k.op("dve", lambda h: h.tensor_copy(out=g.identb[:], in_=g.identf[:]), reads=[g.identf], writes=[g.identb])
    ld(g, g.pk_iota, g.pk_iota[:], g.c_iota, g.c_iota[:, :])
    g.pk_iotab = g.k.sb("pk_iotab", [128, 64])
    cp(g, "dve", g.pk_iotab, bf(g.pk_iotab[:])[:, 0:128], g.pk_iota, g.pk_iota[:, 0:128])
    g.pk_thr = g.pk_iota[:, 128:144]
    tsc(g, "dve", g.pk_iota, g.pk_thr, g.pk_iota, g.pk_iota[:, 0:16], 1.0, 16.0, ALU.add, ALU.mult)
    for r in range(2):
        ld(g, g.csT, g.csT[:, :, r], g.cond, g.cond[r, :].rearrange("(k p) -> p k", p=128),
           allow_slow_non_contiguous=True)
    k.op("act", lambda h: h.activation(out=g.csT[:], in_=g.csT[:], func=AF.Silu), reads=[g.csT], writes=[g.csT])


def epilogue(g):
    pass


def mm(g, pb, out, lb, lhsT, rb, rhs, start=True, stop=True):
    g.k.op("pe", lambda h: h.matmul(out, lhsT=lhsT, rhs=rhs, start=start, stop=stop), reads=[lb, rb], writes=[pb])


def tr(g, pb, out, sb, in_, fp32=True):
    ib = g.identf if fp32 else g.identb
    np_ = in_.shape[0]
    ident = ib[0:np_, 0:np_]
    g.k.op("pe", lambda h: h.transpose(out, in_, ident), reads=[sb, ib], writes=[pb])


def act(g, ob, out, ib, in_, func, bias=None, scale=None, accum=None, rd=(), wr=(), eng="act"):
    kw = {}
    if bias is not None:
        kw["bias"] = bias
    if scale is not None:
        kw["scale"] = scale
    if accum is not None:
        kw["accum_out"] = accum
    g.k.op("act", lambda h: h.activation(out=out, in_=in_, func=func, **kw), reads=[ib, *rd], writes=[ob, *wr])


def tsc(g, eng, ob, out, ib, in0, s1, s2=None, op0=ALU.mult, op1=None, rd=(), wr=(), accum=None):
    kw = {}
    if op1 is not None:
        kw["op1"] = op1
    if accum is not None:
        kw["accum_out"] = accum
    g.k.op(eng, lambda h: h.tensor_scalar(out=out, in0=in0, scalar1=s1, scalar2=s2, op0=op0, **kw),
           reads=[ib, *rd], writes=[ob, *wr])


def tt(g, eng, ob, out, ab, a, bb, b, op):
    g.k.op(eng, lambda h: h.tensor_tensor(out=out, in0=a, in1=b, op=op), reads=[ab, bb], writes=[ob])


def stt(g, ob, out, ab, a, scalar, bb, b, op0, op1, rd=(), accum=None, wr=()):
    kw = {}
    if accum is not None:
        kw["accum_out"] = accum
    g.k.op("dve", lambda h: h.scalar_tensor_tensor(out=out, in0=a, scalar=scalar, in1=b, op0=op0, op1=op1, **kw),
           reads=[ab, bb, *rd], writes=[ob, *wr])


def cp(g, eng, ob, out, ib, in_):
    if eng == "act":
        g.k.op("act", lambda h: h.copy(out=out, in_=in_), reads=[ib], writes=[ob])
    else:
        g.k.op(eng, lambda h: h.tensor_copy(out=out, in_=in_), reads=[ib], writes=[ob])


def rows(t, n=1):
    return slice(t * 128, (t + n) * 128)


def bcast_row(ap_row):
    return ap_row.partition_broadcast(128).rearrange("p o n -> p (o n)")


def ada(g, i):
    k = g.k
    for qt in range(4):
        c0 = qt * 3072
        banks = [(g.psb[s], g.psb[s][0:2, 0:512]) for s in range(5)] + [(g.psA, g.psA[0:2, 0:512])]
        for kc in range(16):
            wb = rot(g, "adaw", [g.B[0], g.B[1], g.B[2]])
            ld(g, wb, wb[:, 0:3072], g.ada_w, g.ada_w[i, kc * 128:(kc + 1) * 128, c0:c0 + 3072])
            for s in range(6):
                pb, out = banks[s]
                mm(g, pb, out, g.csT, g.csT[:, kc, :], wb, wb[:, s * 512:(s + 1) * 512], kc == 0, kc == 15)
        bt = g.B[3]
        for r in range(2):
            ld(g, bt, bt[r:r + 1, 0:3072], g.ada_b, g.ada_b[i:i + 1, c0:c0 + 3072])
        for s in range(6):
            pb, out = banks[s]
            tt(g, "dve", bt, bt[0:2, s * 512:(s + 1) * 512], pb, out, bt, bt[0:2, s * 512:(s + 1) * 512], ALU.add)
        st_(g, g.MOD, g.MOD[2 * i:2 * i + 2, c0:c0 + 3072], bt, bt[0:2, 0:3072])


def load_modT(g, i, which):
    for r in range(2):
        for slot, j in ((0, 3 * which), (1, 3 * which + 1)):
            ld(g, g.modT, g.modT[:, r, slot, :],
               g.MOD, g.MOD[2 * i + r, j * D:(j + 1) * D].rearrange("(k p) -> p k", p=128),
               allow_slow_non_contiguous=True)
    tsc(g, "dve", g.modT, g.modT[:, :, 1, :], g.modT, g.modT[:, :, 1, :], 1.0, None, ALU.add)


def load_bc(g, i, which):
    gj = 3 * which + 2
    for r in range(2):
        ld(g, g.B[0], g.B[0][:, r * D:(r + 1) * D], g.MOD, bcast_row(g.MOD[2 * i + r:2 * i + r + 1, gj * D:(gj + 1) * D]))
    ld(g, g.B[1], g.B[1][:, 0:D], g.ln_g[which], bcast_row(g.ln_g[which][i:i + 1, :]))
    ld(g, g.B[1], g.B[1][:, D:2 * D], g.ln_b[which], bcast_row(g.ln_b[which][i:i + 1, :]))


def emit_ut(g, xb, t, r, dst=None, mod=True):
    dst = dst or g.UT
    stg = rot(g, "utst", [g.T[6], g.T[5]])
    sv = bf(stg[:]).rearrange("p (k n) -> p k n", k=32)
    for q4 in range(4):
        pb = rot(g, "trps", [g.psb[0], g.psb[1]])
        for j in range(4):
            kc = q4 * 4 + j
            tr(g, pb, pb[:, j * 128:(j + 1) * 128], xb, xb[:, kc * 128:(kc + 1) * 128])
        if mod:
            for j in range(4):
                kc = q4 * 4 + j
                act(g, stg, sv[:, kc, :], pb, pb[:, j * 128:(j + 1) * 128], AF.Identity,
                    bias=g.modT[:, r, 0, kc:kc + 1], scale=g.modT[:, r, 1, kc:kc + 1], rd=[g.modT])
        else:
            cp(g, "act", stg, sv[:, q4 * 4:q4 * 4 + 4, :], pb, pb[:, 0:512].rearrange("p (c n) -> p c n", c=4))
    st_(g, dst, dst[:, 0:16, rows(t)], stg, sv[:, 0:16, :], q="act")


def load_w(g, W_buf, W_ap, nK, c0, w):
    wb = rot(g, "wbf", [g.B[4], g.B[5]])
    wv = bf(wb[:])[:, 0:nK * w].rearrange("p (k n) -> p k n", k=nK)
    kstep = 4
    for k0 in range(0, nK, kstep):
        kn = min(kstep, nK - k0)
        ld(g, wb, wv[:, k0:k0 + kn, :], W_buf, W_ap[k0 * 128:(k0 + kn) * 128, c0:c0 + w].rearrange("(k p) n -> p k n", p=128), q="pool")
    return wb, wv


def ut_lhs(g, src, nK=16, bufs=None, kc0=0):
    state = {}
    bufs = bufs or [g.B[6], g.B[7]]

    def fn(t):
        grp = t // 4
        if state.get("grp") != grp:
            ub = rot(g, "ug", bufs)
            uv = bf(ub[:])[:, 0:nK * 512].rearrange("p (k n) -> p k n", k=nK)
            ld(g, ub, uv, src, src[:, kc0:kc0 + nK, grp * 512:(grp + 1) * 512])
            state.update(grp=grp, ub=ub, uv=uv)
        uv = state["uv"]
        j = t % 4
        return state["ub"], (lambda kc: uv[:, kc, j * 128:(j + 1) * 128])
    return fn


def linear_tm(g, lhs_fn, nK, W_buf, W_ap, slices, tiles, consume):
    for si, (c0, w) in enumerate(slices):
        wb, wv = load_w(g, W_buf, W_ap, nK, c0, w)
        for t in tiles:
            lb, lfn = lhs_fn(t)
            pb = rot(g, "linps", [g.psb[2], g.psb[3], g.psb[4]])
            for kc in range(nK):
                mm(g, pb, pb[:, 0:w], lb, lfn(kc), wb, wv[:, kc, :], kc == 0, kc == nK - 1)
            consume(t, si, c0, w, pb)


def pn_consume(g):
    def consume(t, si, c0, w, pb):
        r = 0 if t < 4 else 1
        for h0 in range(0, w, 256):
            hw = min(256, w - h0)
            xs = rot(g, "pnx", [g.sm[0], g.sm[1], g.sm[2]])
            ys = rot(g, "pny", [g.sm[3], g.sm[4], g.sm[5]])
            ld(g, xs, xs[:, 0:hw], g.X, g.X[rows(t), c0 + h0:c0 + h0 + hw])
            tt(g, "dve", ys, ys[:, 0:hw], pb, pb[:, h0:h0 + hw], g.B[0], g.B[0][:, r * D + c0 + h0:r * D + c0 + h0 + hw], ALU.mult)
            stt(g, ys, ys[:, 0:hw], xs, xs[:, 0:hw], ALPHA, ys, ys[:, 0:hw], ALU.mult, ALU.add)
            st_(g, g.Y, g.Y[rows(t), c0 + h0:c0 + h0 + hw], ys, ys[:, 0:hw])
    return consume


def lnt(g, i, which):
    last = (i == DEPTH - 1 and which == 1) or (g.cfg.get("last_layer", DEPTH - 1) == i and which == 1)
    for t in range(NT):
        r = 0 if t < 4 else 1
        yt = rot(g, "lny", [g.T[0], g.T[1]])
        ld(g, yt, yt[:], g.Y, g.Y[rows(t), :])
        s6 = rot(g, "lns", [g.sm[6], g.sm[7]])
        for c in range(4):
            g.k.op("dve", lambda h, c=c, s6=s6, yt=yt: h.bn_stats(out=s6[:, c * 6:(c + 1) * 6], in_=yt[:, c * 512:(c + 1) * 512]),
                   reads=[yt], writes=[s6])
        mv = s6[:, 32:34]
        g.k.op("dve", lambda h, s6=s6, mv=mv: h.bn_aggr(out=mv, in_=s6[:, 0:24]), reads=[s6], writes=[s6])
        rs = s6[:, 40:41]
        tsc(g, "dve", s6, rs, s6, s6[:, 33:34], LN_EPS, None, ALU.add)
        act(g, s6, rs, s6, rs, AF.Sqrt)
        g.k.op("dve", lambda h, rs=rs: h.reciprocal(out=rs, in_=rs), reads=[s6], writes=[s6])
        tsc(g, "dve", yt, yt[:], yt, yt[:], s6[:, 32:33], rs, ALU.subtract, ALU.mult, rd=[s6])
        tt(g, "pool", yt, yt[:], yt, yt[:], g.B[1], g.B[1][:, 0:D], ALU.mult)
        tt(g, "pool", yt, yt[:], yt, yt[:], g.B[1], g.B[1][:, D:2 * D], ALU.add)
        if last:
            st_(g, g.y, g.y[rows(t), :], yt, yt[:])
            continue
        st_(g, g.X, g.X[rows(t), :], yt, yt[:])
        emit_ut(g, yt, t, r)


def layer(g, i):
    cfg = g.cfg
    if i == cfg.get("first_layer", 0):
        ada(g, i)
        load_modT(g, i, 0)
        for t in range(NT):
            r = 0 if t < 4 else 1
            xt = rot(g, "lny", [g.T[0], g.T[1]])
            ld(g, xt, xt[:], g.xin, g.xin[rows(t), :])
            st_(g, g.X, g.X[rows(t), :], xt, xt[:])
            emit_ut(g, xt, t, r)
    g.pre_pn = lambda: load_bc(g, i, 0)
    mixer = cfg.get("mixer", {}).get(i, i % 4)
    if mixer == 0:
        mixer_mla(g)
    elif mixer == 1:
        mixer_gqa(g)
    elif mixer == 2:
        mixer_fnet(g)
    elif mixer == 3:
        mixer_conv(g)
    else:
        g.pre_pn()
        linear_tm(g, ut_lhs(g, g.UT), 16, g.conv_w_out, g.conv_w_out[:, :], [(c, 512) for c in range(0, D, 512)],
                  range(NT), pn_consume(g))
    load_modT(g, i, 1)
    lnt(g, i, 0)
    if cfg.get("stop_after_mixer"):
        return
    if i + 1 < DEPTH:
        ada(g, i + 1)
    peer(g, i)
    load_bc(g, i, 1)
    if i + 1 < DEPTH:
        load_modT(g, i + 1, 0)
    lnt(g, i, 1)


def load_rope_tabs(g):
    tab = g.T[3]
    tv = tab[:, 0:NT * 128].rearrange("p (t c) -> p t c", t=NT)
    ld(g, tab, tv[:, :, 0:64], g.c_rcos, g.c_rcos[:, :].rearrange("(t p) c -> p t c", p=128))
    ld(g, tab, tv[:, :, 64:128], g.c_rsin, g.c_rsin[:, :].rearrange("(t p) c -> p t c", p=128))
    return tab, tv


def rope(g, xb, x3, H, t, tmp_a, tmp_b):
    tab = g.T[3]
    tv = tab[:, 0:NT * 128].rearrange("p (t c) -> p t c", t=NT)
    a3 = tmp_a[:, 0:H * 64].rearrange("p (h c) -> p h c", h=H)
    b3 = tmp_b[:, 0:H * 64].rearrange("p (h c) -> p h c", h=H)
    cosb = tv[:, t, 0:64].unsqueeze(1).to_broadcast([128, H, 64])
    tt(g, "dve", tmp_a, a3, xb, x3, tab, cosb, ALU.mult)
    for qa, qb in ((0, 1), (1, 0), (2, 3), (3, 2)):
        sinb = tv[:, t, 64 + qa * 16:64 + (qa + 1) * 16].unsqueeze(1).to_broadcast([128, H, 16])
        tt(g, "pool" if H > 4 else "dve", tmp_b, b3[:, :, qa * 16:(qa + 1) * 16], xb, x3[:, :, qb * 16:(qb + 1) * 16], tab, sinb, ALU.mult)
    tt(g, "dve", xb, x3, tmp_a, a3, tmp_b, b3, ALU.add)


def tm2ft(g, xb, xap, w, dstb, dst_ap, fp32=True):
    pb = rot(g, "trps", [g.psb[0], g.psb[1]])
    if fp32:
        tr(g, pb, pb[0:w, 0:128], xb, xap, fp32=True)
        cp(g, "act", dstb, dst_ap, pb, pb[0:w, 0:128])
    else:
        pv = bf(pb[:])
        tr(g, pb, pv[0:w, 0:128], xb, xap, fp32=False)
        cp(g, "act", dstb, dst_ap, pb, pv[0:w, 0:128])


def rms_consume(g, norm_ap, dst, ch0, hook=None):
    def consume(t, si, c0, w, pb):
        st6 = rot(g, "lns", [g.sm[6], g.sm[7]])
        junk = rot(g, "rmsj", [g.T[4], g.T[5]])
        act(g, junk, junk[:, 0:512], pb, pb[:, 0:512], AF.Square, accum=st6[:, 0:1], wr=[st6])
        tsc(g, "dve", st6, st6[:, 1:2], st6, st6[:, 0:1], 1.0 / 512.0, RMS_EPS, ALU.mult, ALU.add)
        act(g, st6, st6[:, 1:2], st6, st6[:, 1:2], AF.Sqrt)
        g.k.op("dve", lambda h: h.reciprocal(out=st6[:, 2:3], in_=st6[:, 1:2]), reads=[st6], writes=[st6])
        stt(g, junk, junk[:, 512:1024], pb, pb[:, 0:512], st6[:, 2:3], g.T[2], norm_ap, ALU.mult, ALU.mult, rd=[st6])
        if hook:
            hook(t, junk, junk[:, 512:1024])
        stg = rot(g, "rmst", [g.sm[3], g.sm[4]])
        sv = bf(stg[:]).rearrange("p (k n) -> p k n", k=4)
        pt = rot(g, "trps", [g.psb[0], g.psb[1]])
        for j in range(4):
            tr(g, pt, pt[:, j * 128:(j + 1) * 128], junk, junk[:, 512 + j * 128:512 + (j + 1) * 128])
        cp(g, "act", stg, sv, pt, pt[:, 0:512].rearrange("p (k n) -> p k n", k=4))
        st_(g, dst, dst[:, ch0:ch0 + 4, rows(t)], stg, sv, q="act")
    return consume


def att_flush(g):
    prev = g.rr.pop("att_pending", None)
    if prev is not None:
        for _ in prev:
            pass


def attention(g, *a, **kw):
    gen = _attention_gen(g, *a, **kw)
    next(gen)
    att_flush(g)
    g.rr["att_pending"] = gen


def _attention_gen(g, S, qk_parts, v_fn, nblk, out_fn, scale, bias_fn=None, sink_ap=None, ranges=None, vblocks=None):
    k = g.k
    if S <= 512:
        ps = rot(g, "attps", g.psA3)
        pbase = g.psA3.index(ps) * 512
    else:
        ps = g.psA
        pbase = 0
    pst = g.psA.t
    ranges = ranges or [(0, S)]
    vblocks = list(vblocks) if vblocks is not None else list(range(nblk))
    col = 0
    pieces = []
    for (k0, kw) in ranges:
        while kw > 0:
            room = 512 - (col % 512)
            w_ = min(kw, room)
            pieces.append((col, k0, w_))
            col += w_; k0 += w_; kw -= w_
    assert col == S and len(vblocks) == nblk
    for (c0, k0, cw) in pieces:
        for pi, (lb, lhsT, rb, rfn) in enumerate(qk_parts):
            mm(g, ps, pst[:, pbase + c0:pbase + c0 + cw], lb, lhsT, rb, rfn(k0, cw), pi == 0, pi == len(qk_parts) - 1)
    st6 = rot(g, "lns", [g.sm[6], g.sm[7]])
    aset = g.rr.get("attset", 0) % 2
    g.rr["attset"] = g.rr.get("attset", 0) + 1
    pf = [g.T[4], g.B[4]][aset]
    if bias_fn is not None:
        src_b, src = pf, pf[:, 0:S]
        bias_fn(ps, pst[:, pbase:pbase + S], pf, pf[:, 0:S])
    else:
        src_b, src = ps, pst[:, pbase:pbase + S]
    k.op("dve", lambda h: h.reduce_max(out=st6[:, 0:1], in_=src, axis=AX.X), reads=[src_b], writes=[st6])
    if sink_ap is not None:
        tsc(g, "dve", st6, st6[:, 0:1], st6, st6[:, 0:1], scale, sink_ap[1], ALU.mult, ALU.max, rd=[sink_ap[0]])
        tsc(g, "dve", st6, st6[:, 1:2], st6, st6[:, 0:1], -1.0, None, ALU.mult)
    else:
        tsc(g, "dve", st6, st6[:, 1:2], st6, st6[:, 0:1], -scale, None, ALU.mult)
    act(g, pf, pf[:, 0:S], src_b, src, AF.Exp, bias=st6[:, 1:2], scale=scale, accum=st6[:, 2:3], rd=[st6], wr=[st6])
    if sink_ap is not None:
        act(g, st6, st6[:, 3:4], st6, st6[:, 1:2], AF.Exp, bias=sink_ap[1], scale=1.0, rd=[sink_ap[0]])
        tt(g, "dve", st6, st6[:, 2:3], st6, st6[:, 2:3], st6, st6[:, 3:4], ALU.add)
    k.op("dve", lambda h: h.reciprocal(out=st6[:, 4:5], in_=st6[:, 2:3]), reads=[st6], writes=[st6])
    pn = [g.T[5], g.B[5]][aset]
    pnv = bf(pn[:])
    tsc(g, "dve", pn, pnv[:, 0:S], pf, pf[:, 0:S], st6[:, 4:5], None, ALU.mult, rd=[st6])
    yield
    pT = [g.T[6], g.B[6]][aset]
    pTv = bf(pT[:])[:, 0:nblk * 128].rearrange("p (b n) -> p b n", b=nblk)
    for b0 in range(0, nblk, 8):
        bn = min(8, nblk - b0)
        pb = rot(g, "trps", [g.psb[0], g.psb[1]])
        pv = bf(pb[:])
        for j in range(bn):
            tr(g, pb, pv[:, j * 128:(j + 1) * 128], pn, pnv[:, (b0 + j) * 128:(b0 + j + 1) * 128], fp32=False)
        cp(g, "act", pT, pTv[:, b0:b0 + bn, :], pb, pv[:, 0:bn * 128].rearrange("p (b n) -> p b n", b=bn))
    po = rot(g, "linps", [g.psb[2], g.psb[3], g.psb[4]])
    dv = None
    for bi, blk in enumerate(vblocks):
        vb, vap = v_fn(blk)
        dv = vap.shape[-1]
        mm(g, po, po[0:dv, 0:128], vb, vap, pT, pTv[:, bi, :], bi == 0, bi == nblk - 1)
    out_fn(po, po[0:dv, 0:128])


def load_w_to(g, W_buf, W_ap, nK, w, dstb):
    wv = bf(dstb[:])[:, 0:nK * w].rearrange("p (k n) -> p k n", k=nK)
    for k0 in range(nK):
        ld(g, dstb, wv[:, k0, :], W_buf, W_ap[k0 * 128:(k0 + 1) * 128, :], q="pool")
    return wv


def mixer_mla(g):
    k = g.k
    MLA_SCALE = 192 ** -0.5
    FT2 = g.FT2
    nrm = g.T[2]
    ld(g, nrm, nrm[:, 0:512], g.mla_q_norm, bcast_row(g.mla_q_norm[0:1, :]))
    ld(g, nrm, nrm[:, 512:1024], g.mla_kv_norm, bcast_row(g.mla_kv_norm[0:1, :]))
    load_rope_tabs(g)
    linear_tm(g, ut_lhs(g, g.UT), 16, g.mla_w_dq, g.mla_w_dq[:, :], [(0, 512)], range(NT),
              rms_consume(g, nrm[:, 0:512], FT2, 0))
    def ckv_hook(t, b, ap):
        if t < 4:
            st_(g, g.o_ckv, g.o_ckv[rows(t), :], b, ap)
    rc = rms_consume(g, nrm[:, 512:1024], FT2, 4, hook=ckv_hook)

    def kr_to_ft(t_col, kb, kap):
        stg = rot(g, "rmst", [g.sm[3], g.sm[4]])
        sv = bf(stg[:])
        tm2ft(g, kb, kap, 64, stg, sv[0:64, 0:128])
        st_(g, FT2, FT2[0:64, 8, t_col * 128:(t_col + 1) * 128], stg, sv[0:64, 0:128], q="act")

    def kv_consume(t, si, c0, w, pb):
        if si == 0:
            return rc(t, si, c0, w, pb)
        kb = rot(g, "pnx", [g.sm[0], g.sm[1], g.sm[2]])
        cp(g, "act", kb, kb[:, 0:64], pb, pb[:, 0:64])
        if t < 4:
            st_(g, g.o_kr, g.o_kr[rows(t), :], kb, kb[:, 0:64])
        rope(g, kb, kb[:, 0:64].rearrange("p (h c) -> p h c", h=1), 1, t, g.sm[5], g.pk_ca)
        kr_to_ft(t, kb, kb[:, 0:64])
    linear_tm(g, ut_lhs(g, g.UT), 16, g.mla_w_dkv, g.mla_w_dkv[:, :], [(0, 512), (512, 64)], range(NT), kv_consume)
    for cb in range(2):
        ct = rot(g, "rmsj", [g.T[4], g.T[5]])
        ld(g, ct, ct[:, 0:512], g.l0ckv, g.l0ckv[rows(cb), :])
        stg = rot(g, "rmst", [g.sm[3], g.sm[4]])
        sv = bf(stg[:]).rearrange("p (k n) -> p k n", k=4)
        pt = rot(g, "trps", [g.psb[0], g.psb[1]])
        for j in range(4):
            tr(g, pt, pt[:, j * 128:(j + 1) * 128], ct, ct[:, j * 128:(j + 1) * 128])
        cp(g, "act", stg, sv, pt, pt[:, 0:512].rearrange("p (k n) -> p k n", k=4))
        st_(g, FT2, FT2[:, 4:8, NTOK + cb * 128:NTOK + (cb + 1) * 128], stg, sv)
        kb = rot(g, "pnx", [g.sm[0], g.sm[1], g.sm[2]])
        ld(g, kb, kb[:, 0:64], g.l0kr, g.l0kr[rows(cb), :])
        kr_to_ft(NT + cb, kb, kb[:, 0:64])
    def q_consume(t, si, c0, w, pb):
        qb = rot(g, "mlaq", [g.T[0], g.T[1]])
        cp(g, "act", qb, qb[:, 0:384], pb, pb[:, 0:384])
        x3 = qb[:, 0:384].rearrange("p (h c) -> p h c", h=2)[:, :, 128:192]
        rope(g, qb, x3, 2, t, g.sm[5], g.pk_ca)
        st_(g, g.SC2, g.SC2[rows(t), c0:c0 + 384], qb, qb[:, 0:384])
    linear_tm(g, ut_lhs(g, FT2, 4), 4, g.mla_w_uq, g.mla_w_uq[:, :], [(c, 384) for c in range(0, 3072, 384)], range(NT), q_consume)
    wuk = load_w_to(g, g.mla_w_uk, g.mla_w_uk[:, :], 4, D, g.B[2])
    wuv = load_w_to(g, g.mla_w_uv, g.mla_w_uv[:, :], 4, D, g.B[3])
    for (t0, nt, kind) in SEQS:
        S = nt * 128 + (256 if kind == "s" else 0)
        nblk = S // 128
        cb_ = g.B[0]
        ckvT = bf(cb_[:])[:, 0:4 * S].rearrange("p (k n) -> p k n", k=4)
        ld(g, cb_, ckvT[:, :, 0:nt * 128], FT2, FT2[:, 4:8, t0 * 128:(t0 + nt) * 128])
        kb_ = g.B[1]
        krT = bf(kb_[:])[0:64, 0:S]
        ld(g, kb_, krT[:, 0:nt * 128], FT2, FT2[0:64, 8, t0 * 128:(t0 + nt) * 128])
        if kind == "s":
            ld(g, cb_, ckvT[:, :, nt * 128:S], FT2, FT2[:, 4:8, NTOK:NTOK + 256])
            ld(g, kb_, krT[:, nt * 128:S], FT2, FT2[0:64, 8, NTOK:NTOK + 256])
        for h in range(16):
            att_flush(g)
            kTb = rot(g, "mlakT", [g.T[0], g.B[7]])
            kTh = bf(kTb[:])[:, 0:S]
            for c0 in range(0, S, 512):
                cw = min(512, S - c0)
                pb = rot(g, "linps", [g.psb[2], g.psb[3], g.psb[4]])
                for kc in range(4):
                    mm(g, pb, pb[:, 0:cw], g.B[2], wuk[:, kc, h * 128:(h + 1) * 128], cb_, ckvT[:, kc, c0:c0 + cw], kc == 0, kc == 3)
                cp(g, "act", kTb, kTh[:, c0:c0 + cw], pb, pb[:, 0:cw])
            vb_ = g.T[1]
            vh = bf(vb_[:])[:, 0:nblk * 128].rearrange("p (b n) -> p b n", b=nblk)
            for b0 in range(0, nblk, 4):
                bn = min(4, nblk - b0)
                pb = rot(g, "linps", [g.psb[2], g.psb[3], g.psb[4]])
                for j in range(bn):
                    for kc in range(4):
                        mm(g, pb, pb[:, j * 128:(j + 1) * 128], cb_, ckvT[:, kc, (b0 + j) * 128:(b0 + j + 1) * 128],
                           g.B[3], wuv[:, kc, h * 128:(h + 1) * 128], kc == 0, kc == 3)
                cp(g, "act", vb_, vh[:, b0:b0 + bn, :], pb, pb[:, 0:bn * 128].rearrange("p (b n) -> p b n", b=bn))
            for tq in range(nt):
                t = t0 + tq
                qs = rot(g, "pnx", [g.sm[0], g.sm[1], g.sm[2]])
                ld(g, qs, qs[:, 0:192], g.SC2, g.SC2[rows(t), h * 192:(h + 1) * 192])
                qT = rot(g, "mqT", [g.pk_cb, g.pk_oh])
                qTv = bf(qT[:])
                tm2ft(g, qs, qs[:, 0:128], 128, qT, qTv[:, 0:128])
                tm2ft(g, qs, qs[:, 128:192], 64, qT, qTv[0:64, 128:256])

                def out_fn(po, oap, h=h, t=t):
                    ob = rot(g, "rmst", [g.sm[3], g.sm[4]])
                    ov = bf(ob[:])[:, 0:128]
                    cp(g, "act", ob, ov, po, oap)
                    st_(g, g.FT, g.FT[:, h, rows(t)], ob, ov, q="act")
                attention(g, S,
                          [(qT, qTv[:, 0:128], kTb, lambda c0, cw, kTh=kTh: kTh[:, c0:c0 + cw]),
                           (qT, qTv[0:64, 128:256], kb_, lambda c0, cw, krT=krT: krT[:, c0:c0 + cw])],
                          lambda blk, vb_=vb_, vh=vh: (vb_, vh[:, blk, :]), nblk, out_fn, MLA_SCALE)
    att_flush(g)
    g.pre_pn()
    linear_tm(g, ut_lhs(g, g.FT), 16, g.mla_w_o, g.mla_w_o[:, :], [(c, 512) for c in range(0, D, 512)],
              range(NT), pn_consume(g))


def mixer_gqa(g):
    k = g.k
    GQA_SCALE = 64 ** -0.5
    FT2 = g.FT2
    VB = g.VB
    load_rope_tabs(g)
    sinkb = g.pk_w
    ld(g, sinkb, sinkb[:, 0:32], g.gqa_sink, bcast_row(g.gqa_sink[0:1, :]))

    def k_to_ft(col_tile, kb, kap512):
        stg = rot(g, "rmst", [g.sm[3], g.sm[4]])
        sv = bf(stg[:])[0:64, 0:512].rearrange("p (h n) -> p h n", h=4)
        for h4 in range(2):
            pt = rot(g, "trps", [g.psb[0], g.psb[1]])
            for j in range(4):
                tr(g, pt, pt[0:64, j * 128:(j + 1) * 128], kb, kap512[:, (h4 * 4 + j) * 64:(h4 * 4 + j + 1) * 64])
            stg = rot(g, "rmst", [g.sm[3], g.sm[4]])
            sv = bf(stg[:])[0:64, 0:512].rearrange("p (h n) -> p h n", h=4)
            cp(g, "act", stg, sv, pt, pt[0:64, 0:512].rearrange("p (h n) -> p h n", h=4))
            st_(g, FT2, FT2[0:64, h4 * 4:h4 * 4 + 4, col_tile * 128:(col_tile + 1) * 128], stg, sv, q="act")

    def v_to_vb(row_tile, vb_, vap512):
        stg = rot(g, "rmst", [g.sm[3], g.sm[4]])
        sv = bf(stg[:])[:, 0:512]
        cp(g, "dve", stg, sv, vb_, vap512)
        st_(g, VB, VB[row_tile * 128:(row_tile + 1) * 128, :], stg, sv)

    def consume(t, si, c0, w, pb):
        qb = rot(g, "mlaq", [g.T[0], g.T[1]])
        cp(g, "act", qb, qb[:, 0:512], pb, pb[:, 0:512])
        if si == 4 and t < 4:
            st_(g, g.o_k, g.o_k[rows(t), :], qb, qb[:, 0:512])
        if si == 5:
            if t < 4:
                st_(g, g.o_v, g.o_v[rows(t), :], qb, qb[:, 0:512])
            v_to_vb(t, qb, qb[:, 0:512])
            return
        rope(g, qb, qb[:, 0:512].rearrange("p (h c) -> p h c", h=8), 8, t, g.T[4], g.T[5])
        if si < 4:
            st_(g, g.SC2, g.SC2[rows(t), c0:c0 + 512], qb, qb[:, 0:512])
        else:
            k_to_ft(t, qb, qb[:, 0:512])
    linear_tm(g, ut_lhs(g, g.UT), 16, g.gqa_w_qkv, g.gqa_w_qkv[:, :], [(c, 512) for c in range(0, 3072, 512)], range(NT), consume)
    for cb in range(2):
        kb = rot(g, "mlaq", [g.T[0], g.T[1]])
        ld(g, kb, kb[:, 0:512], g.l1k, g.l1k[rows(cb), :])
        k_to_ft(NT + cb, kb, kb[:, 0:512])
        vb_ = rot(g, "mlaq", [g.T[0], g.T[1]])
        ld(g, vb_, vb_[:, 0:512], g.l1v, g.l1v[rows(cb), :])
        v_to_vb(NT + cb, vb_, vb_[:, 0:512])
    mk = g.B[0]
    mkv = mk[:, 0:3 * 640].rearrange("p (m n) -> p m n", m=3)
    k.op("pool", lambda h: h.memset(mk[:, 0:3 * 640], 0.0), reads=[], writes=[mk])
    ld(g, mk, mkv[:, 0, 128:256], g.c_mnext, g.c_mnext[:, :])
    ld(g, mk, mkv[:, 1, 0:128], g.c_mprev, g.c_mprev[:, :])
    ld(g, mk, mkv[:, 1, 256:384], g.c_mnext, g.c_mnext[:, :])
    ld(g, mk, mkv[:, 2, 0:128], g.c_mprev, g.c_mprev[:, :])
    for (t0, nt, kind) in SEQS:
        Sall = nt * 128 + (256 if kind == "s" else 0)
        nball = Sall // 128
        for kvh in range(8):
            att_flush(g)
            kTb = rot(g, "gqakT", [g.T[0], g.T[2]])
            kT = bf(kTb[:])[0:64, 0:Sall]
            ld(g, kTb, kT[:, 0:nt * 128], FT2, FT2[0:64, kvh, t0 * 128:(t0 + nt) * 128])
            vb_ = rot(g, "gqavv", [g.T[1], g.B[7]])
            vv = bf(vb_[:])[:, 0:nball * 64].rearrange("p (b n) -> p b n", b=nball)
            ld(g, vb_, vv[:, 0:nt, :], VB, VB[t0 * 128:(t0 + nt) * 128, kvh * 64:(kvh + 1) * 64].rearrange("(b p) n -> p b n", p=128))
            if kind == "s":
                ld(g, kTb, kT[:, nt * 128:Sall], FT2, FT2[0:64, kvh, NTOK:NTOK + 256])
                ld(g, vb_, vv[:, nt:nball, :], VB, VB[NTOK:NTOK + 256, kvh * 64:(kvh + 1) * 64].rearrange("(b p) n -> p b n", p=128))
            for hi in range(4):
                h = kvh * 4 + hi
                for tq in range(nt):
                    t = t0 + tq
                    qs = rot(g, "pnx", [g.sm[0], g.sm[1], g.sm[2]])
                    ld(g, qs, qs[:, 0:64], g.SC2, g.SC2[rows(t), h * 64:(h + 1) * 64])
                    qT = rot(g, "mqT", [g.pk_cb, g.pk_oh])
                    qTv = bf(qT[:])
                    tm2ft(g, qs, qs[:, 0:64], 64, qT, qTv[0:64, 0:128])
                    if kind == "p":
                        ranges = [(0, Sall)]; vblocks = list(range(nball)); bias_fn = None
                    else:
                        lo, hi_ = max(0, tq - 1), min(nt - 1, tq + 1)
                        ranges = [(lo * 128, (hi_ - lo + 1) * 128), (nt * 128, 256)]
                        vblocks = list(range(lo, hi_ + 1)) + [nt, nt + 1]
                        mi = 0 if tq == 0 else (2 if tq == nt - 1 else 1)
                        def bias_fn(psb_, psap, sbb, sbap, mi=mi):
                            n = sbap.shape[-1]
                            tt(g, "dve", sbb, sbap, psb_, psap, mk, mkv[:, mi, 0:n], ALU.add)
                    Sq = sum(w_ for _, w_ in ranges)

                    def out_fn(po, oap, h=h, t=t):
                        ob = rot(g, "rmst", [g.sm[3], g.sm[4]])
                        ov = bf(ob[:])[0:64, 0:128]
                        cp(g, "act", ob, ov, po, oap)
                        st_(g, g.FT, g.FT[(h % 2) * 64:(h % 2 + 1) * 64, h // 2, rows(t)], ob, ov, q="act")
                    attention(g, Sq, [(qT, qTv[0:64, 0:128], kTb, lambda k0, kw, kT=kT: kT[:, k0:k0 + kw])],
                              lambda blk, vb_=vb_, vv=vv: (vb_, vv[:, blk, :]), len(vblocks), out_fn, GQA_SCALE,
                              bias_fn=bias_fn, sink_ap=(sinkb, sinkb[:, h:h + 1]), ranges=ranges, vblocks=vblocks)
    att_flush(g)
    g.pre_pn()
    linear_tm(g, ut_lhs(g, g.FT), 16, g.gqa_w_o, g.gqa_w_o[:, :], [(c, 512) for c in range(0, D, 512)],
              range(NT), pn_consume(g))


def store_consume(g, dst, col0=0, dt_bf=False):
    def consume(t, si, c0, w, pb):
        for h0 in range(0, w, 256):
            hw = min(256, w - h0)
            ys = rot(g, "pny", [g.sm[3], g.sm[4], g.sm[5]])
            if dt_bf:
                yv = bf(ys[:])[:, 0:hw]
            else:
                yv = ys[:, 0:hw]
            cp(g, "act", ys, yv, pb, pb[:, h0:h0 + hw])
            st_(g, dst, dst[rows(t), col0 + c0 + h0:col0 + c0 + h0 + hw], ys, yv, q="act")
    return consume


def mixer_fnet(g):
    k = g.k
    PQ = g.SC2.t.bitcast(BF16)
    PQb = g.SC2
    for gi in range(4):
        for wi, W in enumerate((g.c_cc, g.c_sc)):
            def consume(t, si, c0, w, pb, gi=gi, wi=wi):
                for h0 in (0, 256):
                    ys = rot(g, "pny", [g.sm[3], g.sm[4], g.sm[5]])
                    yv = bf(ys[:])[:, 0:256]
                    cp(g, "act", ys, yv, pb, pb[:, h0:h0 + 256])
                    c = wi * D + gi * 512 + h0
                    st_(g, PQb, PQ[rows(t), c:c + 256], ys, yv, q="act")
            linear_tm(g, ut_lhs(g, g.UT, 4, kc0=4 * gi), 4, W, W[:, :], [(0, 512)], range(NT), consume)
    for (t0, nt, kind) in SEQS:
        T = nt * 128
        mats = []
        for mi, M in enumerate((g.c_ct[T], g.c_st[T])):
            mb = g.B[4 + mi]
            mv = bf(mb[:])[:, 0:nt * T].rearrange("p (k n) -> p k n", k=nt)
            for k0 in range(0, nt, 2):
                stg = rot(g, "wst", [g.T[4], g.T[3]])
                sv = stg[:, 0:2 * T].rearrange("p (k n) -> p k n", k=2)
                ld(g, stg, sv, M, M[k0 * 128:(k0 + 2) * 128, :].rearrange("(k p) n -> p k n", p=128))
                cp(g, "pool", mb, mv[:, k0:k0 + 2, :], stg, sv)
            mats.append((mb, mv))
        pq = []
        for wi in range(2):
            lst = []
            for k0 in range(0, nt, 4):
                kn = min(4, nt - k0)
                pb_ = g.B[wi * 2 + k0 // 4]
                pv = bf(pb_[:])[:, 0:kn * D].rearrange("p (k n) -> p k n", k=kn)
                ld(g, pb_, pv, PQb, PQ[(t0 + k0) * 128:(t0 + k0 + kn) * 128, wi * D:(wi + 1) * D].rearrange("(k p) n -> p k n", p=128))
                lst.append((pb_, pv))
            pq.append(lst)
        for tq in range(nt):
            ft = rot(g, "lny", [g.T[0], g.T[1]])
            for sl in range(4):
                pb = rot(g, "linps", [g.psb[2], g.psb[3], g.psb[4]])
                n = 0
                for wi in range(2):
                    mb, mv = mats[wi]
                    for tk in range(nt):
                        xb_, xv = pq[wi][tk // 4]
                        mm(g, pb, pb[:, 0:512], mb, mv[:, tk, tq * 128:(tq + 1) * 128], xb_, xv[:, tk % 4, sl * 512:(sl + 1) * 512],
                           n == 0, n == 2 * nt - 1)
                        n += 1
                cp(g, "act", ft, ft[:, sl * 512:(sl + 1) * 512], pb, pb[:, 0:512])
            emit_ut(g, ft, t0 + tq, 0, dst=g.FT, mod=False)
    g.pre_pn()
    linear_tm(g, ut_lhs(g, g.FT), 16, g.fnet_w_out, g.fnet_w_out[:, :], [(c, 512) for c in range(0, D, 512)],
              range(NT), pn_consume(g))


def mixer_conv(g):
    k = g.k
    BCH = g.SC2
    linear_tm(g, ut_lhs(g, g.UT), 16, g.conv_w_in, g.conv_w_in[:, :], [(c, 512) for c in range(0, 3 * D, 512)],
              range(NT), store_consume(g, BCH))
    Z = g.Zp
    zt = g.T[2]
    k.op("pool", lambda h: h.memset(zt[:], 0.0), reads=[], writes=[zt])
    for si, (t0, nt, kind) in enumerate(SEQS):
        for rr_ in (t0 * 128 + si, (t0 + nt) * 128 + si + 1):
            st_(g, Z, Z[rr_:rr_ + 1, :], zt, zt[0:1, :])
    for j in range(3):
        bb = g.B[2 + j // 2]
        ld(g, bb, bb[:, (j % 2) * D:(j % 2 + 1) * D], g.conv_w, bcast_row(g.conv_w[j:j + 1, :]))
    ld(g, g.B[3], g.B[3][:, D:2 * D], g.conv_b, bcast_row(g.conv_b[0:1, :]))
    for si, (t0, nt, kind) in enumerate(SEQS):
        for t in range(t0, t0 + nt):
            ct = rot(g, "cvc", [g.T[2], g.T[3]])
            ht = rot(g, "lny", [g.T[0], g.T[1]])
            ld(g, ct, ct[:], BCH, BCH[rows(t), D:2 * D])
            ld(g, ht, ht[:], BCH, BCH[rows(t), 2 * D:3 * D])
            tt(g, "dve", ct, ct[:], ct, ct[:], ht, ht[:], ALU.mult)
            st_(g, Z, Z[t * 128 + si + 1:t * 128 + si + 129, :], ct, ct[:])
    for si, (t0, nt, kind) in enumerate(SEQS):
        for t in range(t0, t0 + nt):
            acc = rot(g, "lny", [g.T[0], g.T[1]])
            base = t * 128 + si + 1
            for j in range(3):
                zt_ = rot(g, "cvc", [g.T[2], g.T[3]])
                ld(g, zt_, zt_[:], Z, Z[base + j - 1:base + j - 1 + 128, :])
                wbc = g.B[2 + j // 2][:, (j % 2) * D:(j % 2 + 1) * D]
                if j == 0:
                    tt(g, "dve", acc, acc[:], zt_, zt_[:], g.B[2], wbc, ALU.mult)
                else:
                    tt(g, "pool", zt_, zt_[:], zt_, zt_[:], g.B[2 + j // 2], wbc, ALU.mult)
                    tt(g, "dve", acc, acc[:], acc, acc[:], zt_, zt_[:], ALU.add)
            tt(g, "dve", acc, acc[:], acc, acc[:], g.B[3], g.B[3][:, D:2 * D], ALU.add)
            bt = rot(g, "cvc", [g.T[2], g.T[3]])
            ld(g, bt, bt[:], BCH, BCH[rows(t), 0:D])
            tt(g, "dve", acc, acc[:], acc, acc[:], bt, bt[:], ALU.mult)
            emit_ut(g, acc, t, 0, dst=g.FT, mod=False)
    g.pre_pn()
    linear_tm(g, ut_lhs(g, g.FT), 16, g.conv_w_out, g.conv_w_out[:, :], [(c, 512) for c in range(0, D, 512)],
              range(NT), pn_consume(g))


def peer(g, i):
    k = g.k
    nexp = 16384
    keyT = g.T[6]
    kT = keyT[:].rearrange("p (c n) -> p c n", c=16)
    for hc4 in range(4):
        raw = rot(g, "lny", [g.T[0], g.T[1]])
        rv = raw[:, 0:512].rearrange("p (c n) -> p c n", c=4)
        ld(g, raw, rv, g.peer_keys,
           g.peer_keys[(i * 16 + hc4 * 4) * 128:(i * 16 + hc4 * 4 + 4) * 128, :].rearrange("(c p) n -> p c n", p=128))
        pb = rot(g, "trps", [g.psb[0], g.psb[1]])
        for j in range(4):
            tr(g, pb, pb[:, j * 128:(j + 1) * 128], raw, rv[:, j, :])
        cp(g, "act", keyT, kT[:, hc4 * 4:hc4 * 4 + 4, :], pb, pb[:, 0:512].rearrange("p (c n) -> p c n", c=4))
    Wq = g.peer_w_q
    for sl in range(4):
        wb, wv = load_w(g, Wq, Wq[i, :, :], 16, sl * 512, 512)
        for grp in range(3):
            ub = rot(g, "ug", [g.B[6], g.B[7]])
            uv = bf(ub[:])[:, 0:16 * 512].rearrange("p (k n) -> p k n", k=16)
            ld(g, ub, uv, g.UT, g.UT[:, :, grp * 512:(grp + 1) * 512])
            q4 = g.T[3]
            q4v = q4[:].rearrange("p (c n) -> p c n", c=4)
            for j in range(4):
                pb = rot(g, "linps", [g.psb[2], g.psb[3], g.psb[4]])
                for kc in range(16):
                    mm(g, pb, pb[:, 0:512], wb, wv[:, kc, j * 128:(j + 1) * 128], ub, uv[:, kc, :], kc == 0, kc == 15)
                cp(g, "act", q4, q4v[:, j, :], pb, pb[:, 0:512])
            for tt_ in range(4):
                t = grp * 4 + tt_
                pb = rot(g, "trps", [g.psb[0], g.psb[1]])
                for j in range(4):
                    mm(g, pb, pb[:, j * 128:(j + 1) * 128], q4, q4v[:, j, tt_ * 128:(tt_ + 1) * 128],
                       keyT, kT[:, sl * 4 + j, :])
                ss = rot(g, "pss", [g.sm[0], g.sm[1]])
                sb2 = rot(g, "pss2", [g.sm[2], g.sm[3]])
                cp(g, "dve", ss, ss[:, 0:256], pb, pb[:, 0:256])
                cp(g, "dve", sb2, sb2[:, 0:256], pb, pb[:, 256:512])
                st_(g, g.SC, g.SC[rows(t), sl * 512:sl * 512 + 256], ss, ss[:, 0:256])
                st_(g, g.SC, g.SC[rows(t), sl * 512 + 256:sl * 512 + 512], sb2, sb2[:, 0:256])
    IDX = g.pk_idx; GATE = g.pk_gate; HB = g.pk_h
    V = g.pk_v; Vv = V[:, 0:32].rearrange("p (c n) -> p c n", c=2)
    I = g.pk_i; Iv = I[:, 0:32].rearrange("p (c n) -> p c n", c=2)
    IF = g.pk_if; IFv = IF[:, 0:32].rearrange("p (c n) -> p c n", c=2)
    W = g.pk_w; CA = g.pk_ca; CB = g.pk_cb; T8 = g.pk_t8; P8 = g.pk_p8; PF = g.pk_pf; OH = g.pk_oh; SEL = g.pk_sel
    dve = lambda fn, rd, wr: k.op("dve", fn, reads=rd, writes=wr)
    iota16 = g.pk_iota[:, 0:16]
    def topk_gen(t):
            r = 0 if t < 4 else 1
            S = g.T[5]
            ld(g, S, S[:], g.SC, g.SC[rows(t), :])
            Vall = g.pk_ca; V4 = Vall[:, 0:256].rearrange("p (h c k) -> p h c k", h=8, c=2)
            Iall = g.pk_iall; I4 = Iall[:, 0:256].rearrange("p (h c k) -> p h c k", h=8, c=2)
            IFall = g.pk_cb; IF4 = IFall[:, 0:256].rearrange("p (h c k) -> p h c k", h=8, c=2)
            CANDb = g.B[2]
            CAND = CANDb[:, 0:2048].rearrange("p (h n) -> p h n", h=8)
            CAND2 = CANDb[:, 2048:4096].rearrange("p (h n) -> p h n", h=8)
            T8a = g.pk_oh; T8v = T8a[:, 0:128].rearrange("p (h k) -> p h k", h=8)
            PFv = T8a[:, 128:256]
            P8a = g.pk_p8all; P8v = P8a[:, 0:128].rearrange("p (h k) -> p h k", h=8)
            AB = g.pk_h
            for h in range(8):
                for c in range(2):
                    s = S[:, (2 * h + c) * 128:(2 * h + c + 1) * 128]
                    dve(lambda e, s=s, h=h, c=c: e.max(out=V4[:, h, c, 0:8], in_=s), [S], [Vall])
                    dve(lambda e, s=s, h=h, c=c: e.max_index(out=I4[:, h, c, 0:8], in_max=V4[:, h, c, 0:8], in_values=s), [S, Vall], [Iall])
                    dve(lambda e, s=s, h=h, c=c: e.match_replace(out=W[:, 0:128], in_to_replace=V4[:, h, c, 0:8], in_values=s, imm_value=NEG), [S, Vall], [W])
                    dve(lambda e, h=h, c=c: e.max(out=V4[:, h, c, 8:16], in_=W[:, 0:128]), [W], [Vall])
                    dve(lambda e, h=h, c=c: e.max_index(out=I4[:, h, c, 8:16], in_max=V4[:, h, c, 8:16], in_values=W[:, 0:128]), [W, Vall], [Iall])
                    yield
            cp(g, "dve", IFall, IFall[:, 0:256], Iall, Iall[:, 0:256])
            tt(g, "dve", CANDb, CANDb[:, 0:2048].rearrange("p (h a b) -> p h a b", h=8, a=16),
               Vall, V4[:, :, 0, :].unsqueeze(3).to_broadcast([128, 8, 16, 16]),
               Vall, V4[:, :, 1, :].unsqueeze(2).to_broadcast([128, 8, 16, 16]), ALU.add)
            for h in range(8):
                dve(lambda e, h=h: e.max(out=T8v[:, h, 0:8], in_=CAND[:, h, :]), [CANDb], [T8a])
                dve(lambda e, h=h: e.max_index(out=P8v[:, h, 0:8], in_max=T8v[:, h, 0:8], in_values=CAND[:, h, :]), [CANDb, T8a], [P8a])
                dve(lambda e, h=h: e.match_replace(out=CAND2[:, h, :], in_to_replace=T8v[:, h, 0:8], in_values=CAND[:, h, :], imm_value=NEG), [CANDb, T8a], [CANDb])
                dve(lambda e, h=h: e.max(out=T8v[:, h, 8:16], in_=CAND2[:, h, :]), [CANDb], [T8a])
                dve(lambda e, h=h: e.max_index(out=P8v[:, h, 8:16], in_max=T8v[:, h, 8:16], in_values=CAND2[:, h, :]), [CANDb, T8a], [P8a])
                yield
            cp(g, "dve", T8a, PFv, P8a, P8a[:, 0:128])
            GEb = g.B[3]
            GE = GEb[:, 0:2048].rearrange("p (j m) -> p j m", j=128)
            tt(g, "dve", GEb, GE, T8a, PFv.unsqueeze(2).to_broadcast([128, 128, 16]),
               g.pk_iota, g.pk_thr[:, 0:16].unsqueeze(1).to_broadcast([128, 128, 16]), ALU.is_ge)
            dve(lambda e: e.tensor_reduce(out=AB[:, 0:128], in_=GE, axis=AX.X, op=ALU.add), [GEb], [AB])
            yield
            stt(g, AB, AB[:, 128:256], AB, AB[:, 0:128], -16.0, T8a, PFv, ALU.mult, ALU.add)
            for side, dstb in ((0, g.pk_i1f), (1, g.pk_i2f)):
                tt(g, "dve", GEb, GE, AB, AB[:, side * 128:(side + 1) * 128].unsqueeze(2).to_broadcast([128, 128, 16]),
                   g.pk_iota, iota16.unsqueeze(1).to_broadcast([128, 128, 16]), ALU.is_equal)
                GE4 = GEb[:, 0:2048].rearrange("p (h k m) -> p h k m", h=8, k=16)
                tt(g, "dve", GEb, GE4, GEb, GE4, IFall, IF4[:, :, side, :].unsqueeze(2).to_broadcast([128, 8, 16, 16]), ALU.mult)
                dve(lambda e, dstb=dstb: e.tensor_reduce(out=dstb[:, :], in_=GE, axis=AX.X, op=ALU.add), [GEb], [dstb])
            tt(g, "dve", GATE, GATE[:, 0:128].rearrange("p (h k) -> p h k", h=8), T8a, T8v,
               T8a, T8v[:, :, 0:1].to_broadcast([128, 8, 16]), ALU.subtract)
            act(g, GATE, GATE[:, 0:128], GATE, GATE[:, 0:128], AF.Exp)
            dve(lambda e: e.tensor_reduce(out=SEL[:, 0:8], in_=GATE[:, 0:128].rearrange("p (h k) -> p h k", h=8), axis=AX.X, op=ALU.add), [GATE], [SEL])
            dve(lambda e: e.reciprocal(out=SEL[:, 8:16], in_=SEL[:, 0:8]), [SEL], [SEL])
            tt(g, "dve", GATE, GATE[:, 0:128].rearrange("p (h k) -> p h k", h=8), GATE, GATE[:, 0:128].rearrange("p (h k) -> p h k", h=8),
               SEL, SEL[:, 8:16].unsqueeze(2).to_broadcast([128, 8, 16]), ALU.mult)
            yield
    def g_form(t, filler):
            r = 0 if t < 4 else 1
            trb = g.T[4]
            trv = trb[:, 0:384].rearrange("p (a n) -> p a n", a=3)
            pt = rot(g, "trps", [g.psb[0], g.psb[1]])
            tr(g, pt, pt[:, 0:128], g.pk_i1f, g.pk_i1f[:, :])
            tr(g, pt, pt[:, 128:256], g.pk_i2f, g.pk_i2f[:, :])
            tr(g, pt, pt[:, 256:384], GATE, GATE[:, :])
            trv = bf(trb[:])[:, 0:384].rearrange("p (a n) -> p a n", a=3)
            cp(g, "act", trb, trv, pt, pt[:, 0:384].rearrange("p (a n) -> p a n", a=3))
            stA = g.B[0]; stB = g.B[1]
            sA = bf(stA[:]).rearrange("p (c n) -> p c n", c=64)
            sB = bf(stB[:]).rearrange("p (c n) -> p c n", c=64)
            iota128 = bf(g.pk_iotab[:])[:, 0:128]
            NB = 32
            for nb0 in range(0, 128, NB):
                o1 = rot(g, "oh1", [g.T[0], g.T[1]])
                o2 = rot(g, "oh2", [g.T[2], g.T[3]])
                o1v = bf(o1[:]).rearrange("p (n c) -> p n c", n=NB)
                o2v = bf(o2[:]).rearrange("p (n c) -> p n c", n=NB)
                iob = iota128.unsqueeze(1).to_broadcast([128, NB, 128])
                tt(g, "dve", o1, o1v, g.pk_iotab, iob, trb, trv[:, 0, nb0:nb0 + NB].unsqueeze(2).to_broadcast([128, NB, 128]), ALU.is_equal)
                tt(g, "dve", o1, o1v, o1, o1v, trb, trv[:, 2, nb0:nb0 + NB].unsqueeze(2).to_broadcast([128, NB, 128]), ALU.mult)
                tt(g, "dve", o2, o2v, g.pk_iotab, iob, trb, trv[:, 1, nb0:nb0 + NB].unsqueeze(2).to_broadcast([128, NB, 128]), ALU.is_equal)
                for n0 in range(nb0, nb0 + NB, 4):
                    pg = rot(g, "linps", [g.psb[2], g.psb[3], g.psb[4]])
                    for j in range(4):
                        n = n0 + j
                        mm(g, pg, pg[:, j * 128:(j + 1) * 128], o1, o1v[:, n - nb0, :], o2, o2v[:, n - nb0, :])
                    pin = pg[:, 0:512].rearrange("p (n c) -> p n c", n=4)
                    cp(g, "act", stA, sA.rearrange("p c n -> p n c")[:, n0:n0 + 4, :], pg, pin[:, :, 0:64])
                    cp(g, "act", stB, sB.rearrange("p c n -> p n c")[:, n0:n0 + 4, :], pg, pin[:, :, 64:128])
                for _ in range(5):
                    next(filler, None)
            for q8 in range(4):
                st_(g, g.Gd, g.Gd[q8 * 16:(q8 + 1) * 16, :, rows(t)].rearrange("c p n -> p c n"), stA, sA[:, q8 * 16:(q8 + 1) * 16, :], q="sp")
                st_(g, g.Gd, g.Gd[64 + q8 * 16:64 + (q8 + 1) * 16, :, rows(t)].rearrange("c p n -> p c n"), stB, sB[:, q8 * 16:(q8 + 1) * 16, :], q="sp")
    gens = [topk_gen(t) for t in range(NT)]
    for _ in gens[0]:
        pass
    for t in range(NT):
        filler = gens[t + 1] if t + 1 < NT else iter(())
        g_form(t, filler)
        for _ in filler:
            pass
    Ut = g.peer_u[:, :].rearrange("(l p c) d -> l c p d", l=g.cfg.get("nlay", DEPTH), c=128)
    Vt = g.peer_v[:, :].rearrange("(l p c) d -> l c p d", l=g.cfg.get("nlay", DEPTH), c=128)
    NG = 4
    for (tp0, ntp) in ((0, 4), (4, 8)):
        NTp = ntp * 128
        ngrp = ntp // 4
        ubs = [g.B[6], g.B[7]][:ngrp]
        uvs = []
        for gi, ub in enumerate(ubs):
            uv = bf(ub[:])[:, 0:16 * 512].rearrange("p (k n) -> p k n", k=16)
            ld(g, ub, uv, g.UT, g.UT[:, :, (tp0 + gi * 4) * 128:(tp0 + gi * 4 + 4) * 128])
            uvs.append(uv)
        accs = [(g.B[tt_ // 2], g.B[tt_ // 2][:, (tt_ % 2) * D:(tt_ % 2 + 1) * D]) for tt_ in range(ntp)]
        Vsets = []
        for pb_list in ([g.B[4], g.B[4]], [g.T[4], g.T[5]]):
            vl = []
            for cl in range(NG):
                pbuf = pb_list[cl // 2]
                if pbuf is g.B[4]:
                    sb_ = subs(g, pbuf, NG)[cl]
                    vl.append((sb_, bf(pbuf[:]).rearrange("p (c n) -> p c n", c=NG)[:, cl, :]))
                else:
                    sb_ = subs(g, pbuf, 2)[cl % 2]
                    vl.append((sb_, bf(pbuf[:]).rearrange("p (c n) -> p c n", c=2)[:, cl % 2, :]))
            Vsets.append(vl)
        ATb = g.T[6]
        ATv = bf(ATb[:]).rearrange("p (c n) -> p c n", c=NG)
        ATs = subs(g, ATb, NG)
        GX = g.B[5]
        GXv = bf(GX[:]).rearrange("p (s n) -> p s n", s=8)
        GXs = subs(g, GX, 8)
        u16l = []
        utl = []
        for cl in range(NG):
            ub_ = [g.T[0], g.T[1]][cl // 2]
            u16l.append((subs(g, ub_, 2)[cl % 2], bf(ub_[:]).rearrange("p (s n) -> p s n", s=2)[:, cl % 2, :]))
            tb_ = [g.T[2], g.T[3]][cl // 2]
            utl.append((subs(g, tb_, 2)[cl % 2], bf(tb_[:]).rearrange("p (s k n) -> p s k n", s=2, k=16)[:, cl % 2]))
        for cg in range(128 // NG):
            Vset = Vsets[cg % 2]
            for cl in range(NG):
                c = cg * NG + cl
                ld(g, u16l[cl][0], u16l[cl][1], g.peer_u, Ut[i, c], q="pool")
                ld(g, GXs[cl], GXv[:, cl, 0:NTp], g.Gd, g.Gd[c, :, tp0 * 128:tp0 * 128 + NTp])
            for cl in range(NG):
                c = cg * NG + cl
                ld(g, Vset[cl][0], Vset[cl][1], g.peer_v, Vt[i, c], q="pool")
            for cl in range(NG):
                ub_, uap = u16l[cl]
                tb_, tap = utl[cl]
                for half in range(2):
                    pt = rot(g, "dtr", [g.psb[0], g.psA3[2]])
                    pv = bf(pt[:])[:, 0:1024] if pt is g.psb[0] else bf(pt[:])[:, 2048:3072]
                    for j in range(8):
                        kc = half * 8 + j
                        tr(g, pt, pv[:, j * 128:(j + 1) * 128], ub_, uap[:, kc * 128:(kc + 1) * 128], fp32=False)
                    cp(g, "act", tb_, tap[:, half * 8:half * 8 + 8, :], pt, pv.rearrange("p (k n) -> p k n", k=8))
            for cl in range(NG):
                tb_, tap = utl[cl]
                for gi in range(ngrp):
                    ph = rot(g, "dph", [g.psA3[0], g.psA3[1]])
                    phv = ph[:, 0:512] if ph is g.psA3[0] else ph[:, 512:1024]
                    for kc in range(16):
                        mm(g, ph, phv, tb_, tap[:, kc, :], ubs[gi], uvs[gi][:, kc, :], kc == 0, kc == 15)
                    gsel = 4 + (g.rr.get("dge", 0) % 4); g.rr["dge"] = g.rr.get("dge", 0) + 1
                    ge = GXv[:, gsel, 0:512]
                    act(g, GXs[gsel], ge, ph, phv, AF.Gelu)
                    tt(g, "dve", ATs[cl], ATv[:, cl, gi * 512:(gi + 1) * 512], GXs[gsel], ge, GXs[cl], GXv[:, cl, gi * 512:(gi + 1) * 512], ALU.mult)
            for tt_ in range(ntp):
                ab, aap = accs[tt_]
                for sl in range(4):
                    po = g.psb[1 + sl]
                    for cl in range(NG):
                        mm(g, po, po[:, 0:512], ATs[cl], ATv[:, cl, tt_ * 128:(tt_ + 1) * 128], Vset[cl][0], Vset[cl][1][:, sl * 512:(sl + 1) * 512],
                           cl == 0, cl == NG - 1)
                    if cg == 0:
                        cp(g, "dve", ab, aap[:, sl * 512:(sl + 1) * 512], po, po[:, 0:512])
                    else:
                        tt(g, "dve", ab, aap[:, sl * 512:(sl + 1) * 512], po, po[:, 0:512], ab, aap[:, sl * 512:(sl + 1) * 512], ALU.add)
        r = 0 if tp0 < 4 else 1
        gt = g.T[0]
        ld(g, gt, gt[:], g.MOD, bcast_row(g.MOD[2 * i + r:2 * i + r + 1, 5 * D:6 * D]))
        for tt_ in range(ntp):
            t = tp0 + tt_
            ab, aap = accs[tt_]
            xt = g.T[1]
            ld(g, xt, xt[:], g.X, g.X[rows(t), :])
            tt(g, "dve", ab, aap, ab, aap, gt, gt[:], ALU.mult)
            stt(g, ab, aap, xt, xt[:], ALPHA, ab, aap, ALU.mult, ALU.add)
            st_(g, g.Y, g.Y[rows(t), :], ab, aap, q="sp")


def host_consts():
    c = {}
    c["c_ident"] = np.eye(128, dtype=np.float32)
    cos = np.ones((NTOK, 64), np.float32)
    sin = np.zeros((NTOK, 64), np.float32)
    pos = np.arange(NS_TOK)
    row = (pos // 64).astype(np.float32)
    col = (pos % 64).astype(np.float32)
    inv = (10000.0 ** (-np.arange(16, dtype=np.float32) / 16)).astype(np.float32)
    ar = (row[:, None] * inv[None, :]).astype(np.float32)
    ac = (col[:, None] * inv[None, :]).astype(np.float32)
    cos[NP_TOK:, 0:16] = np.cos(ar); cos[NP_TOK:, 16:32] = np.cos(ar)
    cos[NP_TOK:, 32:48] = np.cos(ac); cos[NP_TOK:, 48:64] = np.cos(ac)
    sin[NP_TOK:, 0:16] = -np.sin(ar); sin[NP_TOK:, 16:32] = np.sin(ar)
    sin[NP_TOK:, 32:48] = -np.sin(ac); sin[NP_TOK:, 48:64] = np.sin(ac)
    c["c_rcos"] = cos
    c["c_rsin"] = sin
    j = np.arange(512, dtype=np.float64)
    a = 2 * np.pi * np.outer(j, j) / 512
    c["c_cc"] = np.cos(a).astype(np.float32)
    c["c_sc"] = np.sin(a).astype(np.float32)
    for T in (256, 1024):
        tt_ = np.arange(T, dtype=np.float64)
        a = 2 * np.pi * np.outer(tt_, tt_) / T
        nrm = 1.0 / math.sqrt(T * 512)
        c[f"c_ct{T}"] = (np.cos(a) * nrm).astype(np.float32)
        c[f"c_st{T}"] = (-np.sin(a) * nrm).astype(np.float32)
    ii = np.arange(128)
    c["c_mprev"] = np.where(ii[None, :] >= ii[:, None], 0.0, NEG).astype(np.float32)
    c["c_mnext"] = np.where(ii[None, :] <= ii[:, None], 0.0, NEG).astype(np.float32)
    c["c_iota"] = np.tile(np.arange(256, dtype=np.float32)[None, :], (128, 1))
    return c


def make_in_maps(inp):
    consts = host_consts()
    f = lambda a: np.ascontiguousarray(np.asarray(a, dtype=np.float32))
    shared = {
        "ada_w": f(inp["ada_w"]), "ada_b": f(inp["ada_b"]),
        "ln1_g": f(inp["ln1_g"]), "ln1_b": f(inp["ln1_b"]), "ln2_g": f(inp["ln2_g"]), "ln2_b": f(inp["ln2_b"]),
        "mla_w_dq": f(inp["mla_w_dq"]), "mla_q_norm": f(inp["mla_q_norm"]).reshape(1, 512),
        "mla_w_uq": f(inp["mla_w_uq"]), "mla_w_dkv": f(inp["mla_w_dkv"]),
        "mla_kv_norm": f(inp["mla_kv_norm"]).reshape(1, 512), "mla_w_uk": f(inp["mla_w_uk"]),
        "mla_w_uv": f(inp["mla_w_uv"]), "mla_w_o": f(inp["mla_w_o"]),
        "gqa_w_qkv": f(inp["gqa_w_qkv"]), "gqa_sink": f(inp["gqa_sink"]).reshape(1, 32),
        "gqa_w_o": f(inp["gqa_w_o"]), "fnet_w_out": f(inp["fnet_w_out"]),
        "conv_w_in": f(inp["conv_w_in"]), "conv_w": f(inp["conv_w"]), "conv_b": f(inp["conv_b"]).reshape(1, D),
        "conv_w_out": f(inp["conv_w_out"]), "peer_w_q": f(inp["peer_w_q"]),
        "peer_sub_keys": f(inp["peer_sub_keys"]).reshape(DEPTH * 16 * 128, 128),
        "peer_u": f(inp["peer_u"]).reshape(DEPTH * 16384, D), "peer_v": f(inp["peer_v"]).reshape(DEPTH * 16384, D),
    }
    shared.update(consts)
    xp = f(inp["x_prompt"]); xs = f(inp["x_sample"])
    maps = []
    for c in range(8):
        b = c // 4
        m = dict(shared)
        m["xin"] = np.concatenate([xp[2 * c].reshape(256, D), xp[2 * c + 1].reshape(256, D), xs[b]], axis=0)
        m["cond"] = np.stack([f(inp["c_ctx"]), f(inp["c"])[b]], axis=0)
        m["l0ckv"] = f(inp["cache_l0_ckv"])[b]; m["l0kr"] = f(inp["cache_l0_krope"])[b]
        m["l1k"] = f(inp["cache_l1_k"])[b].reshape(256, 512); m["l1v"] = f(inp["cache_l1_v"])[b].reshape(256, 512)
        maps.append(m)
    return maps


def kernel(**inp):
    nc = bass.Bass("TRN2", target_bir_lowering=False)
    build(nc)
    maps = make_in_maps(inp)
    res = run_bass_kernel_spmd(nc, maps, core_ids=list(range(8))).results
    yp = np.zeros((16, 256, D), np.float32); ys = np.zeros((2, 1024, D), np.float32)
    ckv = np.zeros((16, 256, 512), np.float32); kr = np.zeros((16, 256, 64), np.float32)
    kk = np.zeros((16, 256, 8, 64), np.float32); vv = np.zeros((16, 256, 8, 64), np.float32)
    for c in range(8):
        r = res[c]
        b, qd = c // 4, c % 4
        yp[2 * c] = r["y"][0:256]; yp[2 * c + 1] = r["y"][256:512]
        ys[b, qd * 256:(qd + 1) * 256] = r["y"][512 + qd * 256:512 + (qd + 1) * 256]
        ckv[2 * c] = r["o_ckv"][0:256]; ckv[2 * c + 1] = r["o_ckv"][256:512]
        kr[2 * c] = r["o_kr"][0:256]; kr[2 * c + 1] = r["o_kr"][256:512]
        kk[2 * c] = r["o_k"][0:256].reshape(256, 8, 64); kk[2 * c + 1] = r["o_k"][256:512].reshape(256, 8, 64)
        vv[2 * c] = r["o_v"][0:256].reshape(256, 8, 64); vv[2 * c + 1] = r["o_v"][256:512].reshape(256, 8, 64)
    return (yp, ys, ckv, kr, kk, vv)
```
